# Optimizing a Trainium2 kernel written in Bass

```python
import math
import jax, jax.numpy as jnp
from jax import lax
import numpy as np

D_MODEL = 4096
BATCH = 1
SEQ = 16384
DEPTH = 2

F_GROUPS = 4
F_GROUP_DIM = 256
F_WIDTH = F_GROUPS * F_GROUP_DIM
MLA_HEADS = 8
QK_NOPE = 128
QK_ROPE = 64
V_DIM = 128
Q_LORA = 768
KV_LORA = 512
ROPE_THETA = 10000.0
Q_BLOCK = 128
MLA_WIDTH = MLA_HEADS * V_DIM
SG_GROUPS = 16
SG_GROUP_DIM = 128
SG_WIDTH = SG_GROUPS * SG_GROUP_DIM
CHUNK = 128
N_BRANCH = 3
OFF_F = 0
OFF_CQ = OFF_F + F_WIDTH
OFF_CKV = OFF_CQ + Q_LORA
OFF_KR = OFF_CKV + KV_LORA
OFF_SG = OFF_KR + QK_ROPE
N_IN = OFF_SG + 2 * SG_WIDTH
MEM_TOKENS = 256
MEM_HEADS = 4
MEM_HEAD_DIM = 256
MEM_WIDTH = MEM_HEADS * MEM_HEAD_DIM
D_FF = 11008
CONV_WIDTH = 3
EPS = 1e-6

kernel_name = "hybrid_fourier_mla_sgmlp_encoder"


def rms_norm(x, g):
    xf = x.astype(jnp.float32)
    y = xf * lax.rsqrt(jnp.mean(xf * xf, axis=-1, keepdims=True) + EPS)
    return (y * g.astype(jnp.float32)).astype(x.dtype)


def layer_norm(x, g):
    xf = x.astype(jnp.float32)
    mu = jnp.mean(xf, axis=-1, keepdims=True)
    xc = xf - mu
    y = xc * lax.rsqrt(jnp.mean(xc * xc, axis=-1, keepdims=True) + EPS)
    return (y * g.astype(jnp.float32)).astype(x.dtype)


def apply_rope(t, cos, sin):
    cos = cos.astype(t.dtype)
    sin = sin.astype(t.dtype)
    t1, t2 = t[..., : QK_ROPE // 2], t[..., QK_ROPE // 2:]
    return jnp.concatenate([t1 * cos - t2 * sin, t2 * cos + t1 * sin], axis=-1)


def fourier_branch(z):
    B, S, _ = z.shape
    zg = z.reshape(B, S, F_GROUPS, F_GROUP_DIM).astype(jnp.float32)
    y = jnp.fft.fft2(zg, axes=(1, 3), norm="ortho").real
    return y.reshape(B, S, F_WIDTH).astype(z.dtype)


def mla_branch(c_q, c_kv, k_rope, cos, sin, q_norm, w_uq, kv_norm, w_ukv):
    B, S, _ = c_q.shape
    q = (rms_norm(c_q, q_norm) @ w_uq).reshape(B, S, MLA_HEADS, QK_NOPE + QK_ROPE)
    q_nope = q[..., :QK_NOPE]
    q_rope = apply_rope(q[..., QK_NOPE:], cos[:, :, None, :], sin[:, :, None, :])
    kv = (rms_norm(c_kv, kv_norm) @ w_ukv).reshape(B, S, MLA_HEADS, QK_NOPE + V_DIM)
    k_nope, v = kv[..., :QK_NOPE], kv[..., QK_NOPE:]
    k_rope = apply_rope(k_rope, cos, sin)
    scale = 1.0 / math.sqrt(QK_NOPE + QK_ROPE)
    nb = S // Q_BLOCK

    def to_blocks(t):
        return t.reshape(B, nb, Q_BLOCK, MLA_HEADS, t.shape[-1]).swapaxes(0, 1)

    def attend(qs):
        qn, qr = qs
        s = (jnp.einsum('bqhd,bkhd->bhqk', qn, k_nope)
             + jnp.einsum('bqhr,bkr->bhqk', qr, k_rope))
        p = jax.nn.softmax(s.astype(jnp.float32) * scale, axis=-1).astype(v.dtype)
        return jnp.einsum('bhqk,bkhd->bqhd', p, v)

    o = lax.map(attend, (to_blocks(q_nope), to_blocks(q_rope)))
    return o.swapaxes(0, 1).reshape(B, S, MLA_WIDTH)


def spatial_gating_branch(z, ln_gain, w_spatial, b_spatial):
    B, S, _ = z.shape
    z = jax.nn.gelu(z)
    u, v = z[..., :SG_WIDTH], z[..., SG_WIDTH:]
    v = layer_norm(v.reshape(B, S, SG_GROUPS, SG_GROUP_DIM),
                   ln_gain.reshape(SG_GROUPS, SG_GROUP_DIM))
    v = v.reshape(B, S // CHUNK, CHUNK, SG_GROUPS, SG_GROUP_DIM)
    s = (jnp.einsum('gpq,bcqgd->bcpgd', w_spatial, v)
         + b_spatial.T[None, None, :, :, None])
    return u * s.reshape(B, S, SG_WIDTH)


def memory_cross_attention(h, mem_n, w_mq, w_mk, w_mv, w_mo):
    B, S, _ = h.shape
    q = (h @ w_mq).reshape(B, S, MEM_HEADS, MEM_HEAD_DIM)
    k = (mem_n @ w_mk).reshape(B, MEM_TOKENS, MEM_HEADS, MEM_HEAD_DIM)
    v = (mem_n @ w_mv).reshape(B, MEM_TOKENS, MEM_HEADS, MEM_HEAD_DIM)
    s = jnp.einsum('bqhd,bmhd->bhqm', q, k).astype(jnp.float32) / math.sqrt(MEM_HEAD_DIM)
    p = jax.nn.softmax(s, axis=-1).astype(v.dtype)
    o = jnp.einsum('bhqm,bmhd->bqhd', p, v).reshape(B, S, MEM_WIDTH)
    return o @ w_mo


def conv_ffn(h, w_up, conv_w, conv_b, w_down):
    u = h @ w_up
    up = jnp.pad(u, ((0, 0), (1, 1), (0, 0)))
    S = h.shape[1]
    uc = (up[:, 0:S] * conv_w[0] + up[:, 1:S + 1] * conv_w[1]
          + up[:, 2:S + 2] * conv_w[2] + conv_b)
    g, val = uc[..., :D_FF], uc[..., D_FF:]
    return (jax.nn.gelu(g) * val) @ w_down


def setup_inputs(seed: int = 0) -> dict:
    key = jax.random.key(seed)
    ks = iter(jax.random.split(key, 64))
    L = DEPTH

    def nrm(shape, scale):
        return jax.random.normal(next(ks), shape, jnp.float32) * scale

    def gain(shape):
        return 1.0 + nrm(shape, 0.02)

    x = nrm((BATCH, SEQ, D_MODEL), 1.0)
    mem = nrm((BATCH, MEM_TOKENS, D_MODEL), 1.0)
    offs = jax.random.randint(next(ks), (BATCH, 1), 0, 1024, dtype=jnp.int32)
    positions = jnp.arange(SEQ, dtype=jnp.int32)[None, :] + offs
    return {
        "x": x,
        "mem": mem,
        "positions": positions,
        "mix_pre_norm": gain((L, D_MODEL)),
        "mix_post_norm": gain((L, D_MODEL)),
        "w_in": nrm((L, D_MODEL, N_IN), D_MODEL ** -0.5),
        "mla_q_norm": gain((L, Q_LORA)),
        "w_uq": nrm((L, Q_LORA, MLA_HEADS * (QK_NOPE + QK_ROPE)), Q_LORA ** -0.5),
        "mla_kv_norm": gain((L, KV_LORA)),
        "w_ukv": nrm((L, KV_LORA, MLA_HEADS * (QK_NOPE + V_DIM)), KV_LORA ** -0.5),
        "sg_norm": gain((L, SG_WIDTH)),
        "w_spatial": nrm((L, SG_GROUPS, CHUNK, CHUNK), CHUNK ** -0.5),
        "b_spatial": 1.0 + nrm((L, SG_GROUPS, CHUNK), 0.02),
        "w_br_f": nrm((L, F_WIDTH, D_MODEL), F_WIDTH ** -0.5),
        "w_br_a": nrm((L, MLA_WIDTH, D_MODEL), MLA_WIDTH ** -0.5),
        "w_br_s": nrm((L, SG_WIDTH, D_MODEL), SG_WIDTH ** -0.5),
        "w_gate": nrm((L, D_MODEL, N_BRANCH * D_MODEL), D_MODEL ** -0.5),
        "b_gate": nrm((L, N_BRANCH * D_MODEL), 0.02),
        "w_out": nrm((L, D_MODEL, D_MODEL), D_MODEL ** -0.5),
        "mem_pre_norm": gain((L, D_MODEL)),
        "mem_post_norm": gain((L, D_MODEL)),
        "mem_kv_norm": gain((L, D_MODEL)),
        "w_mq": nrm((L, D_MODEL, MEM_WIDTH), D_MODEL ** -0.5),
        "w_mk": nrm((L, D_MODEL, MEM_WIDTH), D_MODEL ** -0.5),
        "w_mv": nrm((L, D_MODEL, MEM_WIDTH), D_MODEL ** -0.5),
        "w_mo": nrm((L, MEM_WIDTH, D_MODEL), MEM_WIDTH ** -0.5),
        "ffn_pre_norm": gain((L, D_MODEL)),
        "ffn_post_norm": gain((L, D_MODEL)),
        "w_up": nrm((L, D_MODEL, 2 * D_FF), D_MODEL ** -0.5),
        "conv_w": nrm((L, CONV_WIDTH, 2 * D_FF), CONV_WIDTH ** -0.5),
        "conv_b": nrm((L, 2 * D_FF), 0.02),
        "w_down": nrm((L, D_FF, D_MODEL), D_FF ** -0.5),
    }


def reference(x, mem, positions, mix_pre_norm, mix_post_norm, w_in, mla_q_norm, w_uq,
              mla_kv_norm, w_ukv, sg_norm, w_spatial, b_spatial, w_br_f, w_br_a, w_br_s,
              w_gate, b_gate, w_out, mem_pre_norm, mem_post_norm, mem_kv_norm, w_mq, w_mk,
              w_mv, w_mo, ffn_pre_norm, ffn_post_norm, w_up, conv_w, conv_b, w_down):
    B, S, _ = x.shape
    inv_freq = ROPE_THETA ** (-jnp.arange(0, QK_ROPE, 2, dtype=jnp.float32) / QK_ROPE)
    ang = positions.astype(jnp.float32)[..., None] * inv_freq
    cos, sin = jnp.cos(ang), jnp.sin(ang)

    for l in range(DEPTH):
        h = rms_norm(x, mix_pre_norm[l])
        z = h @ w_in[l]
        y_f = fourier_branch(z[..., OFF_F:OFF_CQ]) @ w_br_f[l]
        y_a = mla_branch(z[..., OFF_CQ:OFF_CKV], z[..., OFF_CKV:OFF_KR],
                         z[..., OFF_KR:OFF_SG], cos, sin,
                         mla_q_norm[l], w_uq[l], mla_kv_norm[l], w_ukv[l]) @ w_br_a[l]
        y_s = spatial_gating_branch(z[..., OFF_SG:], sg_norm[l], w_spatial[l],
                                    b_spatial[l]) @ w_br_s[l]
        gates = jax.nn.sigmoid((h @ w_gate[l] + b_gate[l]).astype(jnp.float32))
        gates = gates.astype(x.dtype).reshape(B, S, N_BRANCH, D_MODEL)
        merged = gates[:, :, 0] * y_f + gates[:, :, 1] * y_a + gates[:, :, 2] * y_s
        x = x + rms_norm(merged @ w_out[l], mix_post_norm[l])

        h = rms_norm(x, mem_pre_norm[l])
        mem_n = rms_norm(mem, mem_kv_norm[l])
        y = memory_cross_attention(h, mem_n, w_mq[l], w_mk[l], w_mv[l], w_mo[l])
        x = x + rms_norm(y, mem_post_norm[l])

        h = rms_norm(x, ffn_pre_norm[l])
        y = conv_ffn(h, w_up[l], conv_w[l], conv_b[l], w_down[l])
        x = x + rms_norm(y, ffn_post_norm[l])
    return x
```

```python
import math
from contextlib import ExitStack
import numpy as np
import ml_dtypes
import concourse.bass as bass
import concourse.mybir as mybir
from concourse.bass_utils import run_bass_kernel_spmd

F32 = mybir.dt.float32
BF16 = mybir.dt.bfloat16
I32 = mybir.dt.int32
AF = mybir.ActivationFunctionType
ALU = mybir.AluOpType
AX = mybir.AxisListType

NCORES = 8
SEQ = 16384
TOK = SEQ // NCORES
D = 4096
EPS = 1e-6
NSLOT = 6


class Buf:
    __slots__ = ("wr", "rd")

    def __init__(self):
        self.wr = {}
        self.rd = {}


class Op:
    __slots__ = ("eng", "fn", "reads", "writes", "wpart", "dma", "deps", "marked", "semval", "slot", "key", "xdeps")

    def __init__(self, eng, fn, reads, writes, wpart, dma):
        self.eng = eng
        self.fn = fn
        self.reads = reads
        self.writes = writes
        self.wpart = wpart
        self.dma = dma
        self.deps = ()
        self.marked = False
        self.semval = None
        self.slot = None
        self.key = None
        self.xdeps = ()


class Prog:
    def __init__(self, nc):
        self.nc = nc
        self.ops = []
        self.dma_count = {}

    def op(self, eng, fn, reads=(), writes=(), wpart=(), dma=False):
        o = Op(eng, fn, tuple(reads), tuple(writes), tuple(wpart), dma)
        if dma:
            k = self.dma_count.get(eng, 0)
            self.dma_count[eng] = k + 1
            o.slot = k % NSLOT
            o.semval = 16 * (k // NSLOT + 1)
            o.key = (eng, o.slot)
            o.marked = True
        else:
            o.key = eng
        self.ops.append(o)

    def dma(self, q, out, in_, reads=(), writes=(), wpart=(), **kw):
        self.op(q, lambda e: e.dma_start(out=out, in_=in_, **kw), reads, writes, wpart, dma=True)

    def barrier(self):
        last = {}
        for i, o in enumerate(self.ops):
            last[o.key] = i
        engs = sorted(set(o.eng for o in self.ops))
        tgt = tuple(last.values())
        for e in engs:
            self.op(e, lambda eo: eo.nop())
            self.ops[-1].xdeps = tgt

    def emit(self):
        nc = self.nc
        ops = self.ops
        last_on_slot = {}
        for i, o in enumerate(ops):
            deps = set(o.xdeps)
            for b in o.reads:
                deps.update(b.wr.values())
            for b in o.writes:
                deps.update(b.wr.values())
                deps.update(b.rd.values())
            for b in o.wpart:
                deps.update(b.rd.values())
            for b in o.writes:
                b.wr = {o.key: i}
                b.rd = {}
            for b in o.wpart:
                if b.rd:
                    b.wr = {o.key: i}
                    b.rd = {}
                else:
                    b.wr[o.key] = i
            for b in o.reads:
                if b.wr.get(o.key) != i:
                    b.rd[o.key] = i
            if o.dma:
                prev = last_on_slot.get(o.key)
                if prev is not None:
                    deps.add(prev)
                last_on_slot[o.key] = i
            deps.discard(i)
            fd = []
            for d in deps:
                od = ops[d]
                if od.eng == "pe" and o.eng == "pe" and not od.dma and not o.dma:
                    continue
                fd.append(d)
            o.deps = fd
            for d in fd:
                ops[d].marked = True
        cnt = {}
        for o in ops:
            if o.dma:
                continue
            if o.marked:
                cnt[o.eng] = cnt.get(o.eng, 0) + 1
                o.semval = cnt[o.eng]
        engs = sorted(set(o.eng for o in ops))
        ctxs = []
        sems = {}
        for e in engs:
            if cnt.get(e, 0) > 0:
                cm = nc.semaphore("s_" + e)
                sems[e] = cm.__enter__()
                ctxs.append(cm)
            for s in range(min(NSLOT, self.dma_count.get(e, 0))):
                cm = nc.semaphore("d_%s_%d" % (e, s))
                sems[(e, s)] = cm.__enter__()
                ctxs.append(cm)
        per_eng = {e: [] for e in engs}
        for i, o in enumerate(ops):
            per_eng[o.eng].append(i)

        def run_engine(ename, eobj):
            seen = {}
            for i in per_eng.get(ename, ()):
                o = ops[i]
                need = {}
                for d in o.deps:
                    od = ops[d]
                    v = od.semval
                    if need.get(od.key, 0) < v:
                        need[od.key] = v
                for key, v in need.items():
                    if seen.get(key, 0) >= v:
                        continue
                    eobj.wait_ge(sems[key], v)
                    seen[key] = v
                ins = o.fn(eobj)
                if o.marked:
                    ins.then_inc(sems[o.key], 16 if o.dma else 1)

        with nc.Block() as block:
            if "sp" in per_eng:
                @block.sync
                def _(e):
                    run_engine("sp", e)
            if "act" in per_eng:
                @block.scalar
                def _(e):
                    run_engine("act", e)
            if "dve" in per_eng:
                @block.vector
                def _(e):
                    run_engine("dve", e)
            if "pool" in per_eng:
                @block.gpsimd
                def _(e):
                    run_engine("pool", e)
            if "pe" in per_eng:
                @block.tensor
                def _(e):
                    run_engine("pe", e)
        for cm in reversed(ctxs):
            cm.__exit__(None, None, None)
        return len(ops)


class Ctx:
    def __init__(self):
        self.nc = bass.Bass("TRN2", target_bir_lowering=False)
        self.p = Prog(self.nc)
        self.es = None
        self.n = 0
        self.outs = []
        self.rr = 0

    def din(self, name, shape, dt):
        return self.nc.dram_tensor(name, list(shape), dt, kind="ExternalInput").ap(), Buf()

    def dout(self, name, shape, dt):
        b = Buf()
        self.outs.append(b)
        return self.nc.dram_tensor(name, list(shape), dt, kind="ExternalOutput").ap(), b

    def dint(self, name, shape, dt):
        return self.nc.dram_tensor(name, list(shape), dt, kind="Internal").ap(), Buf()

    def sb(self, shape, dt):
        self.n += 1
        return self.es.enter_context(self.nc.sbuf_tensor("sb%d" % self.n, list(shape), dt))

    def ps(self, shape, dt):
        self.n += 1
        return self.es.enter_context(self.nc.psum_tensor("ps%d" % self.n, list(shape), dt))

    def phase(self):
        c = self

        class _Ph:
            def __enter__(self_):
                self_.es = ExitStack()
                self_.es.__enter__()
                c.es = self_.es
                return c

            def __exit__(self_, *a):
                c.p.barrier()
                c.es = None
                return self_.es.__exit__(*a)
        return _Ph()

    def finish(self):
        self.p.op("sp", lambda e: e.nop(), reads=list(self.outs))
        return self.p.emit()

    def ident(self):
        p = self.p
        idf = self.sb([128, 128], F32)
        idb = self.sb([128, 128], BF16)
        b = Buf()
        p.op("pool", lambda e: e.memset(idf[:], 0.0), writes=[b])
        p.op("pool", lambda e: e.affine_select(out=idf[:], in_=idf[:], pattern=[[-1, 128]],
                                               compare_op=ALU.not_equal, fill=1.0, base=0,
                                               channel_multiplier=1), reads=[b], writes=[b])
        p.op("dve", lambda e: e.tensor_copy(out=idb[:], in_=idf[:]), reads=[b], writes=[b])
        return idb, b


def rstd_ops(p, eng, out, in_, n, rb, wb):
    p.op(eng, lambda e: e.tensor_scalar(out=out, in0=in_, scalar1=1.0 / n, scalar2=EPS,
                                        op0=ALU.mult, op1=ALU.add), reads=rb, writes=wb)
    p.op("act", lambda e: e.activation(out=out, in_=out, func=AF.Sqrt), reads=wb, writes=wb)
    p.op(eng, lambda e: e.reciprocal(out=out, in_=out), reads=wb, writes=wb)


def ph_head(c, x_d, bx, gb_d, bg, hT_d, bh, ntok, col0=0, halo=None):
    p = c.p
    if True:
        idb, b_id = c.ident()
        gb = c.sb([128, D], F32)
        b_gb = Buf()
        p.dma("act", gb[:], gb_d, reads=[bg], writes=[b_gb])
        xt = [c.sb([128, D], F32) for _ in range(2)]
        b_xt = [Buf(), Buf()]
        xb = [c.sb([128, D], BF16) for _ in range(2)]
        b_xb = [Buf(), Buf()]
        ss = [c.sb([128, 1], F32) for _ in range(2)]
        b_ss = [Buf(), Buf()]
        hTt = [c.sb([128, 32, 512], BF16) for _ in range(2)]
        b_hTt = [Buf(), Buf()]
        pt = [c.ps([128, 8, 128], BF16) for _ in range(4)]
        b_pt = [Buf() for _ in range(4)]
        ntile = ntok // 128
        jobs = [(t * 128, 128, None) for t in range(ntile)]
        if halo is not None:
            jobs.append((0, 2, halo))
        ptc = 0
        for ji, (r0, nr, hal) in enumerate(jobs):
            s = ji % 2
            if hal is None:
                p.dma("sp", xt[s][:nr, :], x_d[r0:r0 + nr, :], reads=[bx], writes=[b_xt[s]])
            else:
                p.dma("sp", xt[s][:nr, :], hal[0][0:nr, :], reads=[hal[1]], writes=[b_xt[s]])
            p.op("act", lambda e, s=s, nr=nr: e.activation(out=xb[s][:nr, :], in_=xt[s][:nr, :], func=AF.Square,
                                                           accum_out=ss[s][:nr, :]),
                 reads=[b_xt[s]], writes=[b_xb[s], b_ss[s]])
            rstd_ops(p, "dve", ss[s][:nr, :], ss[s][:nr, :], D, [b_ss[s]], [b_ss[s]])
            p.op("dve", lambda e, s=s, nr=nr: e.scalar_tensor_tensor(out=xb[s][:nr, :], in0=xt[s][:nr, :],
                                                                     scalar=ss[s][:nr, 0:1], in1=gb[:nr, :],
                                                                     op0=ALU.mult, op1=ALU.mult),
                 reads=[b_xt[s], b_ss[s], b_gb], writes=[b_xb[s]])
            hs = (ji // 4) % 2
            tcol = (ji % 4) * 128
            for q in range(4):
                pi = ptc % 4
                ptc += 1
                for cc in range(8):
                    ch = q * 8 + cc
                    p.op("pe", lambda e, s=s, nr=nr, ch=ch, cc=cc, pi=pi: e.transpose(
                        out=pt[pi][:, cc, :nr], in_=xb[s][:nr, ch * 128:(ch + 1) * 128], identity=idb[:nr, :nr]),
                        reads=[b_xb[s], b_id], writes=[b_pt[pi]])
                eng = "act" if q % 2 == 0 else "dve"
                if eng == "act":
                    p.op("act", lambda e, hs=hs, q=q, pi=pi, nr=nr, tcol=tcol: e.activation(
                        out=hTt[hs][:, q * 8:(q + 1) * 8, tcol:tcol + nr], in_=pt[pi][:, :, :nr], func=AF.Copy),
                        reads=[b_pt[pi]], wpart=[b_hTt[hs]])
                else:
                    p.op("dve", lambda e, hs=hs, q=q, pi=pi, nr=nr, tcol=tcol: e.tensor_copy(
                        out=hTt[hs][:, q * 8:(q + 1) * 8, tcol:tcol + nr], in_=pt[pi][:, :, :nr]),
                        reads=[b_pt[pi]], wpart=[b_hTt[hs]])
            if hal is not None:
                p.dma("sp", hT_d[:, :, ntok:ntok + 2].rearrange("c p t -> p c t"), hTt[hs][:, :, tcol:tcol + 2],
                      reads=[b_hTt[hs]], wpart=[bh])
            elif ji % 4 == 3 or ji == ntile - 1:
                g0 = col0 + (ji // 4) * 512
                wdt = (ji % 4 + 1) * 128
                p.dma("sp", hT_d[:, :, g0:g0 + wdt].rearrange("c p t -> p c t"), hTt[hs][:, :, :wdt],
                      reads=[b_hTt[hs]], wpart=[bh])


def ph_lin(c, aT_d, ba, KC, W_d, bw, y_d, by, ssq_d, bs, ntok, acol0=0):
    p = c.p
    KS = 8
    nks = (KC + KS - 1) // KS
    if True:
        aT = c.sb([128, KC, 512], BF16)
        b_aT = Buf()
        wb = [c.sb([128, KS, 512], BF16) for _ in range(3)]
        b_wb = [Buf() for _ in range(3)]
        acc = [c.ps([128, 512], F32) for _ in range(8)]
        b_acc = [Buf() for _ in range(8)]
        ysb = [c.sb([128, 512], F32) for _ in range(4)]
        b_ysb = [Buf() for _ in range(4)]
        junk = c.sb([128, 512], BF16)
        b_junk = Buf()
        ssq = [c.sb([128, 8], F32) for _ in range(4)]
        b_ssq = [Buf() for _ in range(4)]
        wi = 0
        yi = 0
        for st in range(ntok // 512):
            t0 = st * 512
            p.dma("sp", aT[:], aT_d[:, :, acol0 + t0:acol0 + t0 + 512].rearrange("c p t -> p c t"),
                  reads=[ba], writes=[b_aT])
            for mb in range(8):
                par = (mb % 2) * 4
                for ks in range(nks):
                    k0 = ks * KS
                    kn = min(KS, KC - k0)
                    w = wi % 3
                    wi += 1
                    p.dma("pool", wb[w][:, :kn, :],
                          W_d[k0 * 128:(k0 + kn) * 128, mb * 512:(mb + 1) * 512].rearrange("(c p) n -> p c n", p=128),
                          reads=[bw], writes=[b_wb[w]])
                    for tl in range(4):
                        for cc in range(kn):
                            kc = k0 + cc
                            p.op("pe", lambda e, tl=tl, cc=cc, kc=kc, w=w, par=par: e.matmul(
                                acc[par + tl][:], lhsT=aT[:, kc, tl * 128:(tl + 1) * 128], rhs=wb[w][:, cc, :],
                                start=(kc == 0), stop=(kc == KC - 1)),
                                reads=[b_aT, b_wb[w]], writes=[b_acc[par + tl]])
                for tl in range(4):
                    y = yi % 4
                    yi += 1
                    p.op("act", lambda e, y=y, a=par + tl: e.activation(out=ysb[y][:], in_=acc[a][:], func=AF.Copy),
                         reads=[b_acc[par + tl]], writes=[b_ysb[y]])
                    p.op("act", lambda e, y=y, tl=tl, mb=mb: e.activation(out=junk[:], in_=ysb[y][:], func=AF.Square,
                                                                          accum_out=ssq[tl][:, mb:mb + 1]),
                         reads=[b_ysb[y]], writes=[b_junk], wpart=[b_ssq[tl]])
                    r0 = t0 + tl * 128
                    p.dma("sp", y_d[r0:r0 + 128, mb * 512:(mb + 1) * 512], ysb[y][:], reads=[b_ysb[y]], wpart=[by])
            for tl in range(4):
                r0 = t0 + tl * 128
                p.dma("sp", ssq_d[r0:r0 + 128, :], ssq[tl][:], reads=[b_ssq[tl]], wpart=[bs])


def ph_nr(c, y_d, by, ssq_d, bs, gb_d, bg, xi_d, bxi, xo_d, bxo, ntok):
    p = c.p
    if True:
        gb = c.sb([128, D], F32)
        b_gb = Buf()
        p.dma("act", gb[:], gb_d, reads=[bg], writes=[b_gb])
        yt = [c.sb([128, D], F32) for _ in range(2)]
        xt = [c.sb([128, D], F32) for _ in range(2)]
        sq = [c.sb([128, 8], F32) for _ in range(2)]
        rs = [c.sb([128, 1], F32) for _ in range(2)]
        b_yt = [Buf(), Buf()]
        b_xt = [Buf(), Buf()]
        b_sq = [Buf(), Buf()]
        b_rs = [Buf(), Buf()]
        for t in range(ntok // 128):
            s = t % 2
            r0 = t * 128
            p.dma("sp", yt[s][:], y_d[r0:r0 + 128, :], reads=[by], writes=[b_yt[s]])
            p.dma("act", xt[s][:], xi_d[r0:r0 + 128, :], reads=[bxi], writes=[b_xt[s]])
            p.dma("sp", sq[s][:], ssq_d[r0:r0 + 128, :], reads=[bs], writes=[b_sq[s]])
            p.op("dve", lambda e, s=s: e.reduce_sum(out=rs[s][:], in_=sq[s][:], axis=AX.X),
                 reads=[b_sq[s]], writes=[b_rs[s]])
            rstd_ops(p, "dve", rs[s][:], rs[s][:], D, [b_rs[s]], [b_rs[s]])
            p.op("dve", lambda e, s=s: e.scalar_tensor_tensor(out=yt[s][:], in0=yt[s][:], scalar=rs[s][:, 0:1],
                                                              in1=gb[:], op0=ALU.mult, op1=ALU.mult),
                 reads=[b_yt[s], b_rs[s], b_gb], writes=[b_yt[s]])
            p.op("pool", lambda e, s=s: e.tensor_tensor(out=yt[s][:], in0=yt[s][:], in1=xt[s][:], op=ALU.add),
                 reads=[b_yt[s], b_xt[s]], writes=[b_yt[s]])
            p.dma("sp", xo_d[r0:r0 + 128, :], yt[s][:], reads=[b_yt[s]], wpart=[bxo])


class St:
    def __init__(self):
        self.wi = 0
        self.bi = 0
        self.k = 0


def evac(p, k, out, in_, rb, wb=(), wp=()):
    if k % 2 == 0:
        p.op("act", lambda e: e.activation(out=out, in_=in_, func=AF.Copy), reads=rb, writes=wb, wpart=wp)
    else:
        p.op("dve", lambda e: e.tensor_copy(out=out, in_=in_), reads=rb, writes=wb, wpart=wp)


def load_hT(c, hT, b_hT, hT_d, bh, t0, n, KC=32, q="sp"):
    c.p.dma(q, hT[:, :KC, :n], hT_d[:KC, :, t0:t0 + n].rearrange("c p t -> p c t"), reads=[bh], writes=[b_hT])


def fm_stream(c, blocks, KC, rhs_fn, b_rhs, tgs, epi, wbufs, b_wbufs, banks, b_banks, st):
    p = c.p
    for (W_d, bw, col0, ncols, tag) in blocks:
        w = st.wi % len(wbufs)
        st.wi += 1
        p.dma("pool", wbufs[w][:, :KC, :ncols],
              W_d[0:KC * 128, col0:col0 + ncols].rearrange("(c p) n -> p c n", p=128),
              reads=[bw], writes=[b_wbufs[w]])
        for ch in range((ncols + 127) // 128):
            wd = min(128, ncols - ch * 128)
            for ti, (tc0, tn) in enumerate(tgs):
                bk = st.bi % len(banks)
                st.bi += 1
                for kc in range(KC):
                    p.op("pe", lambda e, bk=bk, w=w, kc=kc, ch=ch, wd=wd, tc0=tc0, tn=tn: e.matmul(
                        banks[bk][:wd, :tn], lhsT=wbufs[w][:, kc, ch * 128:ch * 128 + wd], rhs=rhs_fn(kc, tc0, tn),
                        start=(kc == 0), stop=(kc == KC - 1)),
                        reads=[b_wbufs[w], b_rhs], writes=[b_banks[bk]])
                epi(tag, ch, ti, banks[bk], b_banks[bk], wd)


def ones_bf(c, shape):
    t = c.sb(shape, BF16)
    b = Buf()
    c.p.op("pool", lambda e: e.memset(t[:], 1.0), writes=[b])
    return t, b


def fm_rmsnorm(c, raw, b_raw, nch, nfeat, tn, gcol, b_g, ssb, b_ssb, rs, b_rs, outT, b_out):
    p = c.p
    p.op("dve", lambda e: e.tensor_scalar(out=rs[:, :tn], in0=ssb[:, :tn], scalar1=1.0 / nfeat, scalar2=EPS,
                                          op0=ALU.mult, op1=ALU.add), reads=[b_ssb], writes=[b_rs])
    p.op("act", lambda e: e.activation(out=rs[:, :tn], in_=rs[:, :tn], func=AF.Sqrt), reads=[b_rs], writes=[b_rs])
    p.op("dve", lambda e: e.reciprocal(out=rs[:, :tn], in_=rs[:, :tn]), reads=[b_rs], writes=[b_rs])
    for ch in range(nch):
        p.op("dve", lambda e, ch=ch: e.scalar_tensor_tensor(out=outT[:, ch, :tn], in0=raw[:, ch, :tn],
                                                            scalar=gcol[:, ch:ch + 1], in1=rs[:, :tn],
                                                            op0=ALU.mult, op1=ALU.mult),
             reads=[b_raw, b_g, b_rs], wpart=[b_out])


def ph_rope(c, pos_d, bp, invf_d, bi, cs_d, bcs, ntok):
    p = c.p
    pi_ = c.sb([64, ntok], I32)
    pf = c.sb([64, ntok], F32)
    t1 = c.sb([64, ntok], F32)
    t2 = c.sb([64, ntok], F32)
    ki = c.sb([64, ntok], I32)
    iv = c.sb([64, 1], F32)
    b_pi, b_pf, b_t1, b_t2, b_ki, b_iv = Buf(), Buf(), Buf(), Buf(), Buf(), Buf()
    p.dma("sp", pi_[:], pos_d, reads=[bp], writes=[b_pi])
    p.dma("sp", iv[:], invf_d, reads=[bi], writes=[b_iv])
    p.op("dve", lambda e: e.tensor_copy(out=pf[:], in_=pi_[:]), reads=[b_pi], writes=[b_pf])
    p.op("dve", lambda e: e.tensor_scalar(out=pf[:], in0=pf[:], scalar1=iv[:, 0:1], scalar2=1.0 / (2 * math.pi),
                                          op0=ALU.mult, op1=ALU.mult), reads=[b_pf, b_iv], writes=[b_pf])
    for k, sh in enumerate((0.25, 0.0)):
        p.op("dve", lambda e, sh=sh: e.tensor_scalar(out=t1[:], in0=pf[:], scalar1=sh, scalar2=None, op0=ALU.add),
             reads=[b_pf], writes=[b_t1])
        p.op("dve", lambda e: e.tensor_copy(out=ki[:], in_=t1[:]), reads=[b_t1], writes=[b_ki])
        p.op("dve", lambda e: e.tensor_copy(out=t2[:], in_=ki[:]), reads=[b_ki], writes=[b_t2])
        p.op("dve", lambda e: e.tensor_tensor(out=t1[:], in0=t1[:], in1=t2[:], op=ALU.subtract),
             reads=[b_t1, b_t2], writes=[b_t1])
        p.op("dve", lambda e: e.tensor_scalar(out=t2[:], in0=t1[:], scalar1=0.5, scalar2=None, op0=ALU.is_gt),
             reads=[b_t1], writes=[b_t2])
        p.op("dve", lambda e: e.tensor_tensor(out=t1[:], in0=t1[:], in1=t2[:], op=ALU.subtract),
             reads=[b_t1, b_t2], writes=[b_t1])
        p.op("dve", lambda e: e.tensor_scalar(out=t2[:], in0=t1[:], scalar1=-0.5, scalar2=None, op0=ALU.is_lt),
             reads=[b_t1], writes=[b_t2])
        p.op("dve", lambda e: e.tensor_tensor(out=t1[:], in0=t1[:], in1=t2[:], op=ALU.add),
             reads=[b_t1, b_t2], writes=[b_t1])
        p.op("act", lambda e: e.activation(out=t1[:], in_=t1[:], func=AF.Sin, scale=2 * math.pi),
             reads=[b_t1], writes=[b_t1])
        p.dma("sp", cs_d[k], t1[:], reads=[b_t1], wpart=[bcs])


def ph_f(c, hT_d, bh, win_d, bw, dft_d, bd, ab_d, bab, ntok):
    p = c.p
    T = 512
    hT = c.sb([128, 32, T], BF16)
    b_hT = Buf()
    wbufs = [c.sb([128, 32, 512], BF16) for _ in range(2)]
    b_wbufs = [Buf(), Buf()]
    banks = [c.ps([128, 512], F32) for _ in range(6)]
    b_banks = [Buf() for _ in range(6)]
    tab = c.sb([128, 2, 2, 256], BF16)
    b_tab = Buf()
    p.dma("pool", tab[:], dft_d.rearrange("a (c p) j -> p a c j", p=128), reads=[bd], writes=[b_tab])
    zf = c.sb([128, 8, T], BF16)
    b_zf = Buf()
    stg = [c.sb([128, T], BF16) for _ in range(3)]
    b_stg = [Buf() for _ in range(3)]
    st = St()
    for s in range(ntok // T):
        t0 = s * T
        load_hT(c, hT, b_hT, hT_d, bh, t0, T)

        def epi(tag, ch, ti, bank, b_bank, wd):
            fch = tag * 4 + ch
            st.k += 1
            evac(p, st.k, zf[:, fch, :], bank[:, :], [b_bank], wp=[b_zf])

        fm_stream(c, [(win_d, bw, 0, 512, 0), (win_d, bw, 512, 512, 1)], 32,
                  lambda kc, tc0, tn: hT[:, kc, tc0:tc0 + tn], b_hT, [(0, T)], epi, wbufs, b_wbufs, banks, b_banks, st)
        for g in range(4):
            for jc in range(2):
                for ab in range(2):
                    bk = st.bi % 6
                    st.bi += 1
                    for cc in range(2):
                        p.op("pe", lambda e, bk=bk, ab=ab, cc=cc, jc=jc, g=g: e.matmul(
                            banks[bk][:, :], lhsT=tab[:, ab, cc, jc * 128:(jc + 1) * 128], rhs=zf[:, 2 * g + cc, :],
                            start=(cc == 0), stop=(cc == 1)), reads=[b_tab, b_zf], writes=[b_banks[bk]])
                    si = st.k % 3
                    st.k += 1
                    evac(p, st.k, stg[si][:], banks[bk][:, :], [b_banks[bk]], wb=[b_stg[si]])
                    r0 = g * 256 + jc * 128
                    p.dma("sp", ab_d[ab, r0:r0 + 128, t0:t0 + T], stg[si][:], reads=[b_stg[si]], wpart=[bab])


def ph_fft(c, m_d, bm, cs_d, bcs, tw_d, btw, fo_d, bfo):
    p = c.p
    M = c.sb([128, 2, 128, 128], BF16)
    b_M = Buf()
    p.dma("sp", M[:], m_d, reads=[bm], writes=[b_M])
    csf = c.sb([128, 2, 128], F32)
    tw = c.sb([128, 2, 128], F32)
    b_csf, b_tw = Buf(), Buf()
    p.dma("act", csf[:], cs_d.rearrange("a p k -> p a k"), reads=[bcs], writes=[b_csf])
    p.dma("act", tw[:], tw_d.rearrange("a p k -> p a k"), reads=[btw], writes=[b_tw])
    r1 = c.sb([128, 256], BF16)
    r2 = c.sb([128, 256], BF16)
    b_r = Buf()
    p.op("dve", lambda e: e.tensor_copy(out=r1[:, 0:128], in_=csf[:, 0, :]), reads=[b_csf], wpart=[b_r])
    p.op("dve", lambda e: e.tensor_scalar(out=r1[:, 128:256], in0=csf[:, 1, :], scalar1=-1.0, scalar2=None,
                                          op0=ALU.mult), reads=[b_csf], wpart=[b_r])
    p.op("dve", lambda e: e.tensor_copy(out=r2[:, 0:128], in_=csf[:, 1, :]), reads=[b_csf], wpart=[b_r])
    p.op("dve", lambda e: e.tensor_copy(out=r2[:, 128:256], in_=csf[:, 0, :]), reads=[b_csf], wpart=[b_r])
    Yr = c.sb([128, 128, 128], BF16)
    Yi = c.sb([128, 128, 128], BF16)
    b_Y = Buf()
    banks = [c.ps([128, 2, 2, 128], F32) for _ in range(4)]
    b_banks = [Buf() for _ in range(4)]
    tmp = [c.sb([128, 2, 128], F32) for _ in range(4)]
    b_tmp = [Buf() for _ in range(4)]
    tcb = tw[:, 0, :].unsqueeze(1).to_broadcast([128, 2, 128])
    tsb = tw[:, 1, :].unsqueeze(1).to_broadcast([128, 2, 128])
    for cp in range(64):
        bk = cp % 4
        for j in range(2):
            ch = cp * 2 + j
            p.op("pe", lambda e, bk=bk, j=j, ch=ch: e.matmul(banks[bk][:, j, :, :], lhsT=M[:, 0, ch, :], rhs=r1[:],
                                                            start=True, stop=False),
                 reads=[b_M, b_r], writes=[b_banks[bk]])
            p.op("pe", lambda e, bk=bk, j=j, ch=ch: e.matmul(banks[bk][:, j, :, :], lhsT=M[:, 1, ch, :], rhs=r2[:],
                                                            start=False, stop=True),
                 reads=[b_M, b_r], writes=[b_banks[bk]])
        yr = banks[bk][:, :, 0, :]
        yi = banks[bk][:, :, 1, :]
        ch0 = cp * 2
        p.op("dve", lambda e, yr=yr: e.tensor_tensor(out=tmp[0][:], in0=yr, in1=tcb, op=ALU.mult),
             reads=[b_banks[bk], b_tw], writes=[b_tmp[0]])
        p.op("dve", lambda e, yi=yi: e.tensor_tensor(out=tmp[1][:], in0=yi, in1=tsb, op=ALU.mult),
             reads=[b_banks[bk], b_tw], writes=[b_tmp[1]])
        p.op("dve", lambda e, yi=yi: e.tensor_tensor(out=tmp[2][:], in0=yi, in1=tcb, op=ALU.mult),
             reads=[b_banks[bk], b_tw], writes=[b_tmp[2]])
        p.op("dve", lambda e, yr=yr: e.tensor_tensor(out=tmp[3][:], in0=yr, in1=tsb, op=ALU.mult),
             reads=[b_banks[bk], b_tw], writes=[b_tmp[3]])
        p.op("pool", lambda e, ch0=ch0: e.tensor_tensor(out=Yr[:, ch0:ch0 + 2, :], in0=tmp[0][:], in1=tmp[1][:],
                                                        op=ALU.add), reads=[b_tmp[0], b_tmp[1]], wpart=[b_Y])
        p.op("pool", lambda e, ch0=ch0: e.tensor_tensor(out=Yi[:, ch0:ch0 + 2, :], in0=tmp[2][:], in1=tmp[3][:],
                                                        op=ALU.subtract), reads=[b_tmp[2], b_tmp[3]], wpart=[b_Y])
    cc_ = c.sb([128, 128], BF16)
    ss_ = c.sb([128, 128], BF16)
    b_c2 = Buf()
    p.op("dve", lambda e: e.tensor_copy(out=cc_[:], in_=csf[:, 0, :]), reads=[b_csf], wpart=[b_c2])
    p.op("dve", lambda e: e.tensor_copy(out=ss_[:], in_=csf[:, 1, :]), reads=[b_csf], wpart=[b_c2])
    fo = c.sb([128, 128, 128], BF16)
    b_fo = Buf()
    for q in range(32):
        bk = q % 4
        bnk = banks[bk][:].rearrange("p a b k -> p (a b) k")
        p.op("pe", lambda e, bnk=bnk, q=q: e.matmul(bnk, lhsT=cc_[:], rhs=Yr[:, q * 4:(q + 1) * 4, :],
                                                    start=True, stop=False), reads=[b_c2, b_Y], writes=[b_banks[bk]])
        p.op("pe", lambda e, bnk=bnk, q=q: e.matmul(bnk, lhsT=ss_[:], rhs=Yi[:, q * 4:(q + 1) * 4, :],
                                                    start=False, stop=True), reads=[b_c2, b_Y], writes=[b_banks[bk]])
        p.op("act", lambda e, bnk=bnk, q=q: e.activation(out=fo[:, q * 4:(q + 1) * 4, :], in_=bnk, func=AF.Copy,
                                                         scale=1.0 / 2048.0), reads=[b_banks[bk]], wpart=[b_fo])
    p.dma("sp", fo_d, fo[:], reads=[b_fo], writes=[bfo])


def rope_weights(c, w4, b_w4, nk, nh, hd, r0):
    p = c.p
    wr = c.sb([128, nk, nh, 64], BF16)
    b_wr = Buf()
    p.op("dve", lambda e: e.tensor_scalar(out=wr[:, :, :, 0:32], in0=w4[:, :, :, r0 + 32:r0 + 64], scalar1=-1.0,
                                          scalar2=None, op0=ALU.mult), reads=[b_w4], wpart=[b_wr])
    p.op("dve", lambda e: e.tensor_copy(out=wr[:, :, :, 32:64], in_=w4[:, :, :, r0:r0 + 32]),
         reads=[b_w4], wpart=[b_wr])
    return wr, b_wr


def rope_combine(c, br, b_br, brot, b_brot, cs, b_cs, tc0, tn, t1, b_t1, t2, b_t2, out, b_out_w):
    p = c.p
    p.op("dve", lambda e: e.tensor_tensor(out=t1[:64, :tn], in0=br[:64, :tn], in1=cs[:, 0, tc0:tc0 + tn], op=ALU.mult),
         reads=[b_br, b_cs], writes=[b_t1])
    p.op("dve", lambda e: e.tensor_tensor(out=t2[:64, :tn], in0=brot[:64, :tn], in1=cs[:, 1, tc0:tc0 + tn], op=ALU.mult),
         reads=[b_brot, b_cs], writes=[b_t2])
    p.op("pool", lambda e: e.tensor_tensor(out=out, in0=t1[:64, :tn], in1=t2[:64, :tn], op=ALU.add),
         reads=[b_t1, b_t2], writes=b_out_w)


def ph_q(c, hT_d, bh, win_d, bw, wuq_d, bwuq, gq_d, bgq, cs_d, bcs, qT_d, bq, ntok):
    p = c.p
    T = 512
    hT = c.sb([128, 32, T], BF16)
    b_hT = Buf()
    wbufs = [c.sb([128, 32, 512], BF16) for _ in range(2)]
    b_wbufs = [Buf(), Buf()]
    banks = [c.ps([128, 512], F32) for _ in range(6)]
    b_banks = [Buf() for _ in range(6)]
    ssb = c.ps([128, 512], F32)
    b_ssb = Buf()
    wuq = c.sb([128, 6, 8, 192], BF16)
    b_wuq = Buf()
    p.dma("pool", wuq[:], wuq_d.rearrange("(c p) (h d) -> p c h d", p=128, d=192), reads=[bwuq], writes=[b_wuq])
    wr, b_wr = rope_weights(c, wuq, b_wuq, 6, 8, 192, 128)
    gq = c.sb([128, 6], F32)
    b_gq = Buf()
    p.dma("act", gq[:], gq_d, reads=[bgq], writes=[b_gq])
    cs = c.sb([64, 2, ntok], F32)
    b_cs = Buf()
    p.dma("act", cs[:], cs_d.rearrange("a p t -> p a t"), reads=[bcs], writes=[b_cs])
    ones, b_ones = ones_bf(c, [128, 128])
    cq = c.sb([128, 6, T], BF16)
    cqn = c.sb([128, 6, T], BF16)
    b_cq, b_cqn = Buf(), Buf()
    sq = [c.sb([128, T], BF16) for _ in range(2)]
    b_sq = [Buf(), Buf()]
    rs = c.sb([128, T], F32)
    b_rs = Buf()
    stg = [c.sb([128, T], BF16) for _ in range(3)]
    b_stg = [Buf() for _ in range(3)]
    t1 = c.sb([64, T], F32)
    t2 = c.sb([64, T], F32)
    b_t1, b_t2 = Buf(), Buf()
    st = St()
    for s in range(ntok // T):
        t0 = s * T
        load_hT(c, hT, b_hT, hT_d, bh, t0, T)

        def epi(tag, ch, ti, bank, b_bank, wd):
            qc = tag * 4 + ch
            p.op("act", lambda e: e.activation(out=cq[:, qc, :], in_=bank[:, :], func=AF.Copy),
                 reads=[b_bank], wpart=[b_cq])
            si = qc % 2
            p.op("act", lambda e: e.activation(out=sq[si][:], in_=bank[:, :], func=AF.Square),
                 reads=[b_bank], writes=[b_sq[si]])
            p.op("pe", lambda e: e.matmul(ssb[:, :], lhsT=ones[:], rhs=sq[si][:], start=(qc == 0), stop=(qc == 5)),
                 reads=[b_ones, b_sq[si]], writes=[b_ssb])

        fm_stream(c, [(win_d, bw, 1024, 512, 0), (win_d, bw, 1536, 256, 1)], 32,
                  lambda kc, tc0, tn: hT[:, kc, tc0:tc0 + tn], b_hT, [(0, T)], epi, wbufs, b_wbufs, banks, b_banks, st)
        fm_rmsnorm(c, cq, b_cq, 6, 768, T, gq, b_gq, ssb, b_ssb, rs, b_rs, cqn, b_cqn)
        for h in range(8):
            bk = st.bi % 6
            st.bi += 1
            for kc in range(6):
                p.op("pe", lambda e, bk=bk, kc=kc, h=h: e.matmul(banks[bk][:, :], lhsT=wuq[:, kc, h, 0:128],
                                                                 rhs=cqn[:, kc, :], start=(kc == 0), stop=(kc == 5)),
                     reads=[b_wuq, b_cqn], writes=[b_banks[bk]])
            si = st.k % 3
            st.k += 1
            evac(p, st.k, stg[si][:], banks[bk][:, :], [b_banks[bk]], wb=[b_stg[si]])
            p.dma("sp", qT_d[h, 0:128, t0:t0 + T], stg[si][:], reads=[b_stg[si]], wpart=[bq])
            bk1 = st.bi % 6
            bk2 = (st.bi + 1) % 6
            st.bi += 2
            for kc in range(6):
                p.op("pe", lambda e, bk1=bk1, kc=kc, h=h: e.matmul(banks[bk1][:64, :], lhsT=wuq[:, kc, h, 128:192],
                                                                   rhs=cqn[:, kc, :], start=(kc == 0), stop=(kc == 5)),
                     reads=[b_wuq, b_cqn], writes=[b_banks[bk1]])
            for kc in range(6):
                p.op("pe", lambda e, bk2=bk2, kc=kc, h=h: e.matmul(banks[bk2][:64, :], lhsT=wr[:, kc, h, :],
                                                                   rhs=cqn[:, kc, :], start=(kc == 0), stop=(kc == 5)),
                     reads=[b_wr, b_cqn], writes=[b_banks[bk2]])
            si = st.k % 3
            st.k += 1
            rope_combine(c, banks[bk1], b_banks[bk1], banks[bk2], b_banks[bk2], cs, b_cs, t0, T, t1, b_t1, t2, b_t2,
                         stg[si][:64, :], [b_stg[si]])
            p.dma("sp", qT_d[h, 128:192, t0:t0 + T], stg[si][:64, :], reads=[b_stg[si]], wpart=[bq])


def ph_kv(c, hT_d, bh, win_d, bw, wukv_d, bwukv, gkv_d, bgkv, cs_d, bcs, kT_d, bk_, krT_d, bkr, v_d, bv, ntok):
    p = c.p
    T = 512
    hT = c.sb([128, 32, T], BF16)
    b_hT = Buf()
    wbufs = [c.sb([128, 32, 512], BF16) for _ in range(2)]
    b_wbufs = [Buf(), Buf()]
    banks = [c.ps([128, 512], F32) for _ in range(6)]
    b_banks = [Buf() for _ in range(6)]
    ssb = c.ps([128, 512], F32)
    b_ssb = Buf()
    wukv = c.sb([128, 4, 8, 256], BF16)
    b_wukv = Buf()
    p.dma("pool", wukv[:], wukv_d.rearrange("(c p) (h d) -> p c h d", p=128, d=256), reads=[bwukv], writes=[b_wukv])
    wkr = c.sb([128, 32, 1, 64], BF16)
    b_wkr = Buf()
    p.dma("pool", wkr[:, :, 0, :], win_d[:, 2304:2368].rearrange("(c p) n -> p c n", p=128), reads=[bw], writes=[b_wkr])
    wkrr, b_wkrr = rope_weights(c, wkr, b_wkr, 32, 1, 64, 0)
    gkv = c.sb([128, 4], F32)
    b_gkv = Buf()
    p.dma("act", gkv[:], gkv_d, reads=[bgkv], writes=[b_gkv])
    cs = c.sb([64, 2, ntok], F32)
    b_cs = Buf()
    p.dma("act", cs[:], cs_d.rearrange("a p t -> p a t"), reads=[bcs], writes=[b_cs])
    ones, b_ones = ones_bf(c, [128, 128])
    ck = c.sb([128, 4, T], BF16)
    ckn = c.sb([128, 4, T], BF16)
    b_ck, b_ckn = Buf(), Buf()
    sq = [c.sb([128, T], BF16) for _ in range(2)]
    b_sq = [Buf(), Buf()]
    rs = c.sb([128, T], F32)
    b_rs = Buf()
    stg = [c.sb([128, T], BF16) for _ in range(3)]
    b_stg = [Buf() for _ in range(3)]
    t1 = c.sb([64, T], F32)
    t2 = c.sb([64, T], F32)
    b_t1, b_t2 = Buf(), Buf()
    st = St()
    for s in range(ntok // T):
        t0 = s * T
        load_hT(c, hT, b_hT, hT_d, bh, t0, T)

        def epi(tag, ch, ti, bank, b_bank, wd):
            qc = ch
            p.op("act", lambda e: e.activation(out=ck[:, qc, :], in_=bank[:, :], func=AF.Copy),
                 reads=[b_bank], wpart=[b_ck])
            si = qc % 2
            p.op("act", lambda e: e.activation(out=sq[si][:], in_=bank[:, :], func=AF.Square),
                 reads=[b_bank], writes=[b_sq[si]])
            p.op("pe", lambda e: e.matmul(ssb[:, :], lhsT=ones[:], rhs=sq[si][:], start=(qc == 0), stop=(qc == 3)),
                 reads=[b_ones, b_sq[si]], writes=[b_ssb])

        fm_stream(c, [(win_d, bw, 1792, 512, 0)], 32,
                  lambda kc, tc0, tn: hT[:, kc, tc0:tc0 + tn], b_hT, [(0, T)], epi, wbufs, b_wbufs, banks, b_banks, st)
        fm_rmsnorm(c, ck, b_ck, 4, 512, T, gkv, b_gkv, ssb, b_ssb, rs, b_rs, ckn, b_ckn)
        for h in range(8):
            bk = st.bi % 6
            st.bi += 1
            for kc in range(4):
                p.op("pe", lambda e, bk=bk, kc=kc, h=h: e.matmul(banks[bk][:, :], lhsT=wukv[:, kc, h, 0:128],
                                                                 rhs=ckn[:, kc, :], start=(kc == 0), stop=(kc == 3)),
                     reads=[b_wukv, b_ckn], writes=[b_banks[bk]])
            si = st.k % 3
            st.k += 1
            evac(p, st.k, stg[si][:], banks[bk][:, :], [b_banks[bk]], wb=[b_stg[si]])
            p.dma("sp", kT_d[h, :, t0:t0 + T], stg[si][:], reads=[b_stg[si]], wpart=[bk_])
        for tl in range(T // 128):
            for hb in range(2):
                bk = st.bi % 6
                st.bi += 1
                for kc in range(4):
                    p.op("pe", lambda e, bk=bk, kc=kc, hb=hb, tl=tl: e.matmul(
                        banks[bk][:, :].rearrange("p (h d) -> p h d", d=128),
                        lhsT=ckn[:, kc, tl * 128:(tl + 1) * 128],
                        rhs=wukv[:, kc, hb * 4:(hb + 1) * 4, 128:256], start=(kc == 0), stop=(kc == 3)),
                        reads=[b_wukv, b_ckn], writes=[b_banks[bk]])
                si = st.k % 3
                st.k += 1
                evac(p, st.k, stg[si][:], banks[bk][:, :], [b_banks[bk]], wb=[b_stg[si]])
                r0 = t0 + tl * 128
                p.dma("sp", v_d[r0:r0 + 128, hb * 512:(hb + 1) * 512], stg[si][:], reads=[b_stg[si]], wpart=[bv])
        bk1 = st.bi % 6
        bk2 = (st.bi + 1) % 6
        st.bi += 2
        for kc in range(32):
            p.op("pe", lambda e, bk1=bk1, kc=kc: e.matmul(banks[bk1][:64, :], lhsT=wkr[:, kc, 0, :], rhs=hT[:, kc, :],
                                                          start=(kc == 0), stop=(kc == 31)),
                 reads=[b_wkr, b_hT], writes=[b_banks[bk1]])
        for kc in range(32):
            p.op("pe", lambda e, bk2=bk2, kc=kc: e.matmul(banks[bk2][:64, :], lhsT=wkrr[:, kc, 0, :], rhs=hT[:, kc, :],
                                                          start=(kc == 0), stop=(kc == 31)),
                 reads=[b_wkrr, b_hT], writes=[b_banks[bk2]])
        si = st.k % 3
        st.k += 1
        rope_combine(c, banks[bk1], b_banks[bk1], banks[bk2], b_banks[bk2], cs, b_cs, t0, T, t1, b_t1, t2, b_t2,
                     stg[si][:64, :], [b_stg[si]])
        p.dma("sp", krT_d[:, t0:t0 + T], stg[si][:64, :], reads=[b_stg[si]], wpart=[bkr])


def ph_attn(c, qT_d, bq, kT_d, bk_, krT_d, bkr, v_d, bv, aT_d, ba, ntok, nkeys, nheads=8):
    p = c.p
    NKT = nkeys // 128
    scale = 1.0 / math.sqrt(192.0)
    idb, b_id = c.ident()
    kr = c.sb([64, nkeys], BF16)
    b_kr = Buf()
    p.dma("sp", kr[:], krT_d, reads=[bkr], writes=[b_kr])
    kT = c.sb([128, nkeys], BF16)
    b_kT = Buf()
    V = c.sb([128, NKT, 132], BF16)
    b_V = Buf()
    p.op("pool", lambda e: e.memset(V[:, :, 128:132], 1.0), wpart=[b_V])
    qn = c.sb([128, ntok], BF16)
    qr = c.sb([64, ntok], BF16)
    b_qn, b_qr = Buf(), Buf()
    NSB = 3
    sbank = [c.ps([128, 512], F32) for _ in range(NSB)]
    b_sbank = [Buf() for _ in range(NSB)]
    obank = [c.ps([128, 132], F32) for _ in range(4)]
    b_obank = [Buf() for _ in range(4)]
    tbank = c.ps([128, 4, 128], BF16)
    b_tbank = Buf()
    P = [c.sb([128, 512], BF16) for _ in range(NSB)]
    b_P = [Buf() for _ in range(NSB)]
    rec = c.sb([128, 1], F32)
    b_rec = Buf()
    on = c.sb([128, 4, 128], BF16)
    b_on = Buf()
    ast = [c.sb([128, 512], BF16) for _ in range(2)]
    b_ast = [Buf(), Buf()]
    k = 0
    for h in range(nheads):
        p.dma("sp", kT[:], kT_d[h], reads=[bk_], writes=[b_kT])
        p.dma("act", V[:, :, 0:128], v_d[:, h * 128:(h + 1) * 128].rearrange("(t p) d -> p t d", p=128),
              reads=[bv], wpart=[b_V])
        p.dma("sp", qn[:], qT_d[h, 0:128, :], reads=[bq], writes=[b_qn])
        p.dma("sp", qr[:], qT_d[h, 128:192, :], reads=[bq], writes=[b_qr])
        for qg in range(ntok // 512):
            q0 = qg * 512
            kbase = k
            k += NKT

            def emit_S(kt, q0=q0, kbase=kbase):
                sb_ = (kbase + kt) % NSB
                p.op("pe", lambda e: e.matmul(sbank[sb_][:, :], lhsT=kT[:, kt * 128:(kt + 1) * 128],
                                              rhs=qn[:, q0:q0 + 512], start=True, stop=False),
                     reads=[b_kT, b_qn], writes=[b_sbank[sb_]])
                p.op("pe", lambda e: e.matmul(sbank[sb_][:, :], lhsT=kr[:, kt * 128:(kt + 1) * 128],
                                              rhs=qr[:, q0:q0 + 512], start=False, stop=True),
                     reads=[b_kr, b_qr], writes=[b_sbank[sb_]])
                p.op("act", lambda e: e.activation(out=P[sb_][:], in_=sbank[sb_][:, :], func=AF.Exp, scale=scale),
                     reads=[b_sbank[sb_]], writes=[b_P[sb_]])

            emit_S(0)
            for kt in range(NKT):
                if kt + 1 < NKT:
                    emit_S(kt + 1)
                sb_ = (kbase + kt) % NSB
                for qt in range(4):
                    p.op("pe", lambda e, sb_=sb_, kt=kt, qt=qt: e.matmul(obank[qt][:, 0:129],
                                                                         lhsT=P[sb_][:, qt * 128:(qt + 1) * 128],
                                                                         rhs=V[:, kt, 0:129], start=(kt == 0),
                                                                         stop=(kt == NKT - 1)),
                         reads=[b_P[sb_], b_V], writes=[b_obank[qt]])
            for qt in range(4):
                p.op("dve", lambda e, qt=qt: e.reciprocal(out=rec[:], in_=obank[qt][:, 128:129]),
                     reads=[b_obank[qt]], writes=[b_rec])
                p.op("dve", lambda e, qt=qt: e.tensor_scalar(out=on[:, qt, :], in0=obank[qt][:, 0:128],
                                                             scalar1=rec[:, 0:1], scalar2=None, op0=ALU.mult),
                     reads=[b_obank[qt], b_rec], wpart=[b_on])
            for qt in range(4):
                p.op("pe", lambda e, qt=qt: e.transpose(out=tbank[:, qt, :], in_=on[:, qt, :], identity=idb[:]),
                     reads=[b_on, b_id], writes=[b_tbank])
            ai = (h * (ntok // 512) + qg) % 2
            p.op("act", lambda e, ai=ai: e.activation(out=ast[ai][:], in_=tbank[:].rearrange("p a b -> p (a b)"),
                                                      func=AF.Copy), reads=[b_tbank], writes=[b_ast[ai]])
            p.dma("sp", aT_d[h * 128:(h + 1) * 128, q0:q0 + 512], ast[ai][:], reads=[b_ast[ai]], wpart=[ba])


def ph_sg(c, hT_d, bh, win_d, bw, gsg_d, bgsg, wsT_d, bws, bsb_d, bbs, sgT_d, bsg, ntok):
    p = c.p
    T = 512
    NTL = T // 128
    hT = c.sb([128, 32, T], BF16)
    b_hT = Buf()
    wbufs = [c.sb([128, 32, 512], BF16) for _ in range(2)]
    b_wbufs = [Buf(), Buf()]
    banks = [c.ps([128, 512], F32) for _ in range(8)]
    b_banks = [Buf() for _ in range(8)]
    gsg = c.sb([128, 2048], F32)
    b_gsg = Buf()
    p.dma("act", gsg[:], gsg_d, reads=[bgsg], writes=[b_gsg])
    wsT = c.sb([128, 16, 128], BF16)
    b_wsT = Buf()
    p.dma("pool", wsT[:], wsT_d.rearrange("g q p -> q g p"), reads=[bws], writes=[b_wsT])
    bsb = c.sb([128, 16, 128], F32)
    b_bsb = Buf()
    p.dma("act", bsb[:], bsb_d, reads=[bbs], writes=[b_bsb])
    vn = c.sb([128, NTL, 2048], BF16)
    b_vn = Buf()
    gv = [c.sb([128, 4, 128], F32) for _ in range(2)]
    b_gv = [Buf(), Buf()]
    xc = [c.sb([128, 4, 128], F32) for _ in range(2)]
    b_xc = [Buf(), Buf()]
    sqj = c.sb([128, 4, 128], F32)
    b_sqj = Buf()
    mu = [c.sb([128, 4], F32) for _ in range(2)]
    b_mu = [Buf(), Buf()]
    var = [c.sb([128, 4], F32) for _ in range(2)]
    b_var = [Buf(), Buf()]
    uT = [c.sb([128, T], F32) for _ in range(2)]
    b_uT = [Buf(), Buf()]
    tt = [c.sb([128, T], F32) for _ in range(2)]
    b_tt = [Buf(), Buf()]
    stg = [c.sb([128, T], BF16) for _ in range(2)]
    b_stg = [Buf(), Buf()]
    st = St()
    for s in range(ntok // T):
        t0 = s * T
        load_hT(c, hT, b_hT, hT_d, bh, t0, T)
        for vb in range(4):
            w = st.wi % 2
            st.wi += 1
            col0 = 4416 + vb * 512
            p.dma("pool", wbufs[w][:], win_d[:, col0:col0 + 512].rearrange("(c p) n -> p c n", p=128),
                  reads=[bw], writes=[b_wbufs[w]])
            for tl in range(NTL):
                bk = st.bi % 8
                st.bi += 1
                for kc in range(32):
                    p.op("pe", lambda e, bk=bk, kc=kc, tl=tl, w=w: e.matmul(
                        banks[bk][:, :], lhsT=hT[:, kc, tl * 128:(tl + 1) * 128], rhs=wbufs[w][:, kc, :],
                        start=(kc == 0), stop=(kc == 31)), reads=[b_hT, b_wbufs[w]], writes=[b_banks[bk]])
                i = st.k % 2
                st.k += 1
                bank3 = banks[bk][:, :].rearrange("p (g d) -> p g d", d=128)
                p.op("act", lambda e, i=i, bank3=bank3: e.activation(out=gv[i][:], in_=bank3, func=AF.Gelu_apprx_tanh),
                     reads=[b_banks[bk]], writes=[b_gv[i]])
                p.op("dve", lambda e, i=i: e.reduce_sum(out=mu[i][:], in_=gv[i][:], axis=AX.X),
                     reads=[b_gv[i]], writes=[b_mu[i]])
                p.op("dve", lambda e, i=i: e.tensor_scalar(out=mu[i][:], in0=mu[i][:], scalar1=1.0 / 128, scalar2=None,
                                                           op0=ALU.mult), reads=[b_mu[i]], writes=[b_mu[i]])
                p.op("dve", lambda e, i=i: e.tensor_tensor(out=xc[i][:], in0=gv[i][:],
                                                           in1=mu[i][:].unsqueeze(2).to_broadcast([128, 4, 128]),
                                                           op=ALU.subtract), reads=[b_gv[i], b_mu[i]], writes=[b_xc[i]])
                p.op("pool", lambda e, i=i: e.tensor_tensor(out=sqj[:], in0=xc[i][:], in1=xc[i][:], op=ALU.mult),
                     reads=[b_xc[i]], writes=[b_sqj])
                p.op("dve", lambda e, i=i: e.reduce_sum(out=var[i][:], in_=sqj[:], axis=AX.X),
                     reads=[b_sqj], writes=[b_var[i]])
                rstd_ops(p, "dve", var[i][:], var[i][:], 128, [b_var[i]], [b_var[i]])
                p.op("dve", lambda e, i=i: e.tensor_tensor(out=xc[i][:], in0=xc[i][:],
                                                           in1=var[i][:].unsqueeze(2).to_broadcast([128, 4, 128]),
                                                           op=ALU.mult), reads=[b_xc[i], b_var[i]], writes=[b_xc[i]])
                p.op("pool", lambda e, i=i, tl=tl, vb=vb: e.tensor_tensor(
                    out=vn[:, tl, vb * 512:(vb + 1) * 512], in0=xc[i][:].rearrange("p g d -> p (g d)"),
                    in1=gsg[:, vb * 512:(vb + 1) * 512], op=ALU.mult),
                    reads=[b_xc[i], b_gsg], wpart=[b_vn])

        def epi(tag, ch, ti, bank, b_bank, wd):
            g = tag * 4 + ch
            i = g % 2
            p.op("act", lambda e: e.activation(out=uT[i][:], in_=bank[:, :], func=AF.Gelu_apprx_tanh),
                 reads=[b_bank], writes=[b_uT[i]])
            bk = st.bi % 8
            st.bi += 1
            for tl in range(NTL):
                p.op("pe", lambda e, tl=tl: e.matmul(banks[bk][:, tl * 128:(tl + 1) * 128],
                                                     lhsT=vn[:, tl, g * 128:(g + 1) * 128], rhs=wsT[:, g, :],
                                                     start=True, stop=True),
                     reads=[b_vn, b_wsT], writes=[b_banks[bk]])
            p.op("dve", lambda e: e.tensor_tensor(out=tt[i][:].rearrange("p (a b) -> p a b", b=128),
                                                  in0=banks[bk][:, :].rearrange("p (a b) -> p a b", b=128),
                                                  in1=bsb[:, g, :].unsqueeze(1).to_broadcast([128, NTL, 128]),
                                                  op=ALU.add), reads=[b_banks[bk], b_bsb], writes=[b_tt[i]])
            p.op("pool", lambda e: e.tensor_tensor(out=stg[i][:], in0=tt[i][:], in1=uT[i][:], op=ALU.mult),
                 reads=[b_tt[i], b_uT[i]], writes=[b_stg[i]])
            p.dma("sp", sgT_d[g * 128:(g + 1) * 128, t0:t0 + T], stg[i][:], reads=[b_stg[i]], wpart=[bsg])

        fm_stream(c, [(win_d, bw, 2368 + 512 * i, 512, i) for i in range(4)], 32,
                  lambda kc, tc0, tn: hT[:, kc, tc0:tc0 + tn], b_hT, [(0, T)], epi, wbufs, b_wbufs, banks, b_banks, st)


def ph_merge(c, hT_d, bh, fT_d, bf, aT_d, ba, sgT_d, bsg, wg_d, bwg, bgT_d, bbg, wf_d, bwf, wa_d, bwa, ws_d, bws,
             mT_d, bm, ntok):
    p = c.p
    T = 512
    CW = 256
    hT = c.sb([128, 32, T], BF16)
    fT = c.sb([128, 8, T], BF16)
    aT = c.sb([128, 8, T], BF16)
    sT = c.sb([128, 16, T], BF16)
    b_hT, b_fT, b_aT, b_sT = Buf(), Buf(), Buf(), Buf()
    wg = [c.sb([128, 32, 3, CW], BF16) for _ in range(2)]
    wf = [c.sb([128, 8, CW], BF16) for _ in range(2)]
    wa = [c.sb([128, 8, CW], BF16) for _ in range(2)]
    ws = [c.sb([128, 16, CW], BF16) for _ in range(2)]
    b_wg, b_wf, b_wa, b_ws = [Buf(), Buf()], [Buf(), Buf()], [Buf(), Buf()], [Buf(), Buf()]
    bgT = c.sb([128, 96], F32)
    b_bgT = Buf()
    p.dma("act", bgT[:], bgT_d, reads=[bbg], writes=[b_bgT])
    banks = [c.ps([128, 512], F32) for _ in range(8)]
    b_banks = [Buf() for _ in range(8)]
    gt = [c.sb([128, T], F32) for _ in range(3)]
    b_gt = [Buf() for _ in range(3)]
    m1 = c.sb([128, T], F32)
    m2 = c.sb([128, T], F32)
    b_m1, b_m2 = Buf(), Buf()
    stg = [c.sb([128, T], BF16) for _ in range(2)]
    b_stg = [Buf(), Buf()]
    bi = 0
    wi = 0
    for s in range(ntok // T):
        t0 = s * T
        load_hT(c, hT, b_hT, hT_d, bh, t0, T)
        p.dma("sp", fT[:], fT_d[:, t0:t0 + T].rearrange("(c p) t -> p c t", p=128), reads=[bf], writes=[b_fT])
        p.dma("sp", aT[:], aT_d[:, t0:t0 + T].rearrange("(c p) t -> p c t", p=128), reads=[ba], writes=[b_aT])
        p.dma("sp", sT[:], sgT_d[:, t0:t0 + T].rearrange("(c p) t -> p c t", p=128), reads=[bsg], writes=[b_sT])
        for jb in range(D // CW):
            w = wi % 2
            wi += 1
            c0 = jb * CW
            for b in range(3):
                p.dma("pool", wg[w][:, :, b, :],
                      wg_d[:, b * D + c0:b * D + c0 + CW].rearrange("(c p) n -> p c n", p=128),
                      reads=[bwg], wpart=[b_wg[w]])
            p.dma("pool", wf[w][:], wf_d[:, c0:c0 + CW].rearrange("(c p) n -> p c n", p=128), reads=[bwf], writes=[b_wf[w]])
            p.dma("pool", wa[w][:], wa_d[:, c0:c0 + CW].rearrange("(c p) n -> p c n", p=128), reads=[bwa], writes=[b_wa[w]])
            p.dma("pool", ws[w][:], ws_d[:, c0:c0 + CW].rearrange("(c p) n -> p c n", p=128), reads=[bws], writes=[b_ws[w]])
            for sub in range(CW // 128):
                j = jb * (CW // 128) + sub
                n0 = j * 128
                cs_ = slice(sub * 128, (sub + 1) * 128)
                gb_ = []
                for b in range(3):
                    bk = bi % 8
                    bi += 1
                    gb_.append(bk)
                    for kc in range(32):
                        p.op("pe", lambda e, bk=bk, kc=kc, b=b, w=w, cs_=cs_: e.matmul(
                            banks[bk][:, :], lhsT=wg[w][:, kc, b, cs_], rhs=hT[:, kc, :], start=(kc == 0), stop=(kc == 31)),
                            reads=[b_wg[w], b_hT], writes=[b_banks[bk]])
                yb_ = []
                for b, (wt, b_wt, act, b_act, nk) in enumerate(((wf, b_wf, fT, b_fT, 8), (wa, b_wa, aT, b_aT, 8),
                                                                (ws, b_ws, sT, b_sT, 16))):
                    bk = bi % 8
                    bi += 1
                    yb_.append(bk)
                    for kc in range(nk):
                        p.op("pe", lambda e, bk=bk, kc=kc, wt=wt, act=act, nk=nk, w=w, cs_=cs_: e.matmul(
                            banks[bk][:, :], lhsT=wt[w][:, kc, cs_], rhs=act[:, kc, :], start=(kc == 0),
                            stop=(kc == nk - 1)), reads=[b_wt[w], b_act], writes=[b_banks[bk]])
                for b in range(3):
                    p.op("act", lambda e, b=b, bk=gb_[b], j=j: e.activation(out=gt[b][:], in_=banks[bk][:, :],
                                                                            func=AF.Sigmoid,
                                                                            bias=bgT[:, b * 32 + j:b * 32 + j + 1]),
                         reads=[b_banks[gb_[b]], b_bgT], writes=[b_gt[b]])
                p.op("dve", lambda e, bk=yb_[0]: e.tensor_tensor(out=m1[:], in0=banks[bk][:, :], in1=gt[0][:], op=ALU.mult),
                     reads=[b_banks[yb_[0]], b_gt[0]], writes=[b_m1])
                p.op("dve", lambda e, bk=yb_[1]: e.tensor_tensor(out=m2[:], in0=banks[bk][:, :], in1=gt[1][:], op=ALU.mult),
                     reads=[b_banks[yb_[1]], b_gt[1]], writes=[b_m2])
                p.op("pool", lambda e: e.tensor_tensor(out=m1[:], in0=m1[:], in1=m2[:], op=ALU.add),
                     reads=[b_m1, b_m2], writes=[b_m1])
                p.op("dve", lambda e, bk=yb_[2]: e.tensor_tensor(out=m2[:], in0=banks[bk][:, :], in1=gt[2][:], op=ALU.mult),
                     reads=[b_banks[yb_[2]], b_gt[2]], writes=[b_m2])
                si = j % 2
                p.op("pool", lambda e, si=si: e.tensor_tensor(out=stg[si][:], in0=m1[:], in1=m2[:], op=ALU.add),
                     reads=[b_m1, b_m2], writes=[b_stg[si]])
                p.dma("sp", mT_d[n0:n0 + 128, t0:t0 + T], stg[si][:], reads=[b_stg[si]], wpart=[bm])


def ph_memkv(c, mT_d, bmT, wmk_d, bwk, wmv_d, bwv, kmT_d, bkm, vm_d, bvm):
    p = c.p
    mT = c.sb([128, 32, 256], BF16)
    b_mT = Buf()
    load_hT(c, mT, b_mT, mT_d, bmT, 0, 256)
    wbufs = [c.sb([128, 32, 512], BF16) for _ in range(2)]
    b_wbufs = [Buf(), Buf()]
    banks = [c.ps([128, 512], F32) for _ in range(4)]
    b_banks = [Buf() for _ in range(4)]
    stg = [c.sb([128, 512], BF16) for _ in range(2)]
    b_stg = [Buf(), Buf()]
    st = St()

    def epi(tag, ch, ti, bank, b_bank, wd):
        si = st.k % 2
        st.k += 1
        evac(p, st.k, stg[si][:, :256], bank[:, :256], [b_bank], wb=[b_stg[si]])
        r0 = (tag * 4 + ch) * 128
        p.dma("sp", kmT_d[r0:r0 + 128, :], stg[si][:, :256], reads=[b_stg[si]], wpart=[bkm])

    fm_stream(c, [(wmk_d, bwk, 0, 512, 0), (wmk_d, bwk, 512, 512, 1)], 32,
              lambda kc, tc0, tn: mT[:, kc, tc0:tc0 + tn], b_mT, [(0, 256)], epi, wbufs, b_wbufs, banks, b_banks, st)
    for vb in range(2):
        w = st.wi % 2
        st.wi += 1
        p.dma("pool", wbufs[w][:], wmv_d[:, vb * 512:(vb + 1) * 512].rearrange("(c p) n -> p c n", p=128),
              reads=[bwv], writes=[b_wbufs[w]])
        for mt in range(2):
            bk = st.bi % 4
            st.bi += 1
            for kc in range(32):
                p.op("pe", lambda e, bk=bk, kc=kc, mt=mt, w=w: e.matmul(banks[bk][:, :], lhsT=mT[:, kc, mt * 128:(mt + 1) * 128],
                                                                        rhs=wbufs[w][:, kc, :], start=(kc == 0), stop=(kc == 31)),
                     reads=[b_mT, b_wbufs[w]], writes=[b_banks[bk]])
            si = st.k % 2
            st.k += 1
            evac(p, st.k, stg[si][:], banks[bk][:, :], [b_banks[bk]], wb=[b_stg[si]])
            p.dma("sp", vm_d[mt * 128:(mt + 1) * 128, vb * 512:(vb + 1) * 512], stg[si][:], reads=[b_stg[si]], wpart=[bvm])


def ph_mq(c, hT_d, bh, wmq_d, bwq, kmT_d, bkm, vm_d, bvm, omT_d, bom, ntok):
    p = c.p
    T = 512
    idb, b_id = c.ident()
    hT = c.sb([128, 32, T], BF16)
    b_hT = Buf()
    wbufs = [c.sb([128, 32, 512], BF16) for _ in range(2)]
    b_wbufs = [Buf(), Buf()]
    banks = [c.ps([128, 512], F32) for _ in range(6)]
    b_banks = [Buf() for _ in range(6)]
    tbank = c.ps([128, 4, 2, 128], BF16)
    b_tbank = Buf()
    km = c.sb([128, 8, 256], BF16)
    b_km = Buf()
    p.dma("sp", km[:], kmT_d.rearrange("(c p) m -> p c m", p=128), reads=[bkm], writes=[b_km])
    vm = c.sb([128, 2, 4, 260], BF16)
    b_vm = Buf()
    p.op("pool", lambda e: e.memset(vm[:, :, :, 256:260], 1.0), wpart=[b_vm])
    for mt in range(2):
        p.dma("sp", vm[:, mt, :, 0:256], vm_d[mt * 128:(mt + 1) * 128, :].rearrange("p (h d) -> p h d", d=256),
              reads=[bvm], wpart=[b_vm])
    qm = c.sb([128, 8, T], BF16)
    b_qm = Buf()
    P = [c.sb([128, T], BF16) for _ in range(4)]
    b_P = [Buf() for _ in range(4)]
    rec = c.sb([128, 1], F32)
    b_rec = Buf()
    on = c.sb([128, 4, 256], BF16)
    b_on = Buf()
    stg = [c.sb([128, 2, T], BF16) for _ in range(2)]
    b_stg = [Buf(), Buf()]
    st = St()
    pk = 0
    for s in range(ntok // T):
        t0 = s * T
        load_hT(c, hT, b_hT, hT_d, bh, t0, T)

        def epi(tag, ch, ti, bank, b_bank, wd):
            st.k += 1
            evac(p, st.k, qm[:, tag * 4 + ch, :], bank[:, :], [b_bank], wp=[b_qm])

        fm_stream(c, [(wmq_d, bwq, 0, 512, 0), (wmq_d, bwq, 512, 512, 1)], 32,
                  lambda kc, tc0, tn: hT[:, kc, tc0:tc0 + tn], b_hT, [(0, T)], epi, wbufs, b_wbufs, banks, b_banks, st)
        for h in range(4):
            pi = []
            for mt in range(2):
                bk = st.bi % 6
                st.bi += 1
                for dc in range(2):
                    p.op("pe", lambda e, bk=bk, dc=dc, mt=mt, h=h: e.matmul(
                        banks[bk][:, :], lhsT=km[:, 2 * h + dc, mt * 128:(mt + 1) * 128], rhs=qm[:, 2 * h + dc, :],
                        start=(dc == 0), stop=(dc == 1)), reads=[b_km, b_qm], writes=[b_banks[bk]])
                pj = pk % 4
                pk += 1
                pi.append(pj)
                p.op("act", lambda e, bk=bk, pj=pj: e.activation(out=P[pj][:], in_=banks[bk][:, :], func=AF.Exp,
                                                                 scale=1.0 / 16.0), reads=[b_banks[bk]], writes=[b_P[pj]])
            for qt in range(4):
                bk = st.bi % 6
                st.bi += 1
                for mt in range(2):
                    p.op("pe", lambda e, bk=bk, mt=mt, qt=qt, h=h, pj=pi[mt]: e.matmul(
                        banks[bk][:, 0:257], lhsT=P[pj][:, qt * 128:(qt + 1) * 128], rhs=vm[:, mt, h, 0:257],
                        start=(mt == 0), stop=(mt == 1)), reads=[b_P[pi[mt]], b_vm], writes=[b_banks[bk]])
                p.op("dve", lambda e, bk=bk: e.reciprocal(out=rec[:], in_=banks[bk][:, 256:257]),
                     reads=[b_banks[bk]], writes=[b_rec])
                p.op("dve", lambda e, bk=bk, qt=qt: e.tensor_scalar(out=on[:, qt, :], in0=banks[bk][:, 0:256],
                                                                    scalar1=rec[:, 0:1], scalar2=None, op0=ALU.mult),
                     reads=[b_banks[bk], b_rec], wpart=[b_on])
            for qt in range(4):
                for dc in range(2):
                    p.op("pe", lambda e, qt=qt, dc=dc: e.transpose(out=tbank[:, qt, dc, :],
                                                                   in_=on[:, qt, dc * 128:(dc + 1) * 128], identity=idb[:]),
                         reads=[b_on, b_id], writes=[b_tbank])
            si = h % 2
            p.op("act", lambda e, si=si: e.activation(out=stg[si][:].rearrange("p d (q t) -> p q d t", t=128),
                                                      in_=tbank[:], func=AF.Copy), reads=[b_tbank], writes=[b_stg[si]])
            p.dma("sp", omT_d[h * 256:(h + 1) * 256, t0:t0 + T].rearrange("(d p) t -> p d t", p=128), stg[si][:],
                  reads=[b_stg[si]], wpart=[bom])


def ph_up(c, hT_d, bh, wup_d, bwu, cw_d, bcw, pT_d, bpT, ntok):
    p = c.p
    T = 1024
    G = 342
    hT = c.sb([128, 32, T + 2], BF16)
    b_hT = Buf()
    wbufs = [c.sb([128, 32, 256], BF16) for _ in range(4)]
    b_wbufs = [Buf() for _ in range(4)]
    banks = [c.ps([128, 512], F32) for _ in range(8)]
    b_banks = [Buf() for _ in range(8)]
    cw = c.sb([128, 172, 4], F32)
    b_cw = Buf()
    p.dma("act", cw[:], cw_d, reads=[bcw], writes=[b_cw])
    ue = {(k, ch): c.sb([128, T + 2], F32) for k in "gv" for ch in range(2)}
    b_ue = {k: Buf() for k in ue}
    cg = c.sb([128, T], F32)
    cv = c.sb([128, T], F32)
    b_cg, b_cv = Buf(), Buf()
    stg = [c.sb([128, T], BF16) for _ in range(2)]
    b_stg = [Buf(), Buf()]
    st = St()
    NB = 43
    for s in range(ntok // T):
        t0 = s * T
        nst = ntok // T
        lc = ntok if s == 0 else t0 - 1
        rc = ntok + 1 if s == nst - 1 else t0 + T
        p.dma("sp", hT[:, :, 1:T + 1], hT_d[:, :, t0:t0 + T].rearrange("c p t -> p c t"), reads=[bh], writes=[b_hT])
        p.dma("sp", hT[:, :, 0:1], hT_d[:, :, lc:lc + 1].rearrange("c p t -> p c t"), reads=[bh], wpart=[b_hT],
              allow_slow_non_contiguous=True)
        p.dma("sp", hT[:, :, T + 1:T + 2], hT_d[:, :, rc:rc + 1].rearrange("c p t -> p c t"), reads=[bh], wpart=[b_hT],
              allow_slow_non_contiguous=True)

        def epi(tag, ch, ti, bank, b_bank, wd):
            kind, bj = tag
            u = ue[(kind, ch)]
            st.k += 1
            evac(p, st.k, u[:, ti * G:(ti + 1) * G], bank[:, :G], [b_bank], wp=[b_ue[(kind, ch)]])
            if kind == "v" and ti == 2:
                j = bj * 2 + ch
                for (kd, ci, acc, b_acc, eng) in (("g", j, cg, b_cg, "dve"), ("v", 86 + j, cv, b_cv, "dve")):
                    uu = ue[(kd, ch)]
                    bu = b_ue[(kd, ch)]
                    p.op(eng, lambda e, uu=uu, ci=ci, acc=acc: e.tensor_scalar(
                        out=acc[:], in0=uu[:, 0:T], scalar1=cw[:, ci, 0:1], scalar2=cw[:, ci, 3:4],
                        op0=ALU.mult, op1=ALU.add), reads=[bu, b_cw], writes=[b_acc])
                    p.op(eng, lambda e, uu=uu, ci=ci, acc=acc: e.scalar_tensor_tensor(
                        out=acc[:], in0=uu[:, 1:T + 1], scalar=cw[:, ci, 1:2], in1=acc[:], op0=ALU.mult, op1=ALU.add),
                        reads=[bu, b_cw, b_acc], writes=[b_acc])
                    p.op(eng, lambda e, uu=uu, ci=ci, acc=acc: e.scalar_tensor_tensor(
                        out=acc[:], in0=uu[:, 2:T + 2], scalar=cw[:, ci, 2:3], in1=acc[:], op0=ALU.mult, op1=ALU.add),
                        reads=[bu, b_cw, b_acc], writes=[b_acc])
                p.op("act", lambda e: e.activation(out=cg[:], in_=cg[:], func=AF.Gelu_apprx_tanh),
                     reads=[b_cg], writes=[b_cg])
                si = j % 2
                p.op("pool", lambda e: e.tensor_tensor(out=stg[si][:], in0=cg[:], in1=cv[:], op=ALU.mult),
                     reads=[b_cg, b_cv], writes=[b_stg[si]])
                p.dma("sp", pT_d[j, :, t0:t0 + T], stg[si][:], reads=[b_stg[si]], wpart=[bpT])

        blocks = []
        for bj in range(NB):
            blocks.append((wup_d, bwu, 256 * bj, 256, ("g", bj)))
            blocks.append((wup_d, bwu, 11008 + 256 * bj, 256, ("v", bj)))
        fm_stream(c, blocks, 32, lambda kc, tc0, tn: hT[:, kc, tc0:tc0 + tn], b_hT,
                  [(0, G), (G, G), (2 * G, G)], epi, wbufs, b_wbufs, banks, b_banks, st)


BF = ml_dtypes.bfloat16
_PROGS = {}


def _rep(v, n=128):
    return np.ascontiguousarray(np.broadcast_to(v, (n,) + tuple(v.shape)))


def prog_l1():
    c = Ctx()
    x_d, bx = c.din("x", [TOK, D], F32)
    gb_d, bg = c.din("gb", [128, D], F32)
    win_d, bw = c.din("win", [D, 6464], F32)
    dft_d, bd = c.din("dft", [2, 256, 256], F32)
    pos_d, bp = c.din("pos", [64, TOK], I32)
    invf_d, bi = c.din("invf", [64, 1], F32)
    wuq_d, bwuq = c.din("wuq", [768, 1536], F32)
    gq_d, bgq = c.din("gq", [128, 6], F32)
    wukv_d, bwukv = c.din("wukv", [512, 2048], F32)
    gkv_d, bgkv = c.din("gkv", [128, 4], F32)
    gsg_d, bgsg = c.din("gsg", [128, 2048], F32)
    wsT_d, bws = c.din("wsT", [16, 128, 128], F32)
    bsb_d, bbs = c.din("bsb", [128, 16, 128], F32)
    hT_d, bh = c.dint("hT", [32, 128, TOK], BF16)
    cs_d, bcs = c.dint("cs", [2, 64, TOK], F32)
    ab_d, bab = c.dout("ab", [2, 1024, TOK], BF16)
    qT_d, bq = c.dout("qT", [8, 192, TOK], BF16)
    kT_d, bk_ = c.dout("kT", [8, 128, TOK], BF16)
    krT_d, bkr = c.dout("krT", [64, TOK], BF16)
    v_d, bv = c.dout("v", [TOK, 1024], BF16)
    sgT_d, bsg = c.dout("sgT", [2048, TOK], BF16)
    with c.phase():
        ph_head(c, x_d, bx, gb_d, bg, hT_d, bh, TOK)
    with c.phase():
        ph_rope(c, pos_d, bp, invf_d, bi, cs_d, bcs, TOK)
    with c.phase():
        ph_f(c, hT_d, bh, win_d, bw, dft_d, bd, ab_d, bab, TOK)
    with c.phase():
        ph_q(c, hT_d, bh, win_d, bw, wuq_d, bwuq, gq_d, bgq, cs_d, bcs, qT_d, bq, TOK)
    with c.phase():
        ph_kv(c, hT_d, bh, win_d, bw, wukv_d, bwukv, gkv_d, bgkv, cs_d, bcs, kT_d, bk_, krT_d, bkr, v_d, bv, TOK)
    with c.phase():
        ph_sg(c, hT_d, bh, win_d, bw, gsg_d, bgsg, wsT_d, bws, bsb_d, bbs, sgT_d, bsg, TOK)
    c.finish()
    return c.nc


def prog_l2():
    c = Ctx()
    m_d, bm = c.din("m", [128, 2, 128, 128], BF16)
    cs_d, bcs = c.din("cs128", [2, 128, 128], F32)
    tw_d, btw = c.din("tw", [2, 128, 128], F32)
    qT_d, bq = c.din("qT", [8, 192, TOK], BF16)
    kT_d, bk_ = c.din("kT", [8, 128, SEQ], BF16)
    krT_d, bkr = c.din("krT", [64, SEQ], BF16)
    v_d, bv = c.din("v", [SEQ, 1024], BF16)
    fo_d, bfo = c.dout("fo", [128, 128, 128], BF16)
    aT_d, ba = c.dout("aT", [1024, TOK], BF16)
    with c.phase():
        ph_fft(c, m_d, bm, cs_d, bcs, tw_d, btw, fo_d, bfo)
    with c.phase():
        ph_attn(c, qT_d, bq, kT_d, bk_, krT_d, bkr, v_d, bv, aT_d, ba, TOK, SEQ)
    c.finish()
    return c.nc


def prog_l3():
    c = Ctx()
    x_d, bx = c.din("x", [TOK, D], F32)
    gb_d, bg = c.din("gb", [128, D], F32)
    fT_d, bf = c.din("fT", [1024, TOK], BF16)
    aT_d, ba = c.din("aT", [1024, TOK], BF16)
    sT_d, bs_ = c.din("sgT", [2048, TOK], BF16)
    wg_d, bwg = c.din("wg", [D, 3 * D], F32)
    bgT_d, bbg = c.din("bgT", [128, 96], F32)
    wf_d, bwf = c.din("wf", [1024, D], F32)
    wa_d, bwa = c.din("wa", [1024, D], F32)
    ws_d, bws = c.din("ws", [2048, D], F32)
    wo_d, bwo = c.din("wo", [D, D], F32)
    gp1_d, bgp1 = c.din("gp1", [128, D], F32)
    gb2_d, bg2 = c.din("gb2", [128, D], F32)
    mem_d, bmem = c.din("mem", [256, D], F32)
    gkv_d, bgkv = c.din("gmkv", [128, D], F32)
    wmq_d, bwq = c.din("wmq", [D, 1024], F32)
    wmk_d, bwk = c.din("wmk", [D, 1024], F32)
    wmv_d, bwv = c.din("wmv", [D, 1024], F32)
    wmo_d, bwmo = c.din("wmo", [1024, D], F32)
    gp2_d, bgp2 = c.din("gp2", [128, D], F32)
    hT_d, bh = c.dint("hT", [32, 128, TOK], BF16)
    mT_d, bm = c.dint("mgT", [32, 128, TOK], BF16)
    y_d, by = c.dint("y", [TOK, D], F32)
    ssq_d, bss = c.dint("ssq", [TOK, 8], F32)
    x1_d, bx1 = c.dint("x1", [TOK, D], F32)
    memT_d, bmemT = c.dint("memT", [32, 128, 256], BF16)
    kmT_d, bkm = c.dint("kmT", [1024, 256], BF16)
    vm_d, bvm = c.dint("vm", [256, 1024], BF16)
    omT_d, bom = c.dint("omT", [8, 128, TOK], BF16)
    x2_d, bx2 = c.dout("x2", [TOK, D], F32)
    with c.phase():
        ph_head(c, x_d, bx, gb_d, bg, hT_d, bh, TOK)
    with c.phase():
        ph_merge(c, hT_d, bh, fT_d, bf, aT_d, ba, sT_d, bs_, wg_d, bwg, bgT_d, bbg, wf_d, bwf, wa_d, bwa, ws_d, bws,
                 mT_d.rearrange("c p t -> (c p) t"), bm, TOK)
    with c.phase():
        ph_lin(c, mT_d, bm, 32, wo_d, bwo, y_d, by, ssq_d, bss, TOK)
    with c.phase():
        ph_nr(c, y_d, by, ssq_d, bss, gp1_d, bgp1, x_d, bx, x1_d, bx1, TOK)
    with c.phase():
        ph_head(c, x1_d, bx1, gb2_d, bg2, hT_d, bh, TOK)
    with c.phase():
        ph_head(c, mem_d, bmem, gkv_d, bgkv, memT_d, bmemT, 256)
    with c.phase():
        ph_memkv(c, memT_d, bmemT, wmk_d, bwk, wmv_d, bwv, kmT_d, bkm, vm_d, bvm)
    with c.phase():
        ph_mq(c, hT_d, bh, wmq_d, bwq, kmT_d, bkm, vm_d, bvm, omT_d.rearrange("c p t -> (c p) t"), bom, TOK)
    with c.phase():
        ph_lin(c, omT_d, bom, 8, wmo_d, bwmo, y_d, by, ssq_d, bss, TOK)
    with c.phase():
        ph_nr(c, y_d, by, ssq_d, bss, gp2_d, bgp2, x1_d, bx1, x2_d, bx2, TOK)
    c.finish()
    return c.nc


def prog_l4():
    c = Ctx()
    x_d, bx = c.din("x", [TOK, D], F32)
    xh_d, bxh = c.din("xh", [2, D], F32)
    gb_d, bg = c.din("gb", [128, D], F32)
    wup_d, bwu = c.din("wup", [D, 22016], F32)
    cw_d, bcw = c.din("cw", [128, 172, 4], F32)
    wdn_d, bwd = c.din("wdn", [11008, D], F32)
    gp_d, bgp = c.din("gp", [128, D], F32)
    hT_d, bh = c.dint("hT", [32, 128, TOK + 2], BF16)
    pT_d, bpT = c.dint("pT", [86, 128, TOK], BF16)
    y_d, by = c.dint("y", [TOK, D], F32)
    ssq_d, bss = c.dint("ssq", [TOK, 8], F32)
    xo_d, bxo = c.dout("x3", [TOK, D], F32)
    with c.phase():
        ph_head(c, x_d, bx, gb_d, bg, hT_d, bh, TOK, halo=(xh_d, bxh))
    with c.phase():
        ph_up(c, hT_d, bh, wup_d, bwu, cw_d, bcw, pT_d, bpT, TOK)
    with c.phase():
        ph_lin(c, pT_d, bpT, 86, wdn_d, bwd, y_d, by, ssq_d, bss, TOK)
    with c.phase():
        ph_nr(c, y_d, by, ssq_d, bss, gp_d, bgp, x_d, bx, xo_d, bxo, TOK)
    c.finish()
    return c.nc


def _prog(name, fn):
    if name not in _PROGS:
        _PROGS[name] = fn()
    return _PROGS[name]


def _run(nc, maps):
    res = run_bass_kernel_spmd(nc, maps, core_ids=list(range(NCORES)))
    return res.results


def kernel(x, mem, positions, mix_pre_norm, mix_post_norm, w_in, mla_q_norm, w_uq, mla_kv_norm, w_ukv, sg_norm,
           w_spatial, b_spatial, w_br_f, w_br_a, w_br_s, w_gate, b_gate, w_out, mem_pre_norm, mem_post_norm,
           mem_kv_norm, w_mq, w_mk, w_mv, w_mo, ffn_pre_norm, ffn_post_norm, w_up, conv_w, conv_b, w_down):
    f32 = np.float32
    xs = np.asarray(x, f32)[0]
    memv = np.ascontiguousarray(np.asarray(mem, f32)[0])
    pos = np.asarray(positions)[0].astype(np.int32)
    jj = np.arange(256)
    a256 = 2 * np.pi * np.outer(jj, jj) / 256
    dft = np.stack([np.cos(a256), -np.sin(a256)]).astype(f32)
    kk = np.arange(128)
    a128 = 2 * np.pi * np.outer(kk, kk) / 128
    cs128 = np.stack([np.cos(a128), np.sin(a128)]).astype(f32)
    atw = 2 * np.pi * np.outer(kk, kk) / SEQ
    tw = np.stack([np.cos(atw), np.sin(atw)]).astype(f32)
    invf = (10000.0 ** (-np.arange(0, 64, 2, dtype=f32) / 64)).astype(f32)
    invf2 = np.concatenate([invf, invf]).reshape(64, 1)
    A = lambda v: np.ascontiguousarray(np.asarray(v, f32))
    for l in range(2):
        com = {"gb": _rep(A(mix_pre_norm[l])), "win": A(w_in[l]), "dft": dft, "invf": invf2, "wuq": A(w_uq[l]),
               "gq": np.ascontiguousarray(A(mla_q_norm[l]).reshape(6, 128).T), "wukv": A(w_ukv[l]),
               "gkv": np.ascontiguousarray(A(mla_kv_norm[l]).reshape(4, 128).T), "gsg": _rep(A(sg_norm[l])),
               "wsT": np.ascontiguousarray(A(w_spatial[l]).transpose(0, 2, 1)), "bsb": _rep(A(b_spatial[l]))}
        maps = []
        for c in range(NCORES):
            sl = slice(c * TOK, (c + 1) * TOK)
            m = dict(com)
            m["x"] = np.ascontiguousarray(xs[sl])
            m["pos"] = _rep(pos[sl], 64)
            maps.append(m)
        r1 = _run(_prog("l1", prog_l1), maps)
        ab = np.stack([r1[c]["ab"] for c in range(NCORES)])
        abr = ab.reshape(NCORES, 2, 8, 128, 16, 128)
        kT_all = np.ascontiguousarray(np.concatenate([r1[c]["kT"] for c in range(NCORES)], axis=2))
        krT_all = np.ascontiguousarray(np.concatenate([r1[c]["krT"] for c in range(NCORES)], axis=1))
        v_all = np.ascontiguousarray(np.concatenate([r1[c]["v"] for c in range(NCORES)], axis=0))
        maps = []
        for j in range(NCORES):
            mm = abr[:, :, j].transpose(0, 3, 1, 2, 4).reshape(128, 2, 128, 128)
            maps.append({"m": np.ascontiguousarray(mm), "cs128": cs128, "tw": tw, "qT": r1[j]["qT"], "kT": kT_all,
                         "krT": krT_all, "v": v_all})
        r2 = _run(_prog("l2", prog_l2), maps)
        fo = np.stack([r2[j]["fo"] for j in range(NCORES)])
        fall = fo.transpose(1, 3, 0, 2).reshape(SEQ, 1024)
        com = {"gb": _rep(A(mix_pre_norm[l])), "wg": A(w_gate[l]),
               "bgT": np.ascontiguousarray(A(b_gate[l]).reshape(96, 128).T), "wf": A(w_br_f[l]), "wa": A(w_br_a[l]),
               "ws": A(w_br_s[l]), "wo": A(w_out[l]), "gp1": _rep(A(mix_post_norm[l])),
               "gb2": _rep(A(mem_pre_norm[l])), "mem": memv, "gmkv": _rep(A(mem_kv_norm[l])), "wmq": A(w_mq[l]),
               "wmk": A(w_mk[l]), "wmv": A(w_mv[l]), "wmo": A(w_mo[l]), "gp2": _rep(A(mem_post_norm[l]))}
        maps = []
        for c in range(NCORES):
            sl = slice(c * TOK, (c + 1) * TOK)
            m = dict(com)
            m["x"] = np.ascontiguousarray(xs[sl])
            m["fT"] = np.ascontiguousarray(fall[sl].T)
            m["aT"] = r2[c]["aT"]
            m["sgT"] = r1[c]["sgT"]
            maps.append(m)
        r3 = _run(_prog("l3", prog_l3), maps)
        x2 = np.concatenate([r3[c]["x2"] for c in range(NCORES)], axis=0)
        cwp = np.concatenate([A(conv_w[l]), A(conv_b[l])[None]], axis=0)
        cwp = np.ascontiguousarray(cwp.reshape(4, 172, 128).transpose(2, 1, 0))
        com = {"gb": _rep(A(ffn_pre_norm[l])), "wup": A(w_up[l]), "cw": cwp, "wdn": A(w_down[l]),
               "gp": _rep(A(ffn_post_norm[l]))}
        zero = np.zeros(D, f32)
        maps = []
        for c in range(NCORES):
            sl = slice(c * TOK, (c + 1) * TOK)
            m = dict(com)
            m["x"] = np.ascontiguousarray(x2[sl])
            prev = x2[c * TOK - 1] if c > 0 else zero
            nxt = x2[(c + 1) * TOK] if c < NCORES - 1 else zero
            m["xh"] = np.ascontiguousarray(np.stack([prev, nxt]))
            maps.append(m)
        r4 = _run(_prog("l4", prog_l4), maps)
        xs = np.concatenate([r4[c]["x3"] for c in range(NCORES)], axis=0)
    return np.ascontiguousarray(xs[None].astype(f32))
```

```python
import math
from contextlib import ExitStack
import numpy as np
import ml_dtypes
import concourse.bass as bass
import concourse.mybir as mybir
from concourse.bass_utils import run_bass_kernel_spmd

F32 = mybir.dt.float32
BF16 = mybir.dt.bfloat16
I32 = mybir.dt.int32
AF = mybir.ActivationFunctionType
ALU = mybir.AluOpType
AX = mybir.AxisListType

NCORES = 8
SEQ = 16384
TOK = SEQ // NCORES
D = 4096
EPS = 1e-6
NSLOT = 6


class Buf:
    __slots__ = ("wr", "rd")

    def __init__(self):
        self.wr = {}
        self.rd = {}


class Op:
    __slots__ = ("eng", "fn", "reads", "writes", "wpart", "dma", "deps", "marked", "semval", "slot", "key", "xdeps")

    def __init__(self, eng, fn, reads, writes, wpart, dma):
        self.eng = eng
        self.fn = fn
        self.reads = reads
        self.writes = writes
        self.wpart = wpart
        self.dma = dma
        self.deps = ()
        self.marked = False
        self.semval = None
        self.slot = None
        self.key = None
        self.xdeps = ()


class Prog:
    def __init__(self, nc):
        self.nc = nc
        self.ops = []
        self.dma_count = {}

    def op(self, eng, fn, reads=(), writes=(), wpart=(), dma=False):
        o = Op(eng, fn, tuple(reads), tuple(writes), tuple(wpart), dma)
        if dma:
            k = self.dma_count.get(eng, 0)
            self.dma_count[eng] = k + 1
            o.slot = k % NSLOT
            o.semval = 16 * (k // NSLOT + 1)
            o.key = (eng, o.slot)
            o.marked = True
        else:
            o.key = eng
        self.ops.append(o)

    def dma(self, q, out, in_, reads=(), writes=(), wpart=(), **kw):
        self.op(q, lambda e: e.dma_start(out=out, in_=in_, **kw), reads, writes, wpart, dma=True)

    def barrier(self):
        last = {}
        for i, o in enumerate(self.ops):
            last[o.key] = i
        engs = sorted(set(o.eng for o in self.ops))
        tgt = tuple(last.values())
        for e in engs:
            self.op(e, lambda eo: eo.nop())
            self.ops[-1].xdeps = tgt

    def emit(self):
        nc = self.nc
        ops = self.ops
        last_on_slot = {}
        for i, o in enumerate(ops):
            deps = set(o.xdeps)
            for b in o.reads:
                deps.update(b.wr.values())
            for b in o.writes:
                deps.update(b.wr.values())
                deps.update(b.rd.values())
            for b in o.wpart:
                deps.update(b.rd.values())
            for b in o.writes:
                b.wr = {o.key: i}
                b.rd = {}
            for b in o.wpart:
                if b.rd:
                    b.wr = {o.key: i}
                    b.rd = {}
                else:
                    b.wr[o.key] = i
            for b in o.reads:
                if b.wr.get(o.key) != i:
                    b.rd[o.key] = i
            if o.dma:
                prev = last_on_slot.get(o.key)
                if prev is not None:
                    deps.add(prev)
                last_on_slot[o.key] = i
            deps.discard(i)
            fd = []
            for d in deps:
                od = ops[d]
                if od.eng == "pe" and o.eng == "pe" and not od.dma and not o.dma:
                    continue
                fd.append(d)
            o.deps = fd
            for d in fd:
                ops[d].marked = True
        cnt = {}
        for o in ops:
            if o.dma:
                continue
            if o.marked:
                cnt[o.eng] = cnt.get(o.eng, 0) + 1
                o.semval = cnt[o.eng]
        engs = sorted(set(o.eng for o in ops))
        ctxs = []
        sems = {}
        for e in engs:
            if cnt.get(e, 0) > 0:
                cm = nc.semaphore("s_" + e)
                sems[e] = cm.__enter__()
                ctxs.append(cm)
            for s in range(min(NSLOT, self.dma_count.get(e, 0))):
                cm = nc.semaphore("d_%s_%d" % (e, s))
                sems[(e, s)] = cm.__enter__()
                ctxs.append(cm)
        per_eng = {e: [] for e in engs}
        for i, o in enumerate(ops):
            per_eng[o.eng].append(i)

        def run_engine(ename, eobj):
            seen = {}
            for i in per_eng.get(ename, ()):
                o = ops[i]
                need = {}
                for d in o.deps:
                    od = ops[d]
                    v = od.semval
                    if need.get(od.key, 0) < v:
                        need[od.key] = v
                for key, v in need.items():
                    if seen.get(key, 0) >= v:
                        continue
                    eobj.wait_ge(sems[key], v)
                    seen[key] = v
                ins = o.fn(eobj)
                if o.marked:
                    ins.then_inc(sems[o.key], 16 if o.dma else 1)

        with nc.Block() as block:
            if "sp" in per_eng:
                @block.sync
                def _(e):
                    run_engine("sp", e)
            if "act" in per_eng:
                @block.scalar
                def _(e):
                    run_engine("act", e)
            if "dve" in per_eng:
                @block.vector
                def _(e):
                    run_engine("dve", e)
            if "pool" in per_eng:
                @block.gpsimd
                def _(e):
                    run_engine("pool", e)
            if "pe" in per_eng:
                @block.tensor
                def _(e):
                    run_engine("pe", e)
        for cm in reversed(ctxs):
            cm.__exit__(None, None, None)
        return len(ops)


class Ctx:
    def __init__(self):
        self.nc = bass.Bass("TRN2", target_bir_lowering=False)
        self.p = Prog(self.nc)
        self.es = None
        self.n = 0
        self.outs = []
        self.rr = 0

    def din(self, name, shape, dt):
        return self.nc.dram_tensor(name, list(shape), dt, kind="ExternalInput").ap(), Buf()

    def dout(self, name, shape, dt):
        b = Buf()
        self.outs.append(b)
        return self.nc.dram_tensor(name, list(shape), dt, kind="ExternalOutput").ap(), b

    def dint(self, name, shape, dt):
        return self.nc.dram_tensor(name, list(shape), dt, kind="Internal").ap(), Buf()

    def sb(self, shape, dt):
        self.n += 1
        return self.es.enter_context(self.nc.sbuf_tensor("sb%d" % self.n, list(shape), dt))

    def ps(self, shape, dt):
        self.n += 1
        return self.es.enter_context(self.nc.psum_tensor("ps%d" % self.n, list(shape), dt))

    def phase(self):
        c = self

        class _Ph:
            def __enter__(self_):
                self_.es = ExitStack()
                self_.es.__enter__()
                c.es = self_.es
                return c

            def __exit__(self_, *a):
                c.p.barrier()
                c.es = None
                return self_.es.__exit__(*a)
        return _Ph()

    def finish(self):
        self.p.op("sp", lambda e: e.nop(), reads=list(self.outs))
        return self.p.emit()

    def ident(self):
        p = self.p
        idf = self.sb([128, 128], F32)
        idb = self.sb([128, 128], BF16)
        b = Buf()
        p.op("pool", lambda e: e.memset(idf[:], 0.0), writes=[b])
        p.op("pool", lambda e: e.affine_select(out=idf[:], in_=idf[:], pattern=[[-1, 128]],
                                               compare_op=ALU.not_equal, fill=1.0, base=0,
                                               channel_multiplier=1), reads=[b], writes=[b])
        p.op("dve", lambda e: e.tensor_copy(out=idb[:], in_=idf[:]), reads=[b], writes=[b])
        return idb, b


def rstd_ops(p, eng, out, in_, n, rb, wb):
    p.op(eng, lambda e: e.tensor_scalar(out=out, in0=in_, scalar1=1.0 / n, scalar2=EPS,
                                        op0=ALU.mult, op1=ALU.add), reads=rb, writes=wb)
    p.op("act", lambda e: e.activation(out=out, in_=out, func=AF.Sqrt), reads=wb, writes=wb)
    p.op(eng, lambda e: e.reciprocal(out=out, in_=out), reads=wb, writes=wb)


def ph_head(c, x_d, bx, gb_d, bg, hT_d, bh, ntok, col0=0, halo=None):
    p = c.p
    if True:
        idb, b_id = c.ident()
        gb = c.sb([128, D], F32)
        b_gb = Buf()
        p.dma("act", gb[:], gb_d, reads=[bg], writes=[b_gb])
        xt = [c.sb([128, D], F32) for _ in range(2)]
        b_xt = [Buf(), Buf()]
        xb = [c.sb([128, D], BF16) for _ in range(2)]
        b_xb = [Buf(), Buf()]
        ss = [c.sb([128, 1], F32) for _ in range(2)]
        b_ss = [Buf(), Buf()]
        hTt = [c.sb([128, 32, 512], BF16) for _ in range(2)]
        b_hTt = [Buf(), Buf()]
        pt = [c.ps([128, 8, 128], BF16) for _ in range(4)]
        b_pt = [Buf() for _ in range(4)]
        ntile = ntok // 128
        jobs = [(t * 128, 128, None) for t in range(ntile)]
        if halo is not None:
            jobs.append((0, 2, halo))
        ptc = 0
        for ji, (r0, nr, hal) in enumerate(jobs):
            s = ji % 2
            if hal is None:
                p.dma("sp", xt[s][:nr, :], x_d[r0:r0 + nr, :], reads=[bx], writes=[b_xt[s]])
            else:
                p.dma("sp", xt[s][:nr, :], hal[0][0:nr, :], reads=[hal[1]], writes=[b_xt[s]])
            p.op("act", lambda e, s=s, nr=nr: e.activation(out=xb[s][:nr, :], in_=xt[s][:nr, :], func=AF.Square,
                                                           accum_out=ss[s][:nr, :]),
                 reads=[b_xt[s]], writes=[b_xb[s], b_ss[s]])
            rstd_ops(p, "dve", ss[s][:nr, :], ss[s][:nr, :], D, [b_ss[s]], [b_ss[s]])
            p.op("dve", lambda e, s=s, nr=nr: e.scalar_tensor_tensor(out=xb[s][:nr, :], in0=xt[s][:nr, :],
                                                                     scalar=ss[s][:nr, 0:1], in1=gb[:nr, :],
                                                                     op0=ALU.mult, op1=ALU.mult),
                 reads=[b_xt[s], b_ss[s], b_gb], writes=[b_xb[s]])
            hs = (ji // 4) % 2
            tcol = (ji % 4) * 128
            for q in range(4):
                pi = ptc % 4
                ptc += 1
                for cc in range(8):
                    ch = q * 8 + cc
                    p.op("pe", lambda e, s=s, nr=nr, ch=ch, cc=cc, pi=pi: e.transpose(
                        out=pt[pi][:, cc, :nr], in_=xb[s][:nr, ch * 128:(ch + 1) * 128], identity=idb[:nr, :nr]),
                        reads=[b_xb[s], b_id], writes=[b_pt[pi]])
                eng = "act" if q % 2 == 0 else "dve"
                if eng == "act":
                    p.op("act", lambda e, hs=hs, q=q, pi=pi, nr=nr, tcol=tcol: e.activation(
                        out=hTt[hs][:, q * 8:(q + 1) * 8, tcol:tcol + nr], in_=pt[pi][:, :, :nr], func=AF.Copy),
                        reads=[b_pt[pi]], wpart=[b_hTt[hs]])
                else:
                    p.op("dve", lambda e, hs=hs, q=q, pi=pi, nr=nr, tcol=tcol: e.tensor_copy(
                        out=hTt[hs][:, q * 8:(q + 1) * 8, tcol:tcol + nr], in_=pt[pi][:, :, :nr]),
                        reads=[b_pt[pi]], wpart=[b_hTt[hs]])
            if hal is not None:
                p.dma("sp", hT_d[:, :, ntok:ntok + 2].rearrange("c p t -> p c t"), hTt[hs][:, :, tcol:tcol + 2],
                      reads=[b_hTt[hs]], wpart=[bh])
            elif ji % 4 == 3 or ji == ntile - 1:
                g0 = col0 + (ji // 4) * 512
                wdt = (ji % 4 + 1) * 128
                p.dma("sp", hT_d[:, :, g0:g0 + wdt].rearrange("c p t -> p c t"), hTt[hs][:, :, :wdt],
                      reads=[b_hTt[hs]], wpart=[bh])


def ph_lin(c, aT_d, ba, KC, W_d, bw, y_d, by, ssq_d, bs, ntok, acol0=0):
    p = c.p
    KS = 8
    nks = (KC + KS - 1) // KS
    T = 1024 if (KC <= 32 and ntok % 1024 == 0) else 512
    NTL = T // 128
    if True:
        aT = c.sb([128, KC, T], BF16)
        b_aT = Buf()
        wb = [c.sb([128, KS, 512], BF16) for _ in range(3)]
        b_wb = [Buf() for _ in range(3)]
        acc = [c.ps([128, 512], F32) for _ in range(8)]
        b_acc = [Buf() for _ in range(8)]
        ysb = [c.sb([128, 512], F32) for _ in range(4)]
        b_ysb = [Buf() for _ in range(4)]
        junk = c.sb([128, 512], BF16)
        b_junk = Buf()
        ssq = [c.sb([128, 8], F32) for _ in range(NTL)]
        b_ssq = [Buf() for _ in range(NTL)]
        wi = 0
        yi = 0
        for st in range(ntok // T):
            t0 = st * T
            p.dma("sp", aT[:], aT_d[:, :, acol0 + t0:acol0 + t0 + T].rearrange("c p t -> p c t"),
                  reads=[ba], writes=[b_aT])
            for mb in range(8):
                par = (mb % 2) * 4 if NTL == 4 else 0
                for ks in range(nks):
                    k0 = ks * KS
                    kn = min(KS, KC - k0)
                    w = wi % 3
                    wi += 1
                    p.dma("pool", wb[w][:, :kn, :],
                          W_d[k0 * 128:(k0 + kn) * 128, mb * 512:(mb + 1) * 512].rearrange("(c p) n -> p c n", p=128),
                          reads=[bw], writes=[b_wb[w]])
                    for tl in range(NTL):
                        for cc in range(kn):
                            kc = k0 + cc
                            p.op("pe", lambda e, tl=tl, cc=cc, kc=kc, w=w, par=par: e.matmul(
                                acc[par + tl][:], lhsT=aT[:, kc, tl * 128:(tl + 1) * 128], rhs=wb[w][:, cc, :],
                                start=(kc == 0), stop=(kc == KC - 1)),
                                reads=[b_aT, b_wb[w]], writes=[b_acc[par + tl]])
                for tl in range(NTL):
                    y = yi % 4
                    yi += 1
                    p.op("act", lambda e, y=y, a=par + tl: e.activation(out=ysb[y][:], in_=acc[a][:], func=AF.Copy),
                         reads=[b_acc[par + tl]], writes=[b_ysb[y]])
                    p.op("act", lambda e, y=y, tl=tl, mb=mb: e.activation(out=junk[:], in_=ysb[y][:], func=AF.Square,
                                                                          accum_out=ssq[tl][:, mb:mb + 1]),
                         reads=[b_ysb[y]], writes=[b_junk], wpart=[b_ssq[tl]])
                    r0 = t0 + tl * 128
                    p.dma("sp", y_d[r0:r0 + 128, mb * 512:(mb + 1) * 512], ysb[y][:], reads=[b_ysb[y]], wpart=[by])
            for tl in range(NTL):
                r0 = t0 + tl * 128
                p.dma("sp", ssq_d[r0:r0 + 128, :], ssq[tl][:], reads=[b_ssq[tl]], wpart=[bs])


def ph_nr(c, y_d, by, ssq_d, bs, gb_d, bg, xi_d, bxi, xo_d, bxo, ntok):
    p = c.p
    if True:
        gb = c.sb([128, D], F32)
        b_gb = Buf()
        p.dma("act", gb[:], gb_d, reads=[bg], writes=[b_gb])
        yt = [c.sb([128, D], F32) for _ in range(2)]
        xt = [c.sb([128, D], F32) for _ in range(2)]
        sq = [c.sb([128, 8], F32) for _ in range(2)]
        rs = [c.sb([128, 1], F32) for _ in range(2)]
        b_yt = [Buf(), Buf()]
        b_xt = [Buf(), Buf()]
        b_sq = [Buf(), Buf()]
        b_rs = [Buf(), Buf()]
        for t in range(ntok // 128):
            s = t % 2
            r0 = t * 128
            p.dma("sp", yt[s][:], y_d[r0:r0 + 128, :], reads=[by], writes=[b_yt[s]])
            p.dma("act", xt[s][:], xi_d[r0:r0 + 128, :], reads=[bxi], writes=[b_xt[s]])
            p.dma("sp", sq[s][:], ssq_d[r0:r0 + 128, :], reads=[bs], writes=[b_sq[s]])
            p.op("dve", lambda e, s=s: e.reduce_sum(out=rs[s][:], in_=sq[s][:], axis=AX.X),
                 reads=[b_sq[s]], writes=[b_rs[s]])
            rstd_ops(p, "dve", rs[s][:], rs[s][:], D, [b_rs[s]], [b_rs[s]])
            p.op("dve", lambda e, s=s: e.scalar_tensor_tensor(out=yt[s][:], in0=yt[s][:], scalar=rs[s][:, 0:1],
                                                              in1=gb[:], op0=ALU.mult, op1=ALU.mult),
                 reads=[b_yt[s], b_rs[s], b_gb], writes=[b_yt[s]])
            p.op("pool", lambda e, s=s: e.tensor_tensor(out=yt[s][:], in0=yt[s][:], in1=xt[s][:], op=ALU.add),
                 reads=[b_yt[s], b_xt[s]], writes=[b_yt[s]])
            p.dma("sp", xo_d[r0:r0 + 128, :], yt[s][:], reads=[b_yt[s]], wpart=[bxo])


class St:
    def __init__(self):
        self.wi = 0
        self.bi = 0
        self.k = 0


def evac(p, k, out, in_, rb, wb=(), wp=()):
    if k % 2 == 0:
        p.op("act", lambda e: e.activation(out=out, in_=in_, func=AF.Copy), reads=rb, writes=wb, wpart=wp)
    else:
        p.op("dve", lambda e: e.tensor_copy(out=out, in_=in_), reads=rb, writes=wb, wpart=wp)


def load_hT(c, hT, b_hT, hT_d, bh, t0, n, KC=32, q="sp"):
    c.p.dma(q, hT[:, :KC, :n], hT_d[:KC, :, t0:t0 + n].rearrange("c p t -> p c t"), reads=[bh], writes=[b_hT])


def fm_stream(c, blocks, KC, rhs_fn, b_rhs, tgs, epi, wbufs, b_wbufs, banks, b_banks, st):
    p = c.p
    for (W_d, bw, col0, ncols, tag) in blocks:
        w = st.wi % len(wbufs)
        st.wi += 1
        p.dma("pool", wbufs[w][:, :KC, :ncols],
              W_d[0:KC * 128, col0:col0 + ncols].rearrange("(c p) n -> p c n", p=128),
              reads=[bw], writes=[b_wbufs[w]])
        for ch in range((ncols + 127) // 128):
            wd = min(128, ncols - ch * 128)
            for ti, (tc0, tn) in enumerate(tgs):
                bk = st.bi % len(banks)
                st.bi += 1
                for kc in range(KC):
                    p.op("pe", lambda e, bk=bk, w=w, kc=kc, ch=ch, wd=wd, tc0=tc0, tn=tn: e.matmul(
                        banks[bk][:wd, :tn], lhsT=wbufs[w][:, kc, ch * 128:ch * 128 + wd], rhs=rhs_fn(kc, tc0, tn),
                        start=(kc == 0), stop=(kc == KC - 1)),
                        reads=[b_wbufs[w], b_rhs], writes=[b_banks[bk]])
                epi(tag, ch, ti, banks[bk], b_banks[bk], wd)


def ones_bf(c, shape):
    t = c.sb(shape, BF16)
    b = Buf()
    c.p.op("pool", lambda e: e.memset(t[:], 1.0), writes=[b])
    return t, b


def fm_rmsnorm(c, raw, b_raw, nch, nfeat, tn, gcol, b_g, ssb, b_ssb, rs, b_rs, outT, b_out):
    p = c.p
    p.op("dve", lambda e: e.tensor_scalar(out=rs[:, :tn], in0=ssb[:, :tn], scalar1=1.0 / nfeat, scalar2=EPS,
                                          op0=ALU.mult, op1=ALU.add), reads=[b_ssb], writes=[b_rs])
    p.op("act", lambda e: e.activation(out=rs[:, :tn], in_=rs[:, :tn], func=AF.Sqrt), reads=[b_rs], writes=[b_rs])
    p.op("dve", lambda e: e.reciprocal(out=rs[:, :tn], in_=rs[:, :tn]), reads=[b_rs], writes=[b_rs])
    for ch in range(nch):
        p.op("dve", lambda e, ch=ch: e.scalar_tensor_tensor(out=outT[:, ch, :tn], in0=raw[:, ch, :tn],
                                                            scalar=gcol[:, ch:ch + 1], in1=rs[:, :tn],
                                                            op0=ALU.mult, op1=ALU.mult),
             reads=[b_raw, b_g, b_rs], wpart=[b_out])


def ph_rope(c, pos_d, bp, invf_d, bi, cs_d, bcs, ntok):
    p = c.p
    pi_ = c.sb([64, ntok], I32)
    pf = c.sb([64, ntok], F32)
    t1 = c.sb([64, ntok], F32)
    t2 = c.sb([64, ntok], F32)
    ki = c.sb([64, ntok], I32)
    iv = c.sb([64, 1], F32)
    b_pi, b_pf, b_t1, b_t2, b_ki, b_iv = Buf(), Buf(), Buf(), Buf(), Buf(), Buf()
    p.dma("sp", pi_[:], pos_d, reads=[bp], writes=[b_pi])
    p.dma("sp", iv[:], invf_d, reads=[bi], writes=[b_iv])
    p.op("dve", lambda e: e.tensor_copy(out=pf[:], in_=pi_[:]), reads=[b_pi], writes=[b_pf])
    p.op("dve", lambda e: e.tensor_scalar(out=pf[:], in0=pf[:], scalar1=iv[:, 0:1], scalar2=1.0 / (2 * math.pi),
                                          op0=ALU.mult, op1=ALU.mult), reads=[b_pf, b_iv], writes=[b_pf])
    for k, sh in enumerate((0.25, 0.0)):
        p.op("dve", lambda e, sh=sh: e.tensor_scalar(out=t1[:], in0=pf[:], scalar1=sh, scalar2=None, op0=ALU.add),
             reads=[b_pf], writes=[b_t1])
        p.op("dve", lambda e: e.tensor_copy(out=ki[:], in_=t1[:]), reads=[b_t1], writes=[b_ki])
        p.op("dve", lambda e: e.tensor_copy(out=t2[:], in_=ki[:]), reads=[b_ki], writes=[b_t2])
        p.op("dve", lambda e: e.tensor_tensor(out=t1[:], in0=t1[:], in1=t2[:], op=ALU.subtract),
             reads=[b_t1, b_t2], writes=[b_t1])
        p.op("dve", lambda e: e.tensor_scalar(out=t2[:], in0=t1[:], scalar1=0.5, scalar2=None, op0=ALU.is_gt),
             reads=[b_t1], writes=[b_t2])
        p.op("dve", lambda e: e.tensor_tensor(out=t1[:], in0=t1[:], in1=t2[:], op=ALU.subtract),
             reads=[b_t1, b_t2], writes=[b_t1])
        p.op("dve", lambda e: e.tensor_scalar(out=t2[:], in0=t1[:], scalar1=-0.5, scalar2=None, op0=ALU.is_lt),
             reads=[b_t1], writes=[b_t2])
        p.op("dve", lambda e: e.tensor_tensor(out=t1[:], in0=t1[:], in1=t2[:], op=ALU.add),
             reads=[b_t1, b_t2], writes=[b_t1])
        p.op("act", lambda e: e.activation(out=t1[:], in_=t1[:], func=AF.Sin, scale=2 * math.pi),
             reads=[b_t1], writes=[b_t1])
        p.dma("sp", cs_d[k], t1[:], reads=[b_t1], wpart=[bcs])


def ph_f(c, hT_d, bh, win_d, bw, dft_d, bd, ab_d, bab, ntok):
    p = c.p
    T = 512
    hT = c.sb([128, 32, T], BF16)
    b_hT = Buf()
    wbufs = [c.sb([128, 32, 512], BF16) for _ in range(2)]
    b_wbufs = [Buf(), Buf()]
    banks = [c.ps([128, 512], F32) for _ in range(6)]
    b_banks = [Buf() for _ in range(6)]
    tab = c.sb([128, 2, 2, 256], BF16)
    b_tab = Buf()
    p.dma("pool", tab[:], dft_d.rearrange("a (c p) j -> p a c j", p=128), reads=[bd], writes=[b_tab])
    zf = c.sb([128, 8, T], BF16)
    b_zf = Buf()
    stg = [c.sb([128, T], BF16) for _ in range(3)]
    b_stg = [Buf() for _ in range(3)]
    st = St()
    for s in range(ntok // T):
        t0 = s * T
        load_hT(c, hT, b_hT, hT_d, bh, t0, T)

        def epi(tag, ch, ti, bank, b_bank, wd):
            fch = tag * 4 + ch
            st.k += 1
            evac(p, st.k, zf[:, fch, :], bank[:, :], [b_bank], wp=[b_zf])

        fm_stream(c, [(win_d, bw, 0, 512, 0), (win_d, bw, 512, 512, 1)], 32,
                  lambda kc, tc0, tn: hT[:, kc, tc0:tc0 + tn], b_hT, [(0, T)], epi, wbufs, b_wbufs, banks, b_banks, st)
        for g in range(4):
            for jc in range(2):
                for ab in range(2):
                    bk = st.bi % 6
                    st.bi += 1
                    for cc in range(2):
                        p.op("pe", lambda e, bk=bk, ab=ab, cc=cc, jc=jc, g=g: e.matmul(
                            banks[bk][:, :], lhsT=tab[:, ab, cc, jc * 128:(jc + 1) * 128], rhs=zf[:, 2 * g + cc, :],
                            start=(cc == 0), stop=(cc == 1)), reads=[b_tab, b_zf], writes=[b_banks[bk]])
                    si = st.k % 3
                    st.k += 1
                    evac(p, st.k, stg[si][:], banks[bk][:, :], [b_banks[bk]], wb=[b_stg[si]])
                    r0 = g * 256 + jc * 128
                    p.dma("sp", ab_d[ab, r0:r0 + 128, t0:t0 + T], stg[si][:], reads=[b_stg[si]], wpart=[bab])


def ph_fft(c, m_d, bm, cs_d, bcs, tw_d, btw, fo_d, bfo):
    p = c.p
    M = c.sb([128, 2, 128, 128], BF16)
    b_M = Buf()
    p.dma("sp", M[:], m_d, reads=[bm], writes=[b_M])
    csf = c.sb([128, 2, 128], F32)
    tw = c.sb([128, 2, 128], F32)
    b_csf, b_tw = Buf(), Buf()
    p.dma("act", csf[:], cs_d.rearrange("a p k -> p a k"), reads=[bcs], writes=[b_csf])
    p.dma("act", tw[:], tw_d.rearrange("a p k -> p a k"), reads=[btw], writes=[b_tw])
    r1 = c.sb([128, 256], BF16)
    r2 = c.sb([128, 256], BF16)
    b_r = Buf()
    p.op("dve", lambda e: e.tensor_copy(out=r1[:, 0:128], in_=csf[:, 0, :]), reads=[b_csf], wpart=[b_r])
    p.op("dve", lambda e: e.tensor_scalar(out=r1[:, 128:256], in0=csf[:, 1, :], scalar1=-1.0, scalar2=None,
                                          op0=ALU.mult), reads=[b_csf], wpart=[b_r])
    p.op("dve", lambda e: e.tensor_copy(out=r2[:, 0:128], in_=csf[:, 1, :]), reads=[b_csf], wpart=[b_r])
    p.op("dve", lambda e: e.tensor_copy(out=r2[:, 128:256], in_=csf[:, 0, :]), reads=[b_csf], wpart=[b_r])
    Yr = c.sb([128, 128, 128], BF16)
    Yi = c.sb([128, 128, 128], BF16)
    b_Y = Buf()
    banks = [c.ps([128, 2, 2, 128], F32) for _ in range(4)]
    b_banks = [Buf() for _ in range(4)]
    tmp = [c.sb([128, 2, 128], F32) for _ in range(4)]
    b_tmp = [Buf() for _ in range(4)]
    tcb = tw[:, 0, :].unsqueeze(1).to_broadcast([128, 2, 128])
    tsb = tw[:, 1, :].unsqueeze(1).to_broadcast([128, 2, 128])
    for cp in range(64):
        bk = cp % 4
        for j in range(2):
            ch = cp * 2 + j
            p.op("pe", lambda e, bk=bk, j=j, ch=ch: e.matmul(banks[bk][:, j, :, :], lhsT=M[:, 0, ch, :], rhs=r1[:],
                                                            start=True, stop=False),
                 reads=[b_M, b_r], writes=[b_banks[bk]])
            p.op("pe", lambda e, bk=bk, j=j, ch=ch: e.matmul(banks[bk][:, j, :, :], lhsT=M[:, 1, ch, :], rhs=r2[:],
                                                            start=False, stop=True),
                 reads=[b_M, b_r], writes=[b_banks[bk]])
        yr = banks[bk][:, :, 0, :]
        yi = banks[bk][:, :, 1, :]
        ch0 = cp * 2
        p.op("dve", lambda e, yr=yr: e.tensor_tensor(out=tmp[0][:], in0=yr, in1=tcb, op=ALU.mult),
             reads=[b_banks[bk], b_tw], writes=[b_tmp[0]])
        p.op("dve", lambda e, yi=yi: e.tensor_tensor(out=tmp[1][:], in0=yi, in1=tsb, op=ALU.mult),
             reads=[b_banks[bk], b_tw], writes=[b_tmp[1]])
        p.op("dve", lambda e, yi=yi: e.tensor_tensor(out=tmp[2][:], in0=yi, in1=tcb, op=ALU.mult),
             reads=[b_banks[bk], b_tw], writes=[b_tmp[2]])
        p.op("dve", lambda e, yr=yr: e.tensor_tensor(out=tmp[3][:], in0=yr, in1=tsb, op=ALU.mult),
             reads=[b_banks[bk], b_tw], writes=[b_tmp[3]])
        p.op("pool", lambda e, ch0=ch0: e.tensor_tensor(out=Yr[:, ch0:ch0 + 2, :], in0=tmp[0][:], in1=tmp[1][:],
                                                        op=ALU.add), reads=[b_tmp[0], b_tmp[1]], wpart=[b_Y])
        p.op("pool", lambda e, ch0=ch0: e.tensor_tensor(out=Yi[:, ch0:ch0 + 2, :], in0=tmp[2][:], in1=tmp[3][:],
                                                        op=ALU.subtract), reads=[b_tmp[2], b_tmp[3]], wpart=[b_Y])
    cc_ = c.sb([128, 128], BF16)
    ss_ = c.sb([128, 128], BF16)
    b_c2 = Buf()
    p.op("dve", lambda e: e.tensor_copy(out=cc_[:], in_=csf[:, 0, :]), reads=[b_csf], wpart=[b_c2])
    p.op("dve", lambda e: e.tensor_copy(out=ss_[:], in_=csf[:, 1, :]), reads=[b_csf], wpart=[b_c2])
    fo = c.sb([128, 128, 128], BF16)
    b_fo = Buf()
    for q in range(32):
        bk = q % 4
        bnk = banks[bk][:].rearrange("p a b k -> p (a b) k")
        p.op("pe", lambda e, bnk=bnk, q=q: e.matmul(bnk, lhsT=cc_[:], rhs=Yr[:, q * 4:(q + 1) * 4, :],
                                                    start=True, stop=False), reads=[b_c2, b_Y], writes=[b_banks[bk]])
        p.op("pe", lambda e, bnk=bnk, q=q: e.matmul(bnk, lhsT=ss_[:], rhs=Yi[:, q * 4:(q + 1) * 4, :],
                                                    start=False, stop=True), reads=[b_c2, b_Y], writes=[b_banks[bk]])
        p.op("act", lambda e, bnk=bnk, q=q: e.activation(out=fo[:, q * 4:(q + 1) * 4, :], in_=bnk, func=AF.Copy,
                                                         scale=1.0 / 2048.0), reads=[b_banks[bk]], wpart=[b_fo])
    p.dma("sp", fo_d, fo[:], reads=[b_fo], writes=[bfo])


def rope_weights(c, w4, b_w4, nk, nh, hd, r0):
    p = c.p
    wr = c.sb([128, nk, nh, 64], BF16)
    b_wr = Buf()
    p.op("dve", lambda e: e.tensor_scalar(out=wr[:, :, :, 0:32], in0=w4[:, :, :, r0 + 32:r0 + 64], scalar1=-1.0,
                                          scalar2=None, op0=ALU.mult), reads=[b_w4], wpart=[b_wr])
    p.op("dve", lambda e: e.tensor_copy(out=wr[:, :, :, 32:64], in_=w4[:, :, :, r0:r0 + 32]),
         reads=[b_w4], wpart=[b_wr])
    return wr, b_wr


def rope_combine(c, br, b_br, brot, b_brot, cs, b_cs, tc0, tn, t1, b_t1, t2, b_t2, out, b_out_w):
    p = c.p
    p.op("dve", lambda e: e.tensor_tensor(out=t1[:64, :tn], in0=br[:64, :tn], in1=cs[:, 0, tc0:tc0 + tn], op=ALU.mult),
         reads=[b_br, b_cs], writes=[b_t1])
    p.op("dve", lambda e: e.tensor_tensor(out=t2[:64, :tn], in0=brot[:64, :tn], in1=cs[:, 1, tc0:tc0 + tn], op=ALU.mult),
         reads=[b_brot, b_cs], writes=[b_t2])
    p.op("dve", lambda e: e.tensor_tensor(out=out, in0=t1[:64, :tn], in1=t2[:64, :tn], op=ALU.add),
         reads=[b_t1, b_t2], writes=b_out_w)


def ph_q(c, hT_d, bh, win_d, bw, wuq_d, bwuq, gq_d, bgq, cs_d, bcs, qT_d, bq, ntok):
    p = c.p
    T = 512
    hT = c.sb([128, 32, T], BF16)
    b_hT = Buf()
    wbufs = [c.sb([128, 32, 512], BF16) for _ in range(2)]
    b_wbufs = [Buf(), Buf()]
    banks = [c.ps([128, 512], F32) for _ in range(6)]
    b_banks = [Buf() for _ in range(6)]
    ssb = c.ps([128, 512], F32)
    b_ssb = Buf()
    wuq = c.sb([128, 6, 8, 192], BF16)
    b_wuq = Buf()
    p.dma("pool", wuq[:], wuq_d.rearrange("(c p) (h d) -> p c h d", p=128, d=192), reads=[bwuq], writes=[b_wuq])
    wr, b_wr = rope_weights(c, wuq, b_wuq, 6, 8, 192, 128)
    gq = c.sb([128, 6], F32)
    b_gq = Buf()
    p.dma("act", gq[:], gq_d, reads=[bgq], writes=[b_gq])
    cs = c.sb([64, 2, ntok], F32)
    b_cs = Buf()
    p.dma("act", cs[:], cs_d.rearrange("a p t -> p a t"), reads=[bcs], writes=[b_cs])
    ones, b_ones = ones_bf(c, [128, 128])
    cq = c.sb([128, 6, T], BF16)
    cqn = c.sb([128, 6, T], BF16)
    b_cq, b_cqn = Buf(), Buf()
    sq = [c.sb([128, T], BF16) for _ in range(2)]
    b_sq = [Buf(), Buf()]
    rs = c.sb([128, T], F32)
    b_rs = Buf()
    stg = [c.sb([128, T], BF16) for _ in range(3)]
    b_stg = [Buf() for _ in range(3)]
    t1 = c.sb([64, T], F32)
    t2 = c.sb([64, T], F32)
    b_t1, b_t2 = Buf(), Buf()
    st = St()
    for s in range(ntok // T):
        t0 = s * T
        load_hT(c, hT, b_hT, hT_d, bh, t0, T)

        def epi(tag, ch, ti, bank, b_bank, wd):
            qc = tag * 4 + ch
            p.op("act", lambda e: e.activation(out=cq[:, qc, :], in_=bank[:, :], func=AF.Copy),
                 reads=[b_bank], wpart=[b_cq])
            si = qc % 2
            p.op("act", lambda e: e.activation(out=sq[si][:], in_=bank[:, :], func=AF.Square),
                 reads=[b_bank], writes=[b_sq[si]])
            p.op("pe", lambda e: e.matmul(ssb[:, :], lhsT=ones[:], rhs=sq[si][:], start=(qc == 0), stop=(qc == 5)),
                 reads=[b_ones, b_sq[si]], writes=[b_ssb])

        fm_stream(c, [(win_d, bw, 1024, 512, 0), (win_d, bw, 1536, 256, 1)], 32,
                  lambda kc, tc0, tn: hT[:, kc, tc0:tc0 + tn], b_hT, [(0, T)], epi, wbufs, b_wbufs, banks, b_banks, st)
        fm_rmsnorm(c, cq, b_cq, 6, 768, T, gq, b_gq, ssb, b_ssb, rs, b_rs, cqn, b_cqn)
        for h in range(8):
            bk = st.bi % 6
            st.bi += 1
            for kc in range(6):
                p.op("pe", lambda e, bk=bk, kc=kc, h=h: e.matmul(banks[bk][:, :], lhsT=wuq[:, kc, h, 0:128],
                                                                 rhs=cqn[:, kc, :], start=(kc == 0), stop=(kc == 5)),
                     reads=[b_wuq, b_cqn], writes=[b_banks[bk]])
            si = st.k % 3
            st.k += 1
            evac(p, st.k, stg[si][:], banks[bk][:, :], [b_banks[bk]], wb=[b_stg[si]])
            p.dma("sp", qT_d[h, 0:128, t0:t0 + T], stg[si][:], reads=[b_stg[si]], wpart=[bq])
            bk1 = st.bi % 6
            bk2 = (st.bi + 1) % 6
            st.bi += 2
            for kc in range(6):
                p.op("pe", lambda e, bk1=bk1, kc=kc, h=h: e.matmul(banks[bk1][:64, :], lhsT=wuq[:, kc, h, 128:192],
                                                                   rhs=cqn[:, kc, :], start=(kc == 0), stop=(kc == 5)),
                     reads=[b_wuq, b_cqn], writes=[b_banks[bk1]])
            for kc in range(6):
                p.op("pe", lambda e, bk2=bk2, kc=kc, h=h: e.matmul(banks[bk2][:64, :], lhsT=wr[:, kc, h, :],
                                                                   rhs=cqn[:, kc, :], start=(kc == 0), stop=(kc == 5)),
                     reads=[b_wr, b_cqn], writes=[b_banks[bk2]])
            si = st.k % 3
            st.k += 1
            rope_combine(c, banks[bk1], b_banks[bk1], banks[bk2], b_banks[bk2], cs, b_cs, t0, T, t1, b_t1, t2, b_t2,
                         stg[si][:64, :], [b_stg[si]])
            p.dma("sp", qT_d[h, 128:192, t0:t0 + T], stg[si][:64, :], reads=[b_stg[si]], wpart=[bq])


def ph_kv(c, hT_d, bh, win_d, bw, wukv_d, bwukv, gkv_d, bgkv, cs_d, bcs, kT_d, bk_, krT_d, bkr, v_d, bv, ntok):
    p = c.p
    T = 512
    hT = c.sb([128, 32, T], BF16)
    b_hT = Buf()
    wbufs = [c.sb([128, 32, 512], BF16) for _ in range(2)]
    b_wbufs = [Buf(), Buf()]
    banks = [c.ps([128, 512], F32) for _ in range(6)]
    b_banks = [Buf() for _ in range(6)]
    ssb = c.ps([128, 512], F32)
    b_ssb = Buf()
    wukv = c.sb([128, 4, 8, 256], BF16)
    b_wukv = Buf()
    p.dma("pool", wukv[:], wukv_d.rearrange("(c p) (h d) -> p c h d", p=128, d=256), reads=[bwukv], writes=[b_wukv])
    wkr = c.sb([128, 32, 1, 64], BF16)
    b_wkr = Buf()
    p.dma("pool", wkr[:, :, 0, :], win_d[:, 2304:2368].rearrange("(c p) n -> p c n", p=128), reads=[bw], writes=[b_wkr])
    wkrr, b_wkrr = rope_weights(c, wkr, b_wkr, 32, 1, 64, 0)
    gkv = c.sb([128, 4], F32)
    b_gkv = Buf()
    p.dma("act", gkv[:], gkv_d, reads=[bgkv], writes=[b_gkv])
    cs = c.sb([64, 2, ntok], F32)
    b_cs = Buf()
    p.dma("act", cs[:], cs_d.rearrange("a p t -> p a t"), reads=[bcs], writes=[b_cs])
    ones, b_ones = ones_bf(c, [128, 128])
    ck = c.sb([128, 4, T], BF16)
    ckn = c.sb([128, 4, T], BF16)
    b_ck, b_ckn = Buf(), Buf()
    sq = [c.sb([128, T], BF16) for _ in range(2)]
    b_sq = [Buf(), Buf()]
    rs = c.sb([128, T], F32)
    b_rs = Buf()
    stg = [c.sb([128, T], BF16) for _ in range(3)]
    b_stg = [Buf() for _ in range(3)]
    t1 = c.sb([64, T], F32)
    t2 = c.sb([64, T], F32)
    b_t1, b_t2 = Buf(), Buf()
    st = St()
    for s in range(ntok // T):
        t0 = s * T
        load_hT(c, hT, b_hT, hT_d, bh, t0, T)

        def epi(tag, ch, ti, bank, b_bank, wd):
            qc = ch
            p.op("act", lambda e: e.activation(out=ck[:, qc, :], in_=bank[:, :], func=AF.Copy),
                 reads=[b_bank], wpart=[b_ck])
            si = qc % 2
            p.op("act", lambda e: e.activation(out=sq[si][:], in_=bank[:, :], func=AF.Square),
                 reads=[b_bank], writes=[b_sq[si]])
            p.op("pe", lambda e: e.matmul(ssb[:, :], lhsT=ones[:], rhs=sq[si][:], start=(qc == 0), stop=(qc == 3)),
                 reads=[b_ones, b_sq[si]], writes=[b_ssb])

        fm_stream(c, [(win_d, bw, 1792, 512, 0)], 32,
                  lambda kc, tc0, tn: hT[:, kc, tc0:tc0 + tn], b_hT, [(0, T)], epi, wbufs, b_wbufs, banks, b_banks, st)
        fm_rmsnorm(c, ck, b_ck, 4, 512, T, gkv, b_gkv, ssb, b_ssb, rs, b_rs, ckn, b_ckn)
        for h in range(8):
            bk = st.bi % 6
            st.bi += 1
            for kc in range(4):
                p.op("pe", lambda e, bk=bk, kc=kc, h=h: e.matmul(banks[bk][:, :], lhsT=wukv[:, kc, h, 0:128],
                                                                 rhs=ckn[:, kc, :], start=(kc == 0), stop=(kc == 3)),
                     reads=[b_wukv, b_ckn], writes=[b_banks[bk]])
            si = st.k % 3
            st.k += 1
            evac(p, st.k, stg[si][:], banks[bk][:, :], [b_banks[bk]], wb=[b_stg[si]])
            p.dma("sp", kT_d[h, :, t0:t0 + T], stg[si][:], reads=[b_stg[si]], wpart=[bk_])
        for tl in range(T // 128):
            for hb in range(2):
                bk = st.bi % 6
                st.bi += 1
                for kc in range(4):
                    p.op("pe", lambda e, bk=bk, kc=kc, hb=hb, tl=tl: e.matmul(
                        banks[bk][:, :].rearrange("p (h d) -> p h d", d=128),
                        lhsT=ckn[:, kc, tl * 128:(tl + 1) * 128],
                        rhs=wukv[:, kc, hb * 4:(hb + 1) * 4, 128:256], start=(kc == 0), stop=(kc == 3)),
                        reads=[b_wukv, b_ckn], writes=[b_banks[bk]])
                si = st.k % 3
                st.k += 1
                evac(p, st.k, stg[si][:], banks[bk][:, :], [b_banks[bk]], wb=[b_stg[si]])
                r0 = t0 + tl * 128
                p.dma("sp", v_d[r0:r0 + 128, hb * 512:(hb + 1) * 512], stg[si][:], reads=[b_stg[si]], wpart=[bv])
        bk1 = st.bi % 6
        bk2 = (st.bi + 1) % 6
        st.bi += 2
        for kc in range(32):
            p.op("pe", lambda e, bk1=bk1, kc=kc: e.matmul(banks[bk1][:64, :], lhsT=wkr[:, kc, 0, :], rhs=hT[:, kc, :],
                                                          start=(kc == 0), stop=(kc == 31)),
                 reads=[b_wkr, b_hT], writes=[b_banks[bk1]])
        for kc in range(32):
            p.op("pe", lambda e, bk2=bk2, kc=kc: e.matmul(banks[bk2][:64, :], lhsT=wkrr[:, kc, 0, :], rhs=hT[:, kc, :],
                                                          start=(kc == 0), stop=(kc == 31)),
                 reads=[b_wkrr, b_hT], writes=[b_banks[bk2]])
        si = st.k % 3
        st.k += 1
        rope_combine(c, banks[bk1], b_banks[bk1], banks[bk2], b_banks[bk2], cs, b_cs, t0, T, t1, b_t1, t2, b_t2,
                     stg[si][:64, :], [b_stg[si]])
        p.dma("sp", krT_d[:, t0:t0 + T], stg[si][:64, :], reads=[b_stg[si]], wpart=[bkr])


def ph_attn(c, qT_d, bq, kT_d, bk_, krT_d, bkr, v_d, bv, aT_d, ba, ntok, nkeys, nheads=8):
    p = c.p
    NKT = nkeys // 128
    scale = 1.0 / math.sqrt(192.0)
    idb, b_id = c.ident()
    kr = c.sb([64, nkeys], BF16)
    b_kr = Buf()
    p.dma("sp", kr[:], krT_d, reads=[bkr], writes=[b_kr])
    kT = c.sb([128, nkeys], BF16)
    b_kT = Buf()
    V = c.sb([128, NKT, 132], BF16)
    b_V = Buf()
    p.op("pool", lambda e: e.memset(V[:, :, 128:132], 1.0), wpart=[b_V])
    qn = c.sb([128, ntok], BF16)
    qr = c.sb([64, ntok], BF16)
    b_qn, b_qr = Buf(), Buf()
    NSB = 3
    sbank = [c.ps([128, 512], F32) for _ in range(NSB)]
    b_sbank = [Buf() for _ in range(NSB)]
    obank = [c.ps([128, 132], F32) for _ in range(4)]
    b_obank = [Buf() for _ in range(4)]
    tbank = c.ps([128, 4, 128], BF16)
    b_tbank = Buf()
    P = [c.sb([128, 512], BF16) for _ in range(NSB)]
    b_P = [Buf() for _ in range(NSB)]
    rec = c.sb([128, 1], F32)
    b_rec = Buf()
    on = c.sb([128, 4, 128], BF16)
    b_on = Buf()
    ast = [c.sb([128, 512], BF16) for _ in range(2)]
    b_ast = [Buf(), Buf()]
    k = 0
    for h in range(nheads):
        p.dma("sp", kT[:], kT_d[h], reads=[bk_], writes=[b_kT])
        p.dma("act", V[:, :, 0:128], v_d[:, h * 128:(h + 1) * 128].rearrange("(t p) d -> p t d", p=128),
              reads=[bv], wpart=[b_V])
        p.dma("sp", qn[:], qT_d[h, 0:128, :], reads=[bq], writes=[b_qn])
        p.dma("sp", qr[:], qT_d[h, 128:192, :], reads=[bq], writes=[b_qr])
        for qg in range(ntok // 512):
            q0 = qg * 512
            kbase = k
            k += NKT

            def emit_S(kt, q0=q0, kbase=kbase):
                sb_ = (kbase + kt) % NSB
                p.op("pe", lambda e: e.matmul(sbank[sb_][:, :], lhsT=kT[:, kt * 128:(kt + 1) * 128],
                                              rhs=qn[:, q0:q0 + 512], start=True, stop=False),
                     reads=[b_kT, b_qn], writes=[b_sbank[sb_]])
                p.op("pe", lambda e: e.matmul(sbank[sb_][:, :], lhsT=kr[:, kt * 128:(kt + 1) * 128],
                                              rhs=qr[:, q0:q0 + 512], start=False, stop=True),
                     reads=[b_kr, b_qr], writes=[b_sbank[sb_]])
                p.op("act", lambda e: e.activation(out=P[sb_][:], in_=sbank[sb_][:, :], func=AF.Exp, scale=scale),
                     reads=[b_sbank[sb_]], writes=[b_P[sb_]])

            emit_S(0)
            for kt in range(NKT):
                if kt + 1 < NKT:
                    emit_S(kt + 1)
                sb_ = (kbase + kt) % NSB
                for qt in range(4):
                    p.op("pe", lambda e, sb_=sb_, kt=kt, qt=qt: e.matmul(obank[qt][:, 0:129],
                                                                         lhsT=P[sb_][:, qt * 128:(qt + 1) * 128],
                                                                         rhs=V[:, kt, 0:129], start=(kt == 0),
                                                                         stop=(kt == NKT - 1)),
                         reads=[b_P[sb_], b_V], writes=[b_obank[qt]])
            for qt in range(4):
                p.op("dve", lambda e, qt=qt: e.reciprocal(out=rec[:], in_=obank[qt][:, 128:129]),
                     reads=[b_obank[qt]], writes=[b_rec])
                p.op("dve", lambda e, qt=qt: e.tensor_scalar(out=on[:, qt, :], in0=obank[qt][:, 0:128],
                                                             scalar1=rec[:, 0:1], scalar2=None, op0=ALU.mult),
                     reads=[b_obank[qt], b_rec], wpart=[b_on])
            for qt in range(4):
                p.op("pe", lambda e, qt=qt: e.transpose(out=tbank[:, qt, :], in_=on[:, qt, :], identity=idb[:]),
                     reads=[b_on, b_id], writes=[b_tbank])
            ai = (h * (ntok // 512) + qg) % 2
            p.op("act", lambda e, ai=ai: e.activation(out=ast[ai][:], in_=tbank[:].rearrange("p a b -> p (a b)"),
                                                      func=AF.Copy), reads=[b_tbank], writes=[b_ast[ai]])
            p.dma("sp", aT_d[h * 128:(h + 1) * 128, q0:q0 + 512], ast[ai][:], reads=[b_ast[ai]], wpart=[ba])


def ph_sg(c, hT_d, bh, win_d, bw, gsg_d, bgsg, wsT_d, bws, bsb_d, bbs, sgT_d, bsg, ntok):
    p = c.p
    T = 512
    NTL = T // 128
    hT = c.sb([128, 32, T], BF16)
    b_hT = Buf()
    wbufs = [c.sb([128, 32, 512], BF16) for _ in range(2)]
    b_wbufs = [Buf(), Buf()]
    banks = [c.ps([128, 512], F32) for _ in range(8)]
    b_banks = [Buf() for _ in range(8)]
    gsg = c.sb([128, 2048], F32)
    b_gsg = Buf()
    p.dma("act", gsg[:], gsg_d, reads=[bgsg], writes=[b_gsg])
    wsT = c.sb([128, 16, 128], BF16)
    b_wsT = Buf()
    p.dma("pool", wsT[:], wsT_d.rearrange("g q p -> q g p"), reads=[bws], writes=[b_wsT])
    bsb = c.sb([128, 16, 128], F32)
    b_bsb = Buf()
    p.dma("act", bsb[:], bsb_d, reads=[bbs], writes=[b_bsb])
    vn = c.sb([128, NTL, 2048], BF16)
    b_vn = Buf()
    gv = [c.sb([128, 4, 128], F32) for _ in range(2)]
    b_gv = [Buf(), Buf()]
    xc = [c.sb([128, 4, 128], F32) for _ in range(2)]
    b_xc = [Buf(), Buf()]
    sqj = c.sb([128, 4, 128], F32)
    b_sqj = Buf()
    mu = [c.sb([128, 4], F32) for _ in range(2)]
    b_mu = [Buf(), Buf()]
    var = [c.sb([128, 4], F32) for _ in range(2)]
    b_var = [Buf(), Buf()]
    uT = [c.sb([128, T], F32) for _ in range(2)]
    b_uT = [Buf(), Buf()]
    tt = [c.sb([128, T], F32) for _ in range(2)]
    b_tt = [Buf(), Buf()]
    stg = [c.sb([128, T], BF16) for _ in range(2)]
    b_stg = [Buf(), Buf()]
    st = St()
    for s in range(ntok // T):
        t0 = s * T
        load_hT(c, hT, b_hT, hT_d, bh, t0, T)
        for vb in range(4):
            w = st.wi % 2
            st.wi += 1
            col0 = 4416 + vb * 512
            p.dma("pool", wbufs[w][:], win_d[:, col0:col0 + 512].rearrange("(c p) n -> p c n", p=128),
                  reads=[bw], writes=[b_wbufs[w]])
            for tl in range(NTL):
                bk = st.bi % 8
                st.bi += 1
                for kc in range(32):
                    p.op("pe", lambda e, bk=bk, kc=kc, tl=tl, w=w: e.matmul(
                        banks[bk][:, :], lhsT=hT[:, kc, tl * 128:(tl + 1) * 128], rhs=wbufs[w][:, kc, :],
                        start=(kc == 0), stop=(kc == 31)), reads=[b_hT, b_wbufs[w]], writes=[b_banks[bk]])
                i = st.k % 2
                st.k += 1
                bank3 = banks[bk][:, :].rearrange("p (g d) -> p g d", d=128)
                p.op("act", lambda e, i=i, bank3=bank3: e.activation(out=gv[i][:], in_=bank3, func=AF.Gelu_apprx_tanh),
                     reads=[b_banks[bk]], writes=[b_gv[i]])
                p.op("dve", lambda e, i=i: e.reduce_sum(out=mu[i][:], in_=gv[i][:], axis=AX.X),
                     reads=[b_gv[i]], writes=[b_mu[i]])
                p.op("dve", lambda e, i=i: e.tensor_scalar(out=mu[i][:], in0=mu[i][:], scalar1=1.0 / 128, scalar2=None,
                                                           op0=ALU.mult), reads=[b_mu[i]], writes=[b_mu[i]])
                p.op("dve", lambda e, i=i: e.tensor_tensor(out=xc[i][:], in0=gv[i][:],
                                                           in1=mu[i][:].unsqueeze(2).to_broadcast([128, 4, 128]),
                                                           op=ALU.subtract), reads=[b_gv[i], b_mu[i]], writes=[b_xc[i]])
                p.op("dve", lambda e, i=i: e.tensor_tensor(out=sqj[:], in0=xc[i][:], in1=xc[i][:], op=ALU.mult),
                     reads=[b_xc[i]], writes=[b_sqj])
                p.op("dve", lambda e, i=i: e.reduce_sum(out=var[i][:], in_=sqj[:], axis=AX.X),
                     reads=[b_sqj], writes=[b_var[i]])
                rstd_ops(p, "dve", var[i][:], var[i][:], 128, [b_var[i]], [b_var[i]])
                p.op("dve", lambda e, i=i: e.tensor_tensor(out=xc[i][:], in0=xc[i][:],
                                                           in1=var[i][:].unsqueeze(2).to_broadcast([128, 4, 128]),
                                                           op=ALU.mult), reads=[b_xc[i], b_var[i]], writes=[b_xc[i]])
                p.op("dve", lambda e, i=i, tl=tl, vb=vb: e.tensor_tensor(
                    out=vn[:, tl, vb * 512:(vb + 1) * 512], in0=xc[i][:].rearrange("p g d -> p (g d)"),
                    in1=gsg[:, vb * 512:(vb + 1) * 512], op=ALU.mult),
                    reads=[b_xc[i], b_gsg], wpart=[b_vn])

        def epi(tag, ch, ti, bank, b_bank, wd):
            g = tag * 4 + ch
            i = g % 2
            p.op("act", lambda e: e.activation(out=uT[i][:], in_=bank[:, :], func=AF.Gelu_apprx_tanh),
                 reads=[b_bank], writes=[b_uT[i]])
            bk = st.bi % 8
            st.bi += 1
            for tl in range(NTL):
                p.op("pe", lambda e, tl=tl: e.matmul(banks[bk][:, tl * 128:(tl + 1) * 128],
                                                     lhsT=vn[:, tl, g * 128:(g + 1) * 128], rhs=wsT[:, g, :],
                                                     start=True, stop=True),
                     reads=[b_vn, b_wsT], writes=[b_banks[bk]])
            p.op("dve", lambda e: e.tensor_tensor(out=tt[i][:].rearrange("p (a b) -> p a b", b=128),
                                                  in0=banks[bk][:, :].rearrange("p (a b) -> p a b", b=128),
                                                  in1=bsb[:, g, :].unsqueeze(1).to_broadcast([128, NTL, 128]),
                                                  op=ALU.add), reads=[b_banks[bk], b_bsb], writes=[b_tt[i]])
            p.op("dve", lambda e: e.tensor_tensor(out=stg[i][:], in0=tt[i][:], in1=uT[i][:], op=ALU.mult),
                 reads=[b_tt[i], b_uT[i]], writes=[b_stg[i]])
            p.dma("sp", sgT_d[g * 128:(g + 1) * 128, t0:t0 + T], stg[i][:], reads=[b_stg[i]], wpart=[bsg])

        fm_stream(c, [(win_d, bw, 2368 + 512 * i, 512, i) for i in range(4)], 32,
                  lambda kc, tc0, tn: hT[:, kc, tc0:tc0 + tn], b_hT, [(0, T)], epi, wbufs, b_wbufs, banks, b_banks, st)


def ph_merge(c, hT_d, bh, fT_d, bf, aT_d, ba, sgT_d, bsg, wg_d, bwg, bgT_d, bbg, wf_d, bwf, wa_d, bwa, ws_d, bws,
             mT_d, bm, ntok):
    p = c.p
    T = 1024 if ntok % 1024 == 0 else 512
    CW = 128
    NTG = T // 512
    hT = c.sb([128, 32, T], BF16)
    fT = c.sb([128, 8, T], BF16)
    aT = c.sb([128, 8, T], BF16)
    sT = c.sb([128, 16, T], BF16)
    b_hT, b_fT, b_aT, b_sT = Buf(), Buf(), Buf(), Buf()
    wg = [c.sb([128, 32, 3, CW], BF16) for _ in range(2)]
    wf = [c.sb([128, 8, CW], BF16) for _ in range(2)]
    wa = [c.sb([128, 8, CW], BF16) for _ in range(2)]
    ws = [c.sb([128, 16, CW], BF16) for _ in range(2)]
    b_wg, b_wf, b_wa, b_ws = [Buf(), Buf()], [Buf(), Buf()], [Buf(), Buf()], [Buf(), Buf()]
    bgT = c.sb([128, 96], F32)
    b_bgT = Buf()
    p.dma("act", bgT[:], bgT_d, reads=[bbg], writes=[b_bgT])
    banks = [c.ps([128, 512], F32) for _ in range(8)]
    b_banks = [Buf() for _ in range(8)]
    gt = [c.sb([128, 512], F32) for _ in range(3)]
    b_gt = [Buf() for _ in range(3)]
    m1 = c.sb([128, 512], F32)
    m2 = c.sb([128, 512], F32)
    b_m1, b_m2 = Buf(), Buf()
    stg = [c.sb([128, 512], BF16) for _ in range(2)]
    b_stg = [Buf(), Buf()]
    bi = 0
    wi = 0
    for s in range(ntok // T):
        t0 = s * T
        load_hT(c, hT, b_hT, hT_d, bh, t0, T)
        p.dma("sp", fT[:], fT_d[:, t0:t0 + T].rearrange("(c p) t -> p c t", p=128), reads=[bf], writes=[b_fT])
        p.dma("sp", aT[:], aT_d[:, t0:t0 + T].rearrange("(c p) t -> p c t", p=128), reads=[ba], writes=[b_aT])
        p.dma("sp", sT[:], sgT_d[:, t0:t0 + T].rearrange("(c p) t -> p c t", p=128), reads=[bsg], writes=[b_sT])
        for jb in range(D // CW):
            w = wi % 2
            wi += 1
            c0 = jb * CW
            for b in range(3):
                p.dma("pool", wg[w][:, :, b, :],
                      wg_d[:, b * D + c0:b * D + c0 + CW].rearrange("(c p) n -> p c n", p=128),
                      reads=[bwg], wpart=[b_wg[w]])
            p.dma("pool", wf[w][:], wf_d[:, c0:c0 + CW].rearrange("(c p) n -> p c n", p=128), reads=[bwf], writes=[b_wf[w]])
            p.dma("pool", wa[w][:], wa_d[:, c0:c0 + CW].rearrange("(c p) n -> p c n", p=128), reads=[bwa], writes=[b_wa[w]])
            p.dma("pool", ws[w][:], ws_d[:, c0:c0 + CW].rearrange("(c p) n -> p c n", p=128), reads=[bws], writes=[b_ws[w]])
            for sub in range(CW // 128):
              for tg in range(NTG):
                tsl = slice(tg * 512, (tg + 1) * 512)
                j = jb * (CW // 128) + sub
                n0 = j * 128
                cs_ = slice(sub * 128, (sub + 1) * 128)
                gb_ = []
                for b in range(3):
                    bk = bi % 8
                    bi += 1
                    gb_.append(bk)
                    for kc in range(32):
                        p.op("pe", lambda e, bk=bk, kc=kc, b=b, w=w, cs_=cs_, tsl=tsl: e.matmul(
                            banks[bk][:, :], lhsT=wg[w][:, kc, b, cs_], rhs=hT[:, kc, tsl], start=(kc == 0), stop=(kc == 31)),
                            reads=[b_wg[w], b_hT], writes=[b_banks[bk]])
                yb_ = []
                for b, (wt, b_wt, act, b_act, nk) in enumerate(((wf, b_wf, fT, b_fT, 8), (wa, b_wa, aT, b_aT, 8),
                                                                (ws, b_ws, sT, b_sT, 16))):
                    bk = bi % 8
                    bi += 1
                    yb_.append(bk)
                    for kc in range(nk):
                        p.op("pe", lambda e, bk=bk, kc=kc, wt=wt, act=act, nk=nk, w=w, cs_=cs_, tsl=tsl: e.matmul(
                            banks[bk][:, :], lhsT=wt[w][:, kc, cs_], rhs=act[:, kc, tsl], start=(kc == 0),
                            stop=(kc == nk - 1)), reads=[b_wt[w], b_act], writes=[b_banks[bk]])
                for b in range(3):
                    p.op("act", lambda e, b=b, bk=gb_[b], j=j: e.activation(out=gt[b][:], in_=banks[bk][:, :],
                                                                            func=AF.Sigmoid,
                                                                            bias=bgT[:, b * 32 + j:b * 32 + j + 1]),
                         reads=[b_banks[gb_[b]], b_bgT], writes=[b_gt[b]])
                p.op("dve", lambda e, bk=yb_[0]: e.tensor_tensor(out=m1[:], in0=banks[bk][:, :], in1=gt[0][:], op=ALU.mult),
                     reads=[b_banks[yb_[0]], b_gt[0]], writes=[b_m1])
                p.op("dve", lambda e, bk=yb_[1]: e.tensor_tensor(out=m2[:], in0=banks[bk][:, :], in1=gt[1][:], op=ALU.mult),
                     reads=[b_banks[yb_[1]], b_gt[1]], writes=[b_m2])
                p.op("dve", lambda e: e.tensor_tensor(out=m1[:], in0=m1[:], in1=m2[:], op=ALU.add),
                     reads=[b_m1, b_m2], writes=[b_m1])
                p.op("dve", lambda e, bk=yb_[2]: e.tensor_tensor(out=m2[:], in0=banks[bk][:, :], in1=gt[2][:], op=ALU.mult),
                     reads=[b_banks[yb_[2]], b_gt[2]], writes=[b_m2])
                si = (j * NTG + tg) % 2
                p.op("dve", lambda e, si=si: e.tensor_tensor(out=stg[si][:], in0=m1[:], in1=m2[:], op=ALU.add),
                     reads=[b_m1, b_m2], writes=[b_stg[si]])
                p.dma("sp", mT_d[n0:n0 + 128, t0 + tg * 512:t0 + (tg + 1) * 512], stg[si][:], reads=[b_stg[si]], wpart=[bm])


def ph_memkv(c, mT_d, bmT, wmk_d, bwk, wmv_d, bwv, kmT_d, bkm, vm_d, bvm):
    p = c.p
    mT = c.sb([128, 32, 256], BF16)
    b_mT = Buf()
    load_hT(c, mT, b_mT, mT_d, bmT, 0, 256)
    wbufs = [c.sb([128, 32, 512], BF16) for _ in range(2)]
    b_wbufs = [Buf(), Buf()]
    banks = [c.ps([128, 512], F32) for _ in range(4)]
    b_banks = [Buf() for _ in range(4)]
    stg = [c.sb([128, 512], BF16) for _ in range(2)]
    b_stg = [Buf(), Buf()]
    st = St()

    def epi(tag, ch, ti, bank, b_bank, wd):
        si = st.k % 2
        st.k += 1
        evac(p, st.k, stg[si][:, :256], bank[:, :256], [b_bank], wb=[b_stg[si]])
        r0 = (tag * 4 + ch) * 128
        p.dma("sp", kmT_d[r0:r0 + 128, :], stg[si][:, :256], reads=[b_stg[si]], wpart=[bkm])

    fm_stream(c, [(wmk_d, bwk, 0, 512, 0), (wmk_d, bwk, 512, 512, 1)], 32,
              lambda kc, tc0, tn: mT[:, kc, tc0:tc0 + tn], b_mT, [(0, 256)], epi, wbufs, b_wbufs, banks, b_banks, st)
    for vb in range(2):
        w = st.wi % 2
        st.wi += 1
        p.dma("pool", wbufs[w][:], wmv_d[:, vb * 512:(vb + 1) * 512].rearrange("(c p) n -> p c n", p=128),
              reads=[bwv], writes=[b_wbufs[w]])
        for mt in range(2):
            bk = st.bi % 4
            st.bi += 1
            for kc in range(32):
                p.op("pe", lambda e, bk=bk, kc=kc, mt=mt, w=w: e.matmul(banks[bk][:, :], lhsT=mT[:, kc, mt * 128:(mt + 1) * 128],
                                                                        rhs=wbufs[w][:, kc, :], start=(kc == 0), stop=(kc == 31)),
                     reads=[b_mT, b_wbufs[w]], writes=[b_banks[bk]])
            si = st.k % 2
            st.k += 1
            evac(p, st.k, stg[si][:], banks[bk][:, :], [b_banks[bk]], wb=[b_stg[si]])
            p.dma("sp", vm_d[mt * 128:(mt + 1) * 128, vb * 512:(vb + 1) * 512], stg[si][:], reads=[b_stg[si]], wpart=[bvm])


def ph_mq(c, hT_d, bh, wmq_d, bwq, kmT_d, bkm, vm_d, bvm, omT_d, bom, ntok):
    p = c.p
    T = 512
    idb, b_id = c.ident()
    hT = c.sb([128, 32, T], BF16)
    b_hT = Buf()
    wbufs = [c.sb([128, 32, 512], BF16) for _ in range(2)]
    b_wbufs = [Buf(), Buf()]
    banks = [c.ps([128, 512], F32) for _ in range(6)]
    b_banks = [Buf() for _ in range(6)]
    tbank = c.ps([128, 4, 2, 128], BF16)
    b_tbank = Buf()
    km = c.sb([128, 8, 256], BF16)
    b_km = Buf()
    p.dma("sp", km[:], kmT_d.rearrange("(c p) m -> p c m", p=128), reads=[bkm], writes=[b_km])
    vm = c.sb([128, 2, 4, 260], BF16)
    b_vm = Buf()
    p.op("pool", lambda e: e.memset(vm[:, :, :, 256:260], 1.0), wpart=[b_vm])
    for mt in range(2):
        p.dma("sp", vm[:, mt, :, 0:256], vm_d[mt * 128:(mt + 1) * 128, :].rearrange("p (h d) -> p h d", d=256),
              reads=[bvm], wpart=[b_vm])
    qm = c.sb([128, 8, T], BF16)
    b_qm = Buf()
    P = [c.sb([128, T], BF16) for _ in range(4)]
    b_P = [Buf() for _ in range(4)]
    rec = c.sb([128, 1], F32)
    b_rec = Buf()
    on = c.sb([128, 4, 256], BF16)
    b_on = Buf()
    stg = [c.sb([128, 2, T], BF16) for _ in range(2)]
    b_stg = [Buf(), Buf()]
    st = St()
    pk = 0
    for s in range(ntok // T):
        t0 = s * T
        load_hT(c, hT, b_hT, hT_d, bh, t0, T)

        def epi(tag, ch, ti, bank, b_bank, wd):
            st.k += 1
            evac(p, st.k, qm[:, tag * 4 + ch, :], bank[:, :], [b_bank], wp=[b_qm])

        fm_stream(c, [(wmq_d, bwq, 0, 512, 0), (wmq_d, bwq, 512, 512, 1)], 32,
                  lambda kc, tc0, tn: hT[:, kc, tc0:tc0 + tn], b_hT, [(0, T)], epi, wbufs, b_wbufs, banks, b_banks, st)
        for h in range(4):
            pi = []
            for mt in range(2):
                bk = st.bi % 6
                st.bi += 1
                for dc in range(2):
                    p.op("pe", lambda e, bk=bk, dc=dc, mt=mt, h=h: e.matmul(
                        banks[bk][:, :], lhsT=km[:, 2 * h + dc, mt * 128:(mt + 1) * 128], rhs=qm[:, 2 * h + dc, :],
                        start=(dc == 0), stop=(dc == 1)), reads=[b_km, b_qm], writes=[b_banks[bk]])
                pj = pk % 4
                pk += 1
                pi.append(pj)
                p.op("act", lambda e, bk=bk, pj=pj: e.activation(out=P[pj][:], in_=banks[bk][:, :], func=AF.Exp,
                                                                 scale=1.0 / 16.0), reads=[b_banks[bk]], writes=[b_P[pj]])
            for qt in range(4):
                bk = st.bi % 6
                st.bi += 1
                for mt in range(2):
                    p.op("pe", lambda e, bk=bk, mt=mt, qt=qt, h=h, pj=pi[mt]: e.matmul(
                        banks[bk][:, 0:257], lhsT=P[pj][:, qt * 128:(qt + 1) * 128], rhs=vm[:, mt, h, 0:257],
                        start=(mt == 0), stop=(mt == 1)), reads=[b_P[pi[mt]], b_vm], writes=[b_banks[bk]])
                p.op("dve", lambda e, bk=bk: e.reciprocal(out=rec[:], in_=banks[bk][:, 256:257]),
                     reads=[b_banks[bk]], writes=[b_rec])
                p.op("dve", lambda e, bk=bk, qt=qt: e.tensor_scalar(out=on[:, qt, :], in0=banks[bk][:, 0:256],
                                                                    scalar1=rec[:, 0:1], scalar2=None, op0=ALU.mult),
                     reads=[b_banks[bk], b_rec], wpart=[b_on])
            for qt in range(4):
                for dc in range(2):
                    p.op("pe", lambda e, qt=qt, dc=dc: e.transpose(out=tbank[:, qt, dc, :],
                                                                   in_=on[:, qt, dc * 128:(dc + 1) * 128], identity=idb[:]),
                         reads=[b_on, b_id], writes=[b_tbank])
            si = h % 2
            p.op("act", lambda e, si=si: e.activation(out=stg[si][:].rearrange("p d (q t) -> p q d t", t=128),
                                                      in_=tbank[:], func=AF.Copy), reads=[b_tbank], writes=[b_stg[si]])
            p.dma("sp", omT_d[h * 256:(h + 1) * 256, t0:t0 + T].rearrange("(d p) t -> p d t", p=128), stg[si][:],
                  reads=[b_stg[si]], wpart=[bom])


def ph_up(c, hT_d, bh, wup_d, bwu, cw_d, bcw, pT_d, bpT, ntok):
    p = c.p
    T = 1024
    G = 342
    hT = c.sb([128, 32, T + 2], BF16)
    b_hT = Buf()
    wbufs = [c.sb([128, 32, 256], BF16) for _ in range(4)]
    b_wbufs = [Buf() for _ in range(4)]
    banks = [c.ps([128, 512], F32) for _ in range(8)]
    b_banks = [Buf() for _ in range(8)]
    cw = c.sb([128, 172, 4], F32)
    b_cw = Buf()
    p.dma("act", cw[:], cw_d, reads=[bcw], writes=[b_cw])
    ue = {(k, ch): c.sb([128, T + 2], F32) for k in "gv" for ch in range(2)}
    b_ue = {k: Buf() for k in ue}
    cg = c.sb([128, T], F32)
    cv = c.sb([128, T], F32)
    b_cg, b_cv = Buf(), Buf()
    stg = [c.sb([128, T], BF16) for _ in range(2)]
    b_stg = [Buf(), Buf()]
    st = St()
    NB = 43
    for s in range(ntok // T):
        t0 = s * T
        nst = ntok // T
        lc = ntok if s == 0 else t0 - 1
        rc = ntok + 1 if s == nst - 1 else t0 + T
        p.dma("sp", hT[:, :, 1:T + 1], hT_d[:, :, t0:t0 + T].rearrange("c p t -> p c t"), reads=[bh], writes=[b_hT])
        p.dma("sp", hT[:, :, 0:1], hT_d[:, :, lc:lc + 1].rearrange("c p t -> p c t"), reads=[bh], wpart=[b_hT],
              allow_slow_non_contiguous=True)
        p.dma("sp", hT[:, :, T + 1:T + 2], hT_d[:, :, rc:rc + 1].rearrange("c p t -> p c t"), reads=[bh], wpart=[b_hT],
              allow_slow_non_contiguous=True)

        def epi(tag, ch, ti, bank, b_bank, wd):
            kind, bj = tag
            u = ue[(kind, ch)]
            st.k += 1
            evac(p, st.k, u[:, ti * G:(ti + 1) * G], bank[:, :G], [b_bank], wp=[b_ue[(kind, ch)]])
            if kind == "v" and ti == 2:
                j = bj * 2 + ch
                for (kd, ci, acc, b_acc, eng) in (("g", j, cg, b_cg, "dve"), ("v", 86 + j, cv, b_cv, "dve")):
                    uu = ue[(kd, ch)]
                    bu = b_ue[(kd, ch)]
                    p.op(eng, lambda e, uu=uu, ci=ci, acc=acc: e.tensor_scalar(
                        out=acc[:], in0=uu[:, 0:T], scalar1=cw[:, ci, 0:1], scalar2=cw[:, ci, 3:4],
                        op0=ALU.mult, op1=ALU.add), reads=[bu, b_cw], writes=[b_acc])
                    p.op(eng, lambda e, uu=uu, ci=ci, acc=acc: e.scalar_tensor_tensor(
                        out=acc[:], in0=uu[:, 1:T + 1], scalar=cw[:, ci, 1:2], in1=acc[:], op0=ALU.mult, op1=ALU.add),
                        reads=[bu, b_cw, b_acc], writes=[b_acc])
                    p.op(eng, lambda e, uu=uu, ci=ci, acc=acc: e.scalar_tensor_tensor(
                        out=acc[:], in0=uu[:, 2:T + 2], scalar=cw[:, ci, 2:3], in1=acc[:], op0=ALU.mult, op1=ALU.add),
                        reads=[bu, b_cw, b_acc], writes=[b_acc])
                p.op("act", lambda e: e.activation(out=cg[:], in_=cg[:], func=AF.Gelu_apprx_tanh),
                     reads=[b_cg], writes=[b_cg])
                si = j % 2
                p.op("dve", lambda e: e.tensor_tensor(out=stg[si][:], in0=cg[:], in1=cv[:], op=ALU.mult),
                     reads=[b_cg, b_cv], writes=[b_stg[si]])
                p.dma("sp", pT_d[j, :, t0:t0 + T], stg[si][:], reads=[b_stg[si]], wpart=[bpT])

        blocks = []
        for bj in range(NB):
            blocks.append((wup_d, bwu, 256 * bj, 256, ("g", bj)))
            blocks.append((wup_d, bwu, 11008 + 256 * bj, 256, ("v", bj)))
        fm_stream(c, blocks, 32, lambda kc, tc0, tn: hT[:, kc, tc0:tc0 + tn], b_hT,
                  [(0, G), (G, G), (2 * G, G)], epi, wbufs, b_wbufs, banks, b_banks, st)


BF = ml_dtypes.bfloat16
_PROGS = {}


def _rep(v, n=128):
    return np.ascontiguousarray(np.broadcast_to(v, (n,) + tuple(v.shape)))


def prog_l1():
    c = Ctx()
    x_d, bx = c.din("x", [TOK, D], F32)
    gb_d, bg = c.din("gb", [128, D], F32)
    win_d, bw = c.din("win", [D, 6464], F32)
    dft_d, bd = c.din("dft", [2, 256, 256], F32)
    pos_d, bp = c.din("pos", [64, TOK], I32)
    invf_d, bi = c.din("invf", [64, 1], F32)
    wuq_d, bwuq = c.din("wuq", [768, 1536], F32)
    gq_d, bgq = c.din("gq", [128, 6], F32)
    wukv_d, bwukv = c.din("wukv", [512, 2048], F32)
    gkv_d, bgkv = c.din("gkv", [128, 4], F32)
    gsg_d, bgsg = c.din("gsg", [128, 2048], F32)
    wsT_d, bws = c.din("wsT", [16, 128, 128], F32)
    bsb_d, bbs = c.din("bsb", [128, 16, 128], F32)
    hT_d, bh = c.dint("hT", [32, 128, TOK], BF16)
    cs_d, bcs = c.dint("cs", [2, 64, TOK], F32)
    ab_d, bab = c.dout("ab", [2, 1024, TOK], BF16)
    qT_d, bq = c.dout("qT", [8, 192, TOK], BF16)
    kT_d, bk_ = c.dout("kT", [8, 128, TOK], BF16)
    krT_d, bkr = c.dout("krT", [64, TOK], BF16)
    v_d, bv = c.dout("v", [TOK, 1024], BF16)
    sgT_d, bsg = c.dout("sgT", [2048, TOK], BF16)
    with c.phase():
        ph_head(c, x_d, bx, gb_d, bg, hT_d, bh, TOK)
    with c.phase():
        ph_rope(c, pos_d, bp, invf_d, bi, cs_d, bcs, TOK)
    with c.phase():
        ph_f(c, hT_d, bh, win_d, bw, dft_d, bd, ab_d, bab, TOK)
    with c.phase():
        ph_q(c, hT_d, bh, win_d, bw, wuq_d, bwuq, gq_d, bgq, cs_d, bcs, qT_d, bq, TOK)
    with c.phase():
        ph_kv(c, hT_d, bh, win_d, bw, wukv_d, bwukv, gkv_d, bgkv, cs_d, bcs, kT_d, bk_, krT_d, bkr, v_d, bv, TOK)
    with c.phase():
        ph_sg(c, hT_d, bh, win_d, bw, gsg_d, bgsg, wsT_d, bws, bsb_d, bbs, sgT_d, bsg, TOK)
    c.finish()
    return c.nc


def prog_l2():
    c = Ctx()
    m_d, bm = c.din("m", [128, 2, 128, 128], BF16)
    cs_d, bcs = c.din("cs128", [2, 128, 128], F32)
    tw_d, btw = c.din("tw", [2, 128, 128], F32)
    qT_d, bq = c.din("qT", [8, 192, TOK], BF16)
    kT_d, bk_ = c.din("kT", [8, 128, SEQ], BF16)
    krT_d, bkr = c.din("krT", [64, SEQ], BF16)
    v_d, bv = c.din("v", [SEQ, 1024], BF16)
    fo_d, bfo = c.dout("fo", [128, 128, 128], BF16)
    aT_d, ba = c.dout("aT", [1024, TOK], BF16)
    with c.phase():
        ph_fft(c, m_d, bm, cs_d, bcs, tw_d, btw, fo_d, bfo)
    with c.phase():
        ph_attn(c, qT_d, bq, kT_d, bk_, krT_d, bkr, v_d, bv, aT_d, ba, TOK, SEQ)
    c.finish()
    return c.nc


def prog_l3():
    c = Ctx()
    x_d, bx = c.din("x", [TOK, D], F32)
    gb_d, bg = c.din("gb", [128, D], F32)
    fT_d, bf = c.din("fT", [1024, TOK], BF16)
    aT_d, ba = c.din("aT", [1024, TOK], BF16)
    sT_d, bs_ = c.din("sgT", [2048, TOK], BF16)
    wg_d, bwg = c.din("wg", [D, 3 * D], F32)
    bgT_d, bbg = c.din("bgT", [128, 96], F32)
    wf_d, bwf = c.din("wf", [1024, D], F32)
    wa_d, bwa = c.din("wa", [1024, D], F32)
    ws_d, bws = c.din("ws", [2048, D], F32)
    wo_d, bwo = c.din("wo", [D, D], F32)
    gp1_d, bgp1 = c.din("gp1", [128, D], F32)
    gb2_d, bg2 = c.din("gb2", [128, D], F32)
    mem_d, bmem = c.din("mem", [256, D], F32)
    gkv_d, bgkv = c.din("gmkv", [128, D], F32)
    wmq_d, bwq = c.din("wmq", [D, 1024], F32)
    wmk_d, bwk = c.din("wmk", [D, 1024], F32)
    wmv_d, bwv = c.din("wmv", [D, 1024], F32)
    wmo_d, bwmo = c.din("wmo", [1024, D], F32)
    gp2_d, bgp2 = c.din("gp2", [128, D], F32)
    hT_d, bh = c.dint("hT", [32, 128, TOK], BF16)
    mT_d, bm = c.dint("mgT", [32, 128, TOK], BF16)
    y_d, by = c.dint("y", [TOK, D], F32)
    ssq_d, bss = c.dint("ssq", [TOK, 8], F32)
    x1_d, bx1 = c.dint("x1", [TOK, D], F32)
    memT_d, bmemT = c.dint("memT", [32, 128, 256], BF16)
    kmT_d, bkm = c.dint("kmT", [1024, 256], BF16)
    vm_d, bvm = c.dint("vm", [256, 1024], BF16)
    omT_d, bom = c.dint("omT", [8, 128, TOK], BF16)
    x2_d, bx2 = c.dout("x2", [TOK, D], F32)
    with c.phase():
        ph_head(c, x_d, bx, gb_d, bg, hT_d, bh, TOK)
    with c.phase():
        ph_merge(c, hT_d, bh, fT_d, bf, aT_d, ba, sT_d, bs_, wg_d, bwg, bgT_d, bbg, wf_d, bwf, wa_d, bwa, ws_d, bws,
                 mT_d.rearrange("c p t -> (c p) t"), bm, TOK)
    with c.phase():
        ph_lin(c, mT_d, bm, 32, wo_d, bwo, y_d, by, ssq_d, bss, TOK)
    with c.phase():
        ph_nr(c, y_d, by, ssq_d, bss, gp1_d, bgp1, x_d, bx, x1_d, bx1, TOK)
    with c.phase():
        ph_head(c, x1_d, bx1, gb2_d, bg2, hT_d, bh, TOK)
    with c.phase():
        ph_head(c, mem_d, bmem, gkv_d, bgkv, memT_d, bmemT, 256)
    with c.phase():
        ph_memkv(c, memT_d, bmemT, wmk_d, bwk, wmv_d, bwv, kmT_d, bkm, vm_d, bvm)
    with c.phase():
        ph_mq(c, hT_d, bh, wmq_d, bwq, kmT_d, bkm, vm_d, bvm, omT_d.rearrange("c p t -> (c p) t"), bom, TOK)
    with c.phase():
        ph_lin(c, omT_d, bom, 8, wmo_d, bwmo, y_d, by, ssq_d, bss, TOK)
    with c.phase():
        ph_nr(c, y_d, by, ssq_d, bss, gp2_d, bgp2, x1_d, bx1, x2_d, bx2, TOK)
    c.finish()
    return c.nc


def prog_l4():
    c = Ctx()
    x_d, bx = c.din("x", [TOK, D], F32)
    xh_d, bxh = c.din("xh", [2, D], F32)
    gb_d, bg = c.din("gb", [128, D], F32)
    wup_d, bwu = c.din("wup", [D, 22016], F32)
    cw_d, bcw = c.din("cw", [128, 172, 4], F32)
    wdn_d, bwd = c.din("wdn", [11008, D], F32)
    gp_d, bgp = c.din("gp", [128, D], F32)
    hT_d, bh = c.dint("hT", [32, 128, TOK + 2], BF16)
    pT_d, bpT = c.dint("pT", [86, 128, TOK], BF16)
    y_d, by = c.dint("y", [TOK, D], F32)
    ssq_d, bss = c.dint("ssq", [TOK, 8], F32)
    xo_d, bxo = c.dout("x3", [TOK, D], F32)
    with c.phase():
        ph_head(c, x_d, bx, gb_d, bg, hT_d, bh, TOK, halo=(xh_d, bxh))
    with c.phase():
        ph_up(c, hT_d, bh, wup_d, bwu, cw_d, bcw, pT_d, bpT, TOK)
    with c.phase():
        ph_lin(c, pT_d, bpT, 86, wdn_d, bwd, y_d, by, ssq_d, bss, TOK)
    with c.phase():
        ph_nr(c, y_d, by, ssq_d, bss, gp_d, bgp, x_d, bx, xo_d, bxo, TOK)
    c.finish()
    return c.nc


def _prog(name, fn):
    if name not in _PROGS:
        _PROGS[name] = fn()
    return _PROGS[name]


def _run(nc, maps):
    res = run_bass_kernel_spmd(nc, maps, core_ids=list(range(NCORES)))
    return res.results


def kernel(x, mem, positions, mix_pre_norm, mix_post_norm, w_in, mla_q_norm, w_uq, mla_kv_norm, w_ukv, sg_norm,
           w_spatial, b_spatial, w_br_f, w_br_a, w_br_s, w_gate, b_gate, w_out, mem_pre_norm, mem_post_norm,
           mem_kv_norm, w_mq, w_mk, w_mv, w_mo, ffn_pre_norm, ffn_post_norm, w_up, conv_w, conv_b, w_down):
    f32 = np.float32
    xs = np.asarray(x, f32)[0]
    memv = np.ascontiguousarray(np.asarray(mem, f32)[0])
    pos = np.asarray(positions)[0].astype(np.int32)
    jj = np.arange(256)
    a256 = 2 * np.pi * np.outer(jj, jj) / 256
    dft = np.stack([np.cos(a256), -np.sin(a256)]).astype(f32)
    kk = np.arange(128)
    a128 = 2 * np.pi * np.outer(kk, kk) / 128
    cs128 = np.stack([np.cos(a128), np.sin(a128)]).astype(f32)
    atw = 2 * np.pi * np.outer(kk, kk) / SEQ
    tw = np.stack([np.cos(atw), np.sin(atw)]).astype(f32)
    invf = (10000.0 ** (-np.arange(0, 64, 2, dtype=f32) / 64)).astype(f32)
    invf2 = np.concatenate([invf, invf]).reshape(64, 1)
    A = lambda v: np.ascontiguousarray(np.asarray(v, f32))
    for l in range(2):
        com = {"gb": _rep(A(mix_pre_norm[l])), "win": A(w_in[l]), "dft": dft, "invf": invf2, "wuq": A(w_uq[l]),
               "gq": np.ascontiguousarray(A(mla_q_norm[l]).reshape(6, 128).T), "wukv": A(w_ukv[l]),
               "gkv": np.ascontiguousarray(A(mla_kv_norm[l]).reshape(4, 128).T), "gsg": _rep(A(sg_norm[l])),
               "wsT": np.ascontiguousarray(A(w_spatial[l]).transpose(0, 2, 1)), "bsb": _rep(A(b_spatial[l]))}
        maps = []
        for c in range(NCORES):
            sl = slice(c * TOK, (c + 1) * TOK)
            m = dict(com)
            m["x"] = np.ascontiguousarray(xs[sl])
            m["pos"] = _rep(pos[sl], 64)
            maps.append(m)
        r1 = _run(_prog("l1", prog_l1), maps)
        ab = np.stack([r1[c]["ab"] for c in range(NCORES)])
        abr = ab.reshape(NCORES, 2, 8, 128, 16, 128)
        kT_all = np.ascontiguousarray(np.concatenate([r1[c]["kT"] for c in range(NCORES)], axis=2))
        krT_all = np.ascontiguousarray(np.concatenate([r1[c]["krT"] for c in range(NCORES)], axis=1))
        v_all = np.ascontiguousarray(np.concatenate([r1[c]["v"] for c in range(NCORES)], axis=0))
        maps = []
        for j in range(NCORES):
            mm = abr[:, :, j].transpose(0, 3, 1, 2, 4).reshape(128, 2, 128, 128)
            maps.append({"m": np.ascontiguousarray(mm), "cs128": cs128, "tw": tw, "qT": r1[j]["qT"], "kT": kT_all,
                         "krT": krT_all, "v": v_all})
        r2 = _run(_prog("l2", prog_l2), maps)
        fo = np.stack([r2[j]["fo"] for j in range(NCORES)])
        fall = fo.transpose(1, 3, 0, 2).reshape(SEQ, 1024)
        com = {"gb": _rep(A(mix_pre_norm[l])), "wg": A(w_gate[l]),
               "bgT": np.ascontiguousarray(A(b_gate[l]).reshape(96, 128).T), "wf": A(w_br_f[l]), "wa": A(w_br_a[l]),
               "ws": A(w_br_s[l]), "wo": A(w_out[l]), "gp1": _rep(A(mix_post_norm[l])),
               "gb2": _rep(A(mem_pre_norm[l])), "mem": memv, "gmkv": _rep(A(mem_kv_norm[l])), "wmq": A(w_mq[l]),
               "wmk": A(w_mk[l]), "wmv": A(w_mv[l]), "wmo": A(w_mo[l]), "gp2": _rep(A(mem_post_norm[l]))}
        maps = []
        for c in range(NCORES):
            sl = slice(c * TOK, (c + 1) * TOK)
            m = dict(com)
            m["x"] = np.ascontiguousarray(xs[sl])
            m["fT"] = np.ascontiguousarray(fall[sl].T)
            m["aT"] = r2[c]["aT"]
            m["sgT"] = r1[c]["sgT"]
            maps.append(m)
        r3 = _run(_prog("l3", prog_l3), maps)
        x2 = np.concatenate([r3[c]["x2"] for c in range(NCORES)], axis=0)
        cwp = np.concatenate([A(conv_w[l]), A(conv_b[l])[None]], axis=0)
        cwp = np.ascontiguousarray(cwp.reshape(4, 172, 128).transpose(2, 1, 0))
        com = {"gb": _rep(A(ffn_pre_norm[l])), "wup": A(w_up[l]), "cw": cwp, "wdn": A(w_down[l]),
               "gp": _rep(A(ffn_post_norm[l]))}
        zero = np.zeros(D, f32)
        maps = []
        for c in range(NCORES):
            sl = slice(c * TOK, (c + 1) * TOK)
            m = dict(com)
            m["x"] = np.ascontiguousarray(x2[sl])
            prev = x2[c * TOK - 1] if c > 0 else zero
            nxt = x2[(c + 1) * TOK] if c < NCORES - 1 else zero
            m["xh"] = np.ascontiguousarray(np.stack([prev, nxt]))
            maps.append(m)
        r4 = _run(_prog("l4", prog_l4), maps)
        xs = np.concatenate([r4[c]["x3"] for c in range(NCORES)], axis=0)
    return np.ascontiguousarray(xs[None].astype(f32))
```

```python
import math
from contextlib import ExitStack
import numpy as np
import ml_dtypes
import concourse.bass as bass
import concourse.mybir as mybir
from concourse.bass_utils import run_bass_kernel_spmd

F32 = mybir.dt.float32
BF16 = mybir.dt.bfloat16
I32 = mybir.dt.int32
AF = mybir.ActivationFunctionType
ALU = mybir.AluOpType
AX = mybir.AxisListType

NCORES = 8
SEQ = 16384
TOK = SEQ // NCORES
D = 4096
EPS = 1e-6
NSLOT = 6


class Buf:
    __slots__ = ("wr", "rd")

    def __init__(self):
        self.wr = {}
        self.rd = {}


class Op:
    __slots__ = ("eng", "fn", "reads", "writes", "wpart", "dma", "deps", "marked", "semval", "slot", "key", "xdeps")

    def __init__(self, eng, fn, reads, writes, wpart, dma):
        self.eng = eng
        self.fn = fn
        self.reads = reads
        self.writes = writes
        self.wpart = wpart
        self.dma = dma
        self.deps = ()
        self.marked = False
        self.semval = None
        self.slot = None
        self.key = None
        self.xdeps = ()


class Prog:
    def __init__(self, nc):
        self.nc = nc
        self.ops = []
        self.dma_count = {}

    def op(self, eng, fn, reads=(), writes=(), wpart=(), dma=False):
        o = Op(eng, fn, tuple(reads), tuple(writes), tuple(wpart), dma)
        if dma:
            k = self.dma_count.get(eng, 0)
            self.dma_count[eng] = k + 1
            o.slot = k % NSLOT
            o.semval = 16 * (k // NSLOT + 1)
            o.key = (eng, o.slot)
            o.marked = True
        else:
            o.key = eng
        self.ops.append(o)

    def dma(self, q, out, in_, reads=(), writes=(), wpart=(), **kw):
        self.op(q, lambda e: e.dma_start(out=out, in_=in_, **kw), reads, writes, wpart, dma=True)

    def barrier(self):
        last = {}
        for i, o in enumerate(self.ops):
            last[o.key] = i
        engs = sorted(set(o.eng for o in self.ops))
        tgt = tuple(last.values())
        for e in engs:
            self.op(e, lambda eo: eo.nop())
            self.ops[-1].xdeps = tgt

    def emit(self):
        nc = self.nc
        ops = self.ops
        last_on_slot = {}
        for i, o in enumerate(ops):
            deps = set(o.xdeps)
            for b in o.reads:
                deps.update(b.wr.values())
            for b in o.writes:
                deps.update(b.wr.values())
                deps.update(b.rd.values())
            for b in o.wpart:
                deps.update(b.rd.values())
            for b in o.writes:
                b.wr = {o.key: i}
                b.rd = {}
            for b in o.wpart:
                if b.rd:
                    b.wr = {o.key: i}
                    b.rd = {}
                else:
                    b.wr[o.key] = i
            for b in o.reads:
                if b.wr.get(o.key) != i:
                    b.rd[o.key] = i
            if o.dma:
                prev = last_on_slot.get(o.key)
                if prev is not None:
                    deps.add(prev)
                last_on_slot[o.key] = i
            deps.discard(i)
            fd = []
            for d in deps:
                od = ops[d]
                if od.eng == "pe" and o.eng == "pe" and not od.dma and not o.dma:
                    continue
                fd.append(d)
            o.deps = fd
            for d in fd:
                ops[d].marked = True
        cnt = {}
        for o in ops:
            if o.dma:
                continue
            if o.marked:
                cnt[o.eng] = cnt.get(o.eng, 0) + 1
                o.semval = cnt[o.eng]
        engs = sorted(set(o.eng for o in ops))
        ctxs = []
        sems = {}
        for e in engs:
            if cnt.get(e, 0) > 0:
                cm = nc.semaphore("s_" + e)
                sems[e] = cm.__enter__()
                ctxs.append(cm)
            for s in range(min(NSLOT, self.dma_count.get(e, 0))):
                cm = nc.semaphore("d_%s_%d" % (e, s))
                sems[(e, s)] = cm.__enter__()
                ctxs.append(cm)
        per_eng = {e: [] for e in engs}
        for i, o in enumerate(ops):
            per_eng[o.eng].append(i)

        def run_engine(ename, eobj):
            seen = {}
            for i in per_eng.get(ename, ()):
                o = ops[i]
                need = {}
                for d in o.deps:
                    od = ops[d]
                    v = od.semval
                    if need.get(od.key, 0) < v:
                        need[od.key] = v
                for key, v in need.items():
                    if seen.get(key, 0) >= v:
                        continue
                    eobj.wait_ge(sems[key], v)
                    seen[key] = v
                ins = o.fn(eobj)
                if o.marked:
                    ins.then_inc(sems[o.key], 16 if o.dma else 1)

        with nc.Block() as block:
            if "sp" in per_eng:
                @block.sync
                def _(e):
                    run_engine("sp", e)
            if "act" in per_eng:
                @block.scalar
                def _(e):
                    run_engine("act", e)
            if "dve" in per_eng:
                @block.vector
                def _(e):
                    run_engine("dve", e)
            if "pool" in per_eng:
                @block.gpsimd
                def _(e):
                    run_engine("pool", e)
            if "pe" in per_eng:
                @block.tensor
                def _(e):
                    run_engine("pe", e)
        for cm in reversed(ctxs):
            cm.__exit__(None, None, None)
        return len(ops)


class Ctx:
    def __init__(self):
        self.nc = bass.Bass("TRN2", target_bir_lowering=False)
        self.p = Prog(self.nc)
        self.es = None
        self.n = 0
        self.outs = []
        self.rr = 0

    def din(self, name, shape, dt):
        return self.nc.dram_tensor(name, list(shape), dt, kind="ExternalInput").ap(), Buf()

    def dout(self, name, shape, dt):
        b = Buf()
        self.outs.append(b)
        return self.nc.dram_tensor(name, list(shape), dt, kind="ExternalOutput").ap(), b

    def dint(self, name, shape, dt):
        return self.nc.dram_tensor(name, list(shape), dt, kind="Internal").ap(), Buf()

    def sb(self, shape, dt):
        self.n += 1
        return self.es.enter_context(self.nc.sbuf_tensor("sb%d" % self.n, list(shape), dt))

    def ps(self, shape, dt):
        self.n += 1
        return self.es.enter_context(self.nc.psum_tensor("ps%d" % self.n, list(shape), dt))

    def phase(self):
        c = self

        class _Ph:
            def __enter__(self_):
                self_.es = ExitStack()
                self_.es.__enter__()
                c.es = self_.es
                return c

            def __exit__(self_, *a):
                c.p.barrier()
                c.es = None
                return self_.es.__exit__(*a)
        return _Ph()

    def finish(self):
        self.p.op("sp", lambda e: e.nop(), reads=list(self.outs))
        return self.p.emit()

    def ident(self):
        p = self.p
        idf = self.sb([128, 128], F32)
        idb = self.sb([128, 128], BF16)
        b = Buf()
        p.op("pool", lambda e: e.memset(idf[:], 0.0), writes=[b])
        p.op("pool", lambda e: e.affine_select(out=idf[:], in_=idf[:], pattern=[[-1, 128]],
                                               compare_op=ALU.not_equal, fill=1.0, base=0,
                                               channel_multiplier=1), reads=[b], writes=[b])
        p.op("dve", lambda e: e.tensor_copy(out=idb[:], in_=idf[:]), reads=[b], writes=[b])
        return idb, b


def rstd_ops(p, eng, out, in_, n, rb, wb):
    p.op(eng, lambda e: e.tensor_scalar(out=out, in0=in_, scalar1=1.0 / n, scalar2=EPS,
                                        op0=ALU.mult, op1=ALU.add), reads=rb, writes=wb)
    p.op("act", lambda e: e.activation(out=out, in_=out, func=AF.Sqrt), reads=wb, writes=wb)
    p.op(eng, lambda e: e.reciprocal(out=out, in_=out), reads=wb, writes=wb)


def ph_head(c, x_d, bx, gb_d, bg, hT_d, bh, ntok, col0=0, halo=None):
    p = c.p
    if True:
        idb, b_id = c.ident()
        gb = c.sb([128, D], F32)
        b_gb = Buf()
        p.dma("act", gb[:], gb_d, reads=[bg], writes=[b_gb])
        xt = [c.sb([128, D], F32) for _ in range(2)]
        b_xt = [Buf(), Buf()]
        xb = [c.sb([128, D], BF16) for _ in range(2)]
        b_xb = [Buf(), Buf()]
        ss = [c.sb([128, 1], F32) for _ in range(2)]
        b_ss = [Buf(), Buf()]
        hTt = [c.sb([128, 32, 512], BF16) for _ in range(2)]
        b_hTt = [Buf(), Buf()]
        pt = [c.ps([128, 8, 128], BF16) for _ in range(4)]
        b_pt = [Buf() for _ in range(4)]
        ntile = ntok // 128
        jobs = [(t * 128, 128, None) for t in range(ntile)]
        if halo is not None:
            jobs.append((0, 2, halo))
        ptc = 0
        for ji, (r0, nr, hal) in enumerate(jobs):
            s = ji % 2
            if hal is None:
                p.dma("sp", xt[s][:nr, :], x_d[r0:r0 + nr, :], reads=[bx], writes=[b_xt[s]])
            else:
                p.dma("sp", xt[s][:nr, :], hal[0][0:nr, :], reads=[hal[1]], writes=[b_xt[s]])
            p.op("act", lambda e, s=s, nr=nr: e.activation(out=xb[s][:nr, :], in_=xt[s][:nr, :], func=AF.Square,
                                                           accum_out=ss[s][:nr, :]),
                 reads=[b_xt[s]], writes=[b_xb[s], b_ss[s]])
            rstd_ops(p, "dve", ss[s][:nr, :], ss[s][:nr, :], D, [b_ss[s]], [b_ss[s]])
            p.op("dve", lambda e, s=s, nr=nr: e.scalar_tensor_tensor(out=xb[s][:nr, :], in0=xt[s][:nr, :],
                                                                     scalar=ss[s][:nr, 0:1], in1=gb[:nr, :],
                                                                     op0=ALU.mult, op1=ALU.mult),
                 reads=[b_xt[s], b_ss[s], b_gb], writes=[b_xb[s]])
            hs = (ji // 4) % 2
            tcol = (ji % 4) * 128
            for q in range(4):
                pi = ptc % 4
                ptc += 1
                for cc in range(8):
                    ch = q * 8 + cc
                    p.op("pe", lambda e, s=s, nr=nr, ch=ch, cc=cc, pi=pi: e.transpose(
                        out=pt[pi][:, cc, :nr], in_=xb[s][:nr, ch * 128:(ch + 1) * 128], identity=idb[:nr, :nr]),
                        reads=[b_xb[s], b_id], writes=[b_pt[pi]])
                eng = "act" if q % 2 == 0 else "dve"
                if eng == "act":
                    p.op("act", lambda e, hs=hs, q=q, pi=pi, nr=nr, tcol=tcol: e.activation(
                        out=hTt[hs][:, q * 8:(q + 1) * 8, tcol:tcol + nr], in_=pt[pi][:, :, :nr], func=AF.Copy),
                        reads=[b_pt[pi]], wpart=[b_hTt[hs]])
                else:
                    p.op("dve", lambda e, hs=hs, q=q, pi=pi, nr=nr, tcol=tcol: e.tensor_copy(
                        out=hTt[hs][:, q * 8:(q + 1) * 8, tcol:tcol + nr], in_=pt[pi][:, :, :nr]),
                        reads=[b_pt[pi]], wpart=[b_hTt[hs]])
            if hal is not None:
                p.dma("sp", hT_d[:, :, ntok:ntok + 2].rearrange("c p t -> p c t"), hTt[hs][:, :, tcol:tcol + 2],
                      reads=[b_hTt[hs]], wpart=[bh])
            elif ji % 4 == 3 or ji == ntile - 1:
                g0 = col0 + (ji // 4) * 512
                wdt = (ji % 4 + 1) * 128
                p.dma("sp", hT_d[:, :, g0:g0 + wdt].rearrange("c p t -> p c t"), hTt[hs][:, :, :wdt],
                      reads=[b_hTt[hs]], wpart=[bh])


def ph_lin(c, aT_d, ba, KC, W_d, bw, y_d, by, ssq_d, bs, ntok, acol0=0):
    p = c.p
    KS = 8
    nks = (KC + KS - 1) // KS
    T = 1024 if (KC <= 32 and ntok % 1024 == 0) else 512
    NTL = T // 128
    if True:
        aT = c.sb([128, KC, T], BF16)
        b_aT = Buf()
        wb = [c.sb([128, KS, 512], BF16) for _ in range(3)]
        b_wb = [Buf() for _ in range(3)]
        acc = [c.ps([128, 512], F32) for _ in range(8)]
        b_acc = [Buf() for _ in range(8)]
        ysb = [c.sb([128, 512], F32) for _ in range(4)]
        b_ysb = [Buf() for _ in range(4)]
        junk = c.sb([128, 512], BF16)
        b_junk = Buf()
        ssq = [c.sb([128, 8], F32) for _ in range(NTL)]
        b_ssq = [Buf() for _ in range(NTL)]
        wi = 0
        yi = 0
        for st in range(ntok // T):
            t0 = st * T
            p.dma("sp", aT[:], aT_d[:, :, acol0 + t0:acol0 + t0 + T].rearrange("c p t -> p c t"),
                  reads=[ba], writes=[b_aT])
            for mb in range(8):
                par = (mb % 2) * 4 if NTL == 4 else 0
                for ks in range(nks):
                    k0 = ks * KS
                    kn = min(KS, KC - k0)
                    w = wi % 3
                    wi += 1
                    p.dma("pool", wb[w][:, :kn, :],
                          W_d[k0 * 128:(k0 + kn) * 128, mb * 512:(mb + 1) * 512].rearrange("(c p) n -> p c n", p=128),
                          reads=[bw], writes=[b_wb[w]])
                    for tl in range(NTL):
                        for cc in range(kn):
                            kc = k0 + cc
                            p.op("pe", lambda e, tl=tl, cc=cc, kc=kc, w=w, par=par: e.matmul(
                                acc[par + tl][:], lhsT=aT[:, kc, tl * 128:(tl + 1) * 128], rhs=wb[w][:, cc, :],
                                start=(kc == 0), stop=(kc == KC - 1)),
                                reads=[b_aT, b_wb[w]], writes=[b_acc[par + tl]])
                for tl in range(NTL):
                    y = yi % 4
                    yi += 1
                    p.op("act", lambda e, y=y, a=par + tl: e.activation(out=ysb[y][:], in_=acc[a][:], func=AF.Copy),
                         reads=[b_acc[par + tl]], writes=[b_ysb[y]])
                    p.op("act", lambda e, y=y, tl=tl, mb=mb: e.activation(out=junk[:], in_=ysb[y][:], func=AF.Square,
                                                                          accum_out=ssq[tl][:, mb:mb + 1]),
                         reads=[b_ysb[y]], writes=[b_junk], wpart=[b_ssq[tl]])
                    r0 = t0 + tl * 128
                    p.dma("sp", y_d[r0:r0 + 128, mb * 512:(mb + 1) * 512], ysb[y][:], reads=[b_ysb[y]], wpart=[by])
            for tl in range(NTL):
                r0 = t0 + tl * 128
                p.dma("sp", ssq_d[r0:r0 + 128, :], ssq[tl][:], reads=[b_ssq[tl]], wpart=[bs])


def ph_nr(c, y_d, by, ssq_d, bs, gb_d, bg, xi_d, bxi, xo_d, bxo, ntok):
    p = c.p
    if True:
        gb = c.sb([128, D], F32)
        b_gb = Buf()
        p.dma("act", gb[:], gb_d, reads=[bg], writes=[b_gb])
        yt = [c.sb([128, D], F32) for _ in range(2)]
        xt = [c.sb([128, D], F32) for _ in range(2)]
        sq = [c.sb([128, 8], F32) for _ in range(2)]
        rs = [c.sb([128, 1], F32) for _ in range(2)]
        b_yt = [Buf(), Buf()]
        b_xt = [Buf(), Buf()]
        b_sq = [Buf(), Buf()]
        b_rs = [Buf(), Buf()]
        for t in range(ntok // 128):
            s = t % 2
            r0 = t * 128
            p.dma("sp", yt[s][:], y_d[r0:r0 + 128, :], reads=[by], writes=[b_yt[s]])
            p.dma("act", xt[s][:], xi_d[r0:r0 + 128, :], reads=[bxi], writes=[b_xt[s]])
            p.dma("sp", sq[s][:], ssq_d[r0:r0 + 128, :], reads=[bs], writes=[b_sq[s]])
            p.op("dve", lambda e, s=s: e.reduce_sum(out=rs[s][:], in_=sq[s][:], axis=AX.X),
                 reads=[b_sq[s]], writes=[b_rs[s]])
            rstd_ops(p, "dve", rs[s][:], rs[s][:], D, [b_rs[s]], [b_rs[s]])
            p.op("dve", lambda e, s=s: e.scalar_tensor_tensor(out=yt[s][:], in0=yt[s][:], scalar=rs[s][:, 0:1],
                                                              in1=gb[:], op0=ALU.mult, op1=ALU.mult),
                 reads=[b_yt[s], b_rs[s], b_gb], writes=[b_yt[s]])
            p.op("pool", lambda e, s=s: e.tensor_tensor(out=yt[s][:], in0=yt[s][:], in1=xt[s][:], op=ALU.add),
                 reads=[b_yt[s], b_xt[s]], writes=[b_yt[s]])
            p.dma("sp", xo_d[r0:r0 + 128, :], yt[s][:], reads=[b_yt[s]], wpart=[bxo])


class St:
    def __init__(self):
        self.wi = 0
        self.bi = 0
        self.k = 0


def evac(p, k, out, in_, rb, wb=(), wp=()):
    if k % 2 == 0:
        p.op("act", lambda e: e.activation(out=out, in_=in_, func=AF.Copy), reads=rb, writes=wb, wpart=wp)
    else:
        p.op("dve", lambda e: e.tensor_copy(out=out, in_=in_), reads=rb, writes=wb, wpart=wp)


def load_hT(c, hT, b_hT, hT_d, bh, t0, n, KC=32, q="sp"):
    c.p.dma(q, hT[:, :KC, :n], hT_d[:KC, :, t0:t0 + n].rearrange("c p t -> p c t"), reads=[bh], writes=[b_hT])


def fm_stream(c, blocks, KC, rhs_fn, b_rhs, tgs, epi, wbufs, b_wbufs, banks, b_banks, st):
    p = c.p
    for (W_d, bw, col0, ncols, tag) in blocks:
        w = st.wi % len(wbufs)
        st.wi += 1
        p.dma("pool", wbufs[w][:, :KC, :ncols],
              W_d[0:KC * 128, col0:col0 + ncols].rearrange("(c p) n -> p c n", p=128),
              reads=[bw], writes=[b_wbufs[w]])
        for ch in range((ncols + 127) // 128):
            wd = min(128, ncols - ch * 128)
            for ti, (tc0, tn) in enumerate(tgs):
                bk = st.bi % len(banks)
                st.bi += 1
                for kc in range(KC):
                    p.op("pe", lambda e, bk=bk, w=w, kc=kc, ch=ch, wd=wd, tc0=tc0, tn=tn: e.matmul(
                        banks[bk][:wd, :tn], lhsT=wbufs[w][:, kc, ch * 128:ch * 128 + wd], rhs=rhs_fn(kc, tc0, tn),
                        start=(kc == 0), stop=(kc == KC - 1)),
                        reads=[b_wbufs[w], b_rhs], writes=[b_banks[bk]])
                epi(tag, ch, ti, banks[bk], b_banks[bk], wd)


def ones_bf(c, shape):
    t = c.sb(shape, BF16)
    b = Buf()
    c.p.op("pool", lambda e: e.memset(t[:], 1.0), writes=[b])
    return t, b


def fm_rmsnorm(c, raw, b_raw, nch, nfeat, tn, gcol, b_g, ssb, b_ssb, rs, b_rs, outT, b_out):
    p = c.p
    p.op("dve", lambda e: e.tensor_scalar(out=rs[:, :tn], in0=ssb[:, :tn], scalar1=1.0 / nfeat, scalar2=EPS,
                                          op0=ALU.mult, op1=ALU.add), reads=[b_ssb], writes=[b_rs])
    p.op("act", lambda e: e.activation(out=rs[:, :tn], in_=rs[:, :tn], func=AF.Sqrt), reads=[b_rs], writes=[b_rs])
    p.op("dve", lambda e: e.reciprocal(out=rs[:, :tn], in_=rs[:, :tn]), reads=[b_rs], writes=[b_rs])
    for ch in range(nch):
        p.op("dve", lambda e, ch=ch: e.scalar_tensor_tensor(out=outT[:, ch, :tn], in0=raw[:, ch, :tn],
                                                            scalar=gcol[:, ch:ch + 1], in1=rs[:, :tn],
                                                            op0=ALU.mult, op1=ALU.mult),
             reads=[b_raw, b_g, b_rs], wpart=[b_out])


def ph_rope(c, pos_d, bp, invf_d, bi, cs_d, bcs, ntok):
    p = c.p
    pi_ = c.sb([64, ntok], I32)
    pf = c.sb([64, ntok], F32)
    t1 = c.sb([64, ntok], F32)
    t2 = c.sb([64, ntok], F32)
    ki = c.sb([64, ntok], I32)
    iv = c.sb([64, 1], F32)
    b_pi, b_pf, b_t1, b_t2, b_ki, b_iv = Buf(), Buf(), Buf(), Buf(), Buf(), Buf()
    p.dma("sp", pi_[:], pos_d, reads=[bp], writes=[b_pi])
    p.dma("sp", iv[:], invf_d, reads=[bi], writes=[b_iv])
    p.op("dve", lambda e: e.tensor_copy(out=pf[:], in_=pi_[:]), reads=[b_pi], writes=[b_pf])
    p.op("dve", lambda e: e.tensor_scalar(out=pf[:], in0=pf[:], scalar1=iv[:, 0:1], scalar2=1.0 / (2 * math.pi),
                                          op0=ALU.mult, op1=ALU.mult), reads=[b_pf, b_iv], writes=[b_pf])
    for k, sh in enumerate((0.25, 0.0)):
        p.op("dve", lambda e, sh=sh: e.tensor_scalar(out=t1[:], in0=pf[:], scalar1=sh, scalar2=None, op0=ALU.add),
             reads=[b_pf], writes=[b_t1])
        p.op("dve", lambda e: e.tensor_copy(out=ki[:], in_=t1[:]), reads=[b_t1], writes=[b_ki])
        p.op("dve", lambda e: e.tensor_copy(out=t2[:], in_=ki[:]), reads=[b_ki], writes=[b_t2])
        p.op("dve", lambda e: e.tensor_tensor(out=t1[:], in0=t1[:], in1=t2[:], op=ALU.subtract),
             reads=[b_t1, b_t2], writes=[b_t1])
        p.op("dve", lambda e: e.tensor_scalar(out=t2[:], in0=t1[:], scalar1=0.5, scalar2=None, op0=ALU.is_gt),
             reads=[b_t1], writes=[b_t2])
        p.op("dve", lambda e: e.tensor_tensor(out=t1[:], in0=t1[:], in1=t2[:], op=ALU.subtract),
             reads=[b_t1, b_t2], writes=[b_t1])
        p.op("dve", lambda e: e.tensor_scalar(out=t2[:], in0=t1[:], scalar1=-0.5, scalar2=None, op0=ALU.is_lt),
             reads=[b_t1], writes=[b_t2])
        p.op("dve", lambda e: e.tensor_tensor(out=t1[:], in0=t1[:], in1=t2[:], op=ALU.add),
             reads=[b_t1, b_t2], writes=[b_t1])
        p.op("act", lambda e: e.activation(out=t1[:], in_=t1[:], func=AF.Sin, scale=2 * math.pi),
             reads=[b_t1], writes=[b_t1])
        p.dma("sp", cs_d[k], t1[:], reads=[b_t1], wpart=[bcs])


def ph_f(c, hT_d, bh, win_d, bw, dft_d, bd, ab_d, bab, ntok):
    p = c.p
    T = 512
    hT = c.sb([128, 32, T], BF16)
    b_hT = Buf()
    wbufs = [c.sb([128, 32, 512], BF16) for _ in range(2)]
    b_wbufs = [Buf(), Buf()]
    banks = [c.ps([128, 512], F32) for _ in range(6)]
    b_banks = [Buf() for _ in range(6)]
    tab = c.sb([128, 2, 2, 256], BF16)
    b_tab = Buf()
    p.dma("pool", tab[:], dft_d.rearrange("a (c p) j -> p a c j", p=128), reads=[bd], writes=[b_tab])
    zf = c.sb([128, 8, T], BF16)
    b_zf = Buf()
    stg = [c.sb([128, T], BF16) for _ in range(3)]
    b_stg = [Buf() for _ in range(3)]
    st = St()
    for s in range(ntok // T):
        t0 = s * T
        load_hT(c, hT, b_hT, hT_d, bh, t0, T)

        def epi(tag, ch, ti, bank, b_bank, wd):
            fch = tag * 4 + ch
            st.k += 1
            evac(p, st.k, zf[:, fch, :], bank[:, :], [b_bank], wp=[b_zf])

        fm_stream(c, [(win_d, bw, 0, 512, 0), (win_d, bw, 512, 512, 1)], 32,
                  lambda kc, tc0, tn: hT[:, kc, tc0:tc0 + tn], b_hT, [(0, T)], epi, wbufs, b_wbufs, banks, b_banks, st)
        for g in range(4):
            for jc in range(2):
                for ab in range(2):
                    bk = st.bi % 6
                    st.bi += 1
                    for cc in range(2):
                        p.op("pe", lambda e, bk=bk, ab=ab, cc=cc, jc=jc, g=g: e.matmul(
                            banks[bk][:, :], lhsT=tab[:, ab, cc, jc * 128:(jc + 1) * 128], rhs=zf[:, 2 * g + cc, :],
                            start=(cc == 0), stop=(cc == 1)), reads=[b_tab, b_zf], writes=[b_banks[bk]])
                    si = st.k % 3
                    st.k += 1
                    evac(p, st.k, stg[si][:], banks[bk][:, :], [b_banks[bk]], wb=[b_stg[si]])
                    r0 = g * 256 + jc * 128
                    p.dma("sp", ab_d[ab, r0:r0 + 128, t0:t0 + T], stg[si][:], reads=[b_stg[si]], wpart=[bab])


def ph_fft(c, m_d, bm, cs_d, bcs, tw_d, btw, fo_d, bfo):
    p = c.p
    M = c.sb([128, 2, 128, 128], BF16)
    b_M = Buf()
    p.dma("sp", M[:], m_d, reads=[bm], writes=[b_M])
    csf = c.sb([128, 2, 128], F32)
    tw = c.sb([128, 2, 128], F32)
    b_csf, b_tw = Buf(), Buf()
    p.dma("act", csf[:], cs_d.rearrange("a p k -> p a k"), reads=[bcs], writes=[b_csf])
    p.dma("act", tw[:], tw_d.rearrange("a p k -> p a k"), reads=[btw], writes=[b_tw])
    r1 = c.sb([128, 256], BF16)
    r2 = c.sb([128, 256], BF16)
    b_r = Buf()
    p.op("dve", lambda e: e.tensor_copy(out=r1[:, 0:128], in_=csf[:, 0, :]), reads=[b_csf], wpart=[b_r])
    p.op("dve", lambda e: e.tensor_scalar(out=r1[:, 128:256], in0=csf[:, 1, :], scalar1=-1.0, scalar2=None,
                                          op0=ALU.mult), reads=[b_csf], wpart=[b_r])
    p.op("dve", lambda e: e.tensor_copy(out=r2[:, 0:128], in_=csf[:, 1, :]), reads=[b_csf], wpart=[b_r])
    p.op("dve", lambda e: e.tensor_copy(out=r2[:, 128:256], in_=csf[:, 0, :]), reads=[b_csf], wpart=[b_r])
    Yr = c.sb([128, 128, 128], BF16)
    Yi = c.sb([128, 128, 128], BF16)
    b_Y = Buf()
    banks = [c.ps([128, 2, 2, 128], F32) for _ in range(4)]
    b_banks = [Buf() for _ in range(4)]
    tmp = [c.sb([128, 2, 128], F32) for _ in range(4)]
    b_tmp = [Buf() for _ in range(4)]
    tcb = tw[:, 0, :].unsqueeze(1).to_broadcast([128, 2, 128])
    tsb = tw[:, 1, :].unsqueeze(1).to_broadcast([128, 2, 128])
    for cp in range(64):
        bk = cp % 4
        for j in range(2):
            ch = cp * 2 + j
            p.op("pe", lambda e, bk=bk, j=j, ch=ch: e.matmul(banks[bk][:, j, :, :], lhsT=M[:, 0, ch, :], rhs=r1[:],
                                                            start=True, stop=False),
                 reads=[b_M, b_r], writes=[b_banks[bk]])
            p.op("pe", lambda e, bk=bk, j=j, ch=ch: e.matmul(banks[bk][:, j, :, :], lhsT=M[:, 1, ch, :], rhs=r2[:],
                                                            start=False, stop=True),
                 reads=[b_M, b_r], writes=[b_banks[bk]])
        yr = banks[bk][:, :, 0, :]
        yi = banks[bk][:, :, 1, :]
        ch0 = cp * 2
        p.op("dve", lambda e, yr=yr: e.tensor_tensor(out=tmp[0][:], in0=yr, in1=tcb, op=ALU.mult),
             reads=[b_banks[bk], b_tw], writes=[b_tmp[0]])
        p.op("dve", lambda e, yi=yi: e.tensor_tensor(out=tmp[1][:], in0=yi, in1=tsb, op=ALU.mult),
             reads=[b_banks[bk], b_tw], writes=[b_tmp[1]])
        p.op("dve", lambda e, yi=yi: e.tensor_tensor(out=tmp[2][:], in0=yi, in1=tcb, op=ALU.mult),
             reads=[b_banks[bk], b_tw], writes=[b_tmp[2]])
        p.op("dve", lambda e, yr=yr: e.tensor_tensor(out=tmp[3][:], in0=yr, in1=tsb, op=ALU.mult),
             reads=[b_banks[bk], b_tw], writes=[b_tmp[3]])
        p.op("pool", lambda e, ch0=ch0: e.tensor_tensor(out=Yr[:, ch0:ch0 + 2, :], in0=tmp[0][:], in1=tmp[1][:],
                                                        op=ALU.add), reads=[b_tmp[0], b_tmp[1]], wpart=[b_Y])
        p.op("pool", lambda e, ch0=ch0: e.tensor_tensor(out=Yi[:, ch0:ch0 + 2, :], in0=tmp[2][:], in1=tmp[3][:],
                                                        op=ALU.subtract), reads=[b_tmp[2], b_tmp[3]], wpart=[b_Y])
    cc_ = c.sb([128, 128], BF16)
    ss_ = c.sb([128, 128], BF16)
    b_c2 = Buf()
    p.op("dve", lambda e: e.tensor_copy(out=cc_[:], in_=csf[:, 0, :]), reads=[b_csf], wpart=[b_c2])
    p.op("dve", lambda e: e.tensor_copy(out=ss_[:], in_=csf[:, 1, :]), reads=[b_csf], wpart=[b_c2])
    fo = c.sb([128, 128, 128], BF16)
    b_fo = Buf()
    for q in range(32):
        bk = q % 4
        bnk = banks[bk][:].rearrange("p a b k -> p (a b) k")
        p.op("pe", lambda e, bnk=bnk, q=q: e.matmul(bnk, lhsT=cc_[:], rhs=Yr[:, q * 4:(q + 1) * 4, :],
                                                    start=True, stop=False), reads=[b_c2, b_Y], writes=[b_banks[bk]])
        p.op("pe", lambda e, bnk=bnk, q=q: e.matmul(bnk, lhsT=ss_[:], rhs=Yi[:, q * 4:(q + 1) * 4, :],
                                                    start=False, stop=True), reads=[b_c2, b_Y], writes=[b_banks[bk]])
        p.op("act", lambda e, bnk=bnk, q=q: e.activation(out=fo[:, q * 4:(q + 1) * 4, :], in_=bnk, func=AF.Copy,
                                                         scale=1.0 / 2048.0), reads=[b_banks[bk]], wpart=[b_fo])
    p.dma("sp", fo_d, fo[:], reads=[b_fo], writes=[bfo])


def rope_weights(c, w4, b_w4, nk, nh, hd, r0):
    p = c.p
    wr = c.sb([128, nk, nh, 64], BF16)
    b_wr = Buf()
    p.op("dve", lambda e: e.tensor_scalar(out=wr[:, :, :, 0:32], in0=w4[:, :, :, r0 + 32:r0 + 64], scalar1=-1.0,
                                          scalar2=None, op0=ALU.mult), reads=[b_w4], wpart=[b_wr])
    p.op("dve", lambda e: e.tensor_copy(out=wr[:, :, :, 32:64], in_=w4[:, :, :, r0:r0 + 32]),
         reads=[b_w4], wpart=[b_wr])
    return wr, b_wr


def rope_combine(c, br, b_br, brot, b_brot, cs, b_cs, tc0, tn, t1, b_t1, t2, b_t2, out, b_out_w):
    p = c.p
    p.op("dve", lambda e: e.tensor_tensor(out=t1[:64, :tn], in0=br[:64, :tn], in1=cs[:, 0, tc0:tc0 + tn], op=ALU.mult),
         reads=[b_br, b_cs], writes=[b_t1])
    p.op("dve", lambda e: e.tensor_tensor(out=t2[:64, :tn], in0=brot[:64, :tn], in1=cs[:, 1, tc0:tc0 + tn], op=ALU.mult),
         reads=[b_brot, b_cs], writes=[b_t2])
    p.op("dve", lambda e: e.tensor_tensor(out=out, in0=t1[:64, :tn], in1=t2[:64, :tn], op=ALU.add),
         reads=[b_t1, b_t2], writes=b_out_w)


def ph_q(c, hT_d, bh, win_d, bw, wuq_d, bwuq, gq_d, bgq, cs_d, bcs, qT_d, bq, ntok):
    p = c.p
    T = 512
    hT = c.sb([128, 32, T], BF16)
    b_hT = Buf()
    wbufs = [c.sb([128, 32, 512], BF16) for _ in range(2)]
    b_wbufs = [Buf(), Buf()]
    banks = [c.ps([128, 512], F32) for _ in range(6)]
    b_banks = [Buf() for _ in range(6)]
    ssb = c.ps([128, 512], F32)
    b_ssb = Buf()
    wuq = c.sb([128, 6, 8, 192], BF16)
    b_wuq = Buf()
    p.dma("pool", wuq[:], wuq_d.rearrange("(c p) (h d) -> p c h d", p=128, d=192), reads=[bwuq], writes=[b_wuq])
    wr, b_wr = rope_weights(c, wuq, b_wuq, 6, 8, 192, 128)
    gq = c.sb([128, 6], F32)
    b_gq = Buf()
    p.dma("act", gq[:], gq_d, reads=[bgq], writes=[b_gq])
    cs = c.sb([64, 2, ntok], F32)
    b_cs = Buf()
    p.dma("act", cs[:], cs_d.rearrange("a p t -> p a t"), reads=[bcs], writes=[b_cs])
    ones, b_ones = ones_bf(c, [128, 128])
    cq = c.sb([128, 6, T], BF16)
    cqn = c.sb([128, 6, T], BF16)
    b_cq, b_cqn = Buf(), Buf()
    sq = [c.sb([128, T], BF16) for _ in range(2)]
    b_sq = [Buf(), Buf()]
    rs = c.sb([128, T], F32)
    b_rs = Buf()
    stg = [c.sb([128, T], BF16) for _ in range(3)]
    b_stg = [Buf() for _ in range(3)]
    t1 = c.sb([64, T], F32)
    t2 = c.sb([64, T], F32)
    b_t1, b_t2 = Buf(), Buf()
    st = St()
    for s in range(ntok // T):
        t0 = s * T
        load_hT(c, hT, b_hT, hT_d, bh, t0, T)

        def epi(tag, ch, ti, bank, b_bank, wd):
            qc = tag * 4 + ch
            p.op("act", lambda e: e.activation(out=cq[:, qc, :], in_=bank[:, :], func=AF.Copy),
                 reads=[b_bank], wpart=[b_cq])
            si = qc % 2
            p.op("act", lambda e: e.activation(out=sq[si][:], in_=bank[:, :], func=AF.Square),
                 reads=[b_bank], writes=[b_sq[si]])
            p.op("pe", lambda e: e.matmul(ssb[:, :], lhsT=ones[:], rhs=sq[si][:], start=(qc == 0), stop=(qc == 5)),
                 reads=[b_ones, b_sq[si]], writes=[b_ssb])

        fm_stream(c, [(win_d, bw, 1024, 512, 0), (win_d, bw, 1536, 256, 1)], 32,
                  lambda kc, tc0, tn: hT[:, kc, tc0:tc0 + tn], b_hT, [(0, T)], epi, wbufs, b_wbufs, banks, b_banks, st)
        fm_rmsnorm(c, cq, b_cq, 6, 768, T, gq, b_gq, ssb, b_ssb, rs, b_rs, cqn, b_cqn)
        for h in range(8):
            bk = st.bi % 6
            st.bi += 1
            for kc in range(6):
                p.op("pe", lambda e, bk=bk, kc=kc, h=h: e.matmul(banks[bk][:, :], lhsT=wuq[:, kc, h, 0:128],
                                                                 rhs=cqn[:, kc, :], start=(kc == 0), stop=(kc == 5)),
                     reads=[b_wuq, b_cqn], writes=[b_banks[bk]])
            si = st.k % 3
            st.k += 1
            evac(p, st.k, stg[si][:], banks[bk][:, :], [b_banks[bk]], wb=[b_stg[si]])
            p.dma("sp", qT_d[h, 0:128, t0:t0 + T], stg[si][:], reads=[b_stg[si]], wpart=[bq])
            bk1 = st.bi % 6
            bk2 = (st.bi + 1) % 6
            st.bi += 2
            for kc in range(6):
                p.op("pe", lambda e, bk1=bk1, kc=kc, h=h: e.matmul(banks[bk1][:64, :], lhsT=wuq[:, kc, h, 128:192],
                                                                   rhs=cqn[:, kc, :], start=(kc == 0), stop=(kc == 5)),
                     reads=[b_wuq, b_cqn], writes=[b_banks[bk1]])
            for kc in range(6):
                p.op("pe", lambda e, bk2=bk2, kc=kc, h=h: e.matmul(banks[bk2][:64, :], lhsT=wr[:, kc, h, :],
                                                                   rhs=cqn[:, kc, :], start=(kc == 0), stop=(kc == 5)),
                     reads=[b_wr, b_cqn], writes=[b_banks[bk2]])
            si = st.k % 3
            st.k += 1
            rope_combine(c, banks[bk1], b_banks[bk1], banks[bk2], b_banks[bk2], cs, b_cs, t0, T, t1, b_t1, t2, b_t2,
                         stg[si][:64, :], [b_stg[si]])
            p.dma("sp", qT_d[h, 128:192, t0:t0 + T], stg[si][:64, :], reads=[b_stg[si]], wpart=[bq])


def ph_kv(c, hT_d, bh, win_d, bw, wukv_d, bwukv, gkv_d, bgkv, cs_d, bcs, kT_d, bk_, krT_d, bkr, v_d, bv, ntok):
    p = c.p
    T = 512
    hT = c.sb([128, 32, T], BF16)
    b_hT = Buf()
    wbufs = [c.sb([128, 32, 512], BF16) for _ in range(2)]
    b_wbufs = [Buf(), Buf()]
    banks = [c.ps([128, 512], F32) for _ in range(6)]
    b_banks = [Buf() for _ in range(6)]
    ssb = c.ps([128, 512], F32)
    b_ssb = Buf()
    wukv = c.sb([128, 4, 8, 256], BF16)
    b_wukv = Buf()
    p.dma("pool", wukv[:], wukv_d.rearrange("(c p) (h d) -> p c h d", p=128, d=256), reads=[bwukv], writes=[b_wukv])
    wkr = c.sb([128, 32, 1, 64], BF16)
    b_wkr = Buf()
    p.dma("pool", wkr[:, :, 0, :], win_d[:, 2304:2368].rearrange("(c p) n -> p c n", p=128), reads=[bw], writes=[b_wkr])
    wkrr, b_wkrr = rope_weights(c, wkr, b_wkr, 32, 1, 64, 0)
    gkv = c.sb([128, 4], F32)
    b_gkv = Buf()
    p.dma("act", gkv[:], gkv_d, reads=[bgkv], writes=[b_gkv])
    cs = c.sb([64, 2, ntok], F32)
    b_cs = Buf()
    p.dma("act", cs[:], cs_d.rearrange("a p t -> p a t"), reads=[bcs], writes=[b_cs])
    ones, b_ones = ones_bf(c, [128, 128])
    ck = c.sb([128, 4, T], BF16)
    ckn = c.sb([128, 4, T], BF16)
    b_ck, b_ckn = Buf(), Buf()
    sq = [c.sb([128, T], BF16) for _ in range(2)]
    b_sq = [Buf(), Buf()]
    rs = c.sb([128, T], F32)
    b_rs = Buf()
    stg = [c.sb([128, T], BF16) for _ in range(3)]
    b_stg = [Buf() for _ in range(3)]
    t1 = c.sb([64, T], F32)
    t2 = c.sb([64, T], F32)
    b_t1, b_t2 = Buf(), Buf()
    st = St()
    for s in range(ntok // T):
        t0 = s * T
        load_hT(c, hT, b_hT, hT_d, bh, t0, T)

        def epi(tag, ch, ti, bank, b_bank, wd):
            qc = ch
            p.op("act", lambda e: e.activation(out=ck[:, qc, :], in_=bank[:, :], func=AF.Copy),
                 reads=[b_bank], wpart=[b_ck])
            si = qc % 2
            p.op("act", lambda e: e.activation(out=sq[si][:], in_=bank[:, :], func=AF.Square),
                 reads=[b_bank], writes=[b_sq[si]])
            p.op("pe", lambda e: e.matmul(ssb[:, :], lhsT=ones[:], rhs=sq[si][:], start=(qc == 0), stop=(qc == 3)),
                 reads=[b_ones, b_sq[si]], writes=[b_ssb])

        fm_stream(c, [(win_d, bw, 1792, 512, 0)], 32,
                  lambda kc, tc0, tn: hT[:, kc, tc0:tc0 + tn], b_hT, [(0, T)], epi, wbufs, b_wbufs, banks, b_banks, st)
        fm_rmsnorm(c, ck, b_ck, 4, 512, T, gkv, b_gkv, ssb, b_ssb, rs, b_rs, ckn, b_ckn)
        for h in range(8):
            bk = st.bi % 6
            st.bi += 1
            for kc in range(4):
                p.op("pe", lambda e, bk=bk, kc=kc, h=h: e.matmul(banks[bk][:, :], lhsT=wukv[:, kc, h, 0:128],
                                                                 rhs=ckn[:, kc, :], start=(kc == 0), stop=(kc == 3)),
                     reads=[b_wukv, b_ckn], writes=[b_banks[bk]])
            si = st.k % 3
            st.k += 1
            evac(p, st.k, stg[si][:], banks[bk][:, :], [b_banks[bk]], wb=[b_stg[si]])
            p.dma("sp", kT_d[h, :, t0:t0 + T], stg[si][:], reads=[b_stg[si]], wpart=[bk_])
        for tl in range(T // 128):
            for hb in range(2):
                bk = st.bi % 6
                st.bi += 1
                for kc in range(4):
                    p.op("pe", lambda e, bk=bk, kc=kc, hb=hb, tl=tl: e.matmul(
                        banks[bk][:, :].rearrange("p (h d) -> p h d", d=128),
                        lhsT=ckn[:, kc, tl * 128:(tl + 1) * 128],
                        rhs=wukv[:, kc, hb * 4:(hb + 1) * 4, 128:256], start=(kc == 0), stop=(kc == 3)),
                        reads=[b_wukv, b_ckn], writes=[b_banks[bk]])
                si = st.k % 3
                st.k += 1
                evac(p, st.k, stg[si][:], banks[bk][:, :], [b_banks[bk]], wb=[b_stg[si]])
                r0 = t0 + tl * 128
                p.dma("sp", v_d[r0:r0 + 128, hb * 512:(hb + 1) * 512], stg[si][:], reads=[b_stg[si]], wpart=[bv])
        bk1 = st.bi % 6
        bk2 = (st.bi + 1) % 6
        st.bi += 2
        for kc in range(32):
            p.op("pe", lambda e, bk1=bk1, kc=kc: e.matmul(banks[bk1][:64, :], lhsT=wkr[:, kc, 0, :], rhs=hT[:, kc, :],
                                                          start=(kc == 0), stop=(kc == 31)),
                 reads=[b_wkr, b_hT], writes=[b_banks[bk1]])
        for kc in range(32):
            p.op("pe", lambda e, bk2=bk2, kc=kc: e.matmul(banks[bk2][:64, :], lhsT=wkrr[:, kc, 0, :], rhs=hT[:, kc, :],
                                                          start=(kc == 0), stop=(kc == 31)),
                 reads=[b_wkrr, b_hT], writes=[b_banks[bk2]])
        si = st.k % 3
        st.k += 1
        rope_combine(c, banks[bk1], b_banks[bk1], banks[bk2], b_banks[bk2], cs, b_cs, t0, T, t1, b_t1, t2, b_t2,
                     stg[si][:64, :], [b_stg[si]])
        p.dma("sp", krT_d[:, t0:t0 + T], stg[si][:64, :], reads=[b_stg[si]], wpart=[bkr])


def ph_attn(c, qT_d, bq, kT_d, bk_, krT_d, bkr, v_d, bv, aT_d, ba, ntok, nkeys, nheads=8):
    p = c.p
    NKT = nkeys // 128
    scale = 1.0 / math.sqrt(192.0)
    kr = c.sb([64, nkeys], BF16)
    b_kr = Buf()
    p.dma("sp", kr[:], krT_d, reads=[bkr], writes=[b_kr])
    kT = c.sb([128, nkeys], BF16)
    b_kT = Buf()
    V = c.sb([128, NKT, 128], BF16)
    b_V = Buf()
    ones, b_ones = ones_bf(c, [128, 128])
    qn = c.sb([128, ntok], BF16)
    qr = c.sb([64, ntok], BF16)
    b_qn, b_qr = Buf(), Buf()
    NSB = 4
    sbank = [c.ps([128, 512], F32) for _ in range(NSB)]
    b_sbank = [Buf() for _ in range(NSB)]
    obank = [c.ps([128, 512], F32) for _ in range(2)]
    b_obank = [Buf() for _ in range(2)]
    rbank = [c.ps([128, 512], F32) for _ in range(2)]
    b_rbank = [Buf() for _ in range(2)]
    P = [c.sb([128, 512], BF16) for _ in range(NSB)]
    b_P = [Buf() for _ in range(NSB)]
    rec = c.sb([128, 512], F32)
    b_rec = Buf()
    ast = [c.sb([128, 512], BF16) for _ in range(2)]
    b_ast = [Buf(), Buf()]
    k = 0
    gi = 0
    for h in range(nheads):
        p.dma("sp", kT[:], kT_d[h], reads=[bk_], writes=[b_kT])
        p.dma("act", V[:], v_d[:, h * 128:(h + 1) * 128].rearrange("(t p) d -> p t d", p=128),
              reads=[bv], writes=[b_V])
        p.dma("sp", qn[:], qT_d[h, 0:128, :], reads=[bq], writes=[b_qn])
        p.dma("sp", qr[:], qT_d[h, 128:192, :], reads=[bq], writes=[b_qr])
        for qg in range(ntok // 512):
            q0 = qg * 512
            kbase = k
            k += NKT
            ob = gi % 2
            gi += 1

            def emit_S(kt, q0=q0, kbase=kbase):
                sb_ = (kbase + kt) % NSB
                p.op("pe", lambda e: e.matmul(sbank[sb_][:, :], lhsT=kT[:, kt * 128:(kt + 1) * 128],
                                              rhs=qn[:, q0:q0 + 512], start=True, stop=False),
                     reads=[b_kT, b_qn], writes=[b_sbank[sb_]])
                p.op("pe", lambda e: e.matmul(sbank[sb_][:, :], lhsT=kr[:, kt * 128:(kt + 1) * 128],
                                              rhs=qr[:, q0:q0 + 512], start=False, stop=True),
                     reads=[b_kr, b_qr], writes=[b_sbank[sb_]])
                p.op("act", lambda e: e.activation(out=P[sb_][:], in_=sbank[sb_][:, :], func=AF.Exp, scale=scale),
                     reads=[b_sbank[sb_]], writes=[b_P[sb_]])

            emit_S(0)
            emit_S(1)
            for kt in range(NKT):
                if kt + 2 < NKT:
                    emit_S(kt + 2)
                sb_ = (kbase + kt) % NSB
                p.op("pe", lambda e, sb_=sb_, kt=kt, ob=ob: e.matmul(obank[ob][:, :], lhsT=V[:, kt, :], rhs=P[sb_][:],
                                                                     start=(kt == 0), stop=(kt == NKT - 1)),
                     reads=[b_P[sb_], b_V], writes=[b_obank[ob]])
                p.op("pe", lambda e, sb_=sb_, kt=kt, ob=ob: e.matmul(rbank[ob][:, :], lhsT=ones[:], rhs=P[sb_][:],
                                                                     start=(kt == 0), stop=(kt == NKT - 1)),
                     reads=[b_P[sb_], b_ones], writes=[b_rbank[ob]])
            p.op("dve", lambda e, ob=ob: e.reciprocal(out=rec[:], in_=rbank[ob][:, :]),
                 reads=[b_rbank[ob]], writes=[b_rec])
            ai = ob
            p.op("dve", lambda e, ob=ob, ai=ai: e.tensor_tensor(out=ast[ai][:], in0=obank[ob][:, :], in1=rec[:],
                                                                op=ALU.mult),
                 reads=[b_obank[ob], b_rec], writes=[b_ast[ai]])
            p.dma("sp", aT_d[h * 128:(h + 1) * 128, q0:q0 + 512], ast[ai][:], reads=[b_ast[ai]], wpart=[ba])


def ph_sg(c, hT_d, bh, win_d, bw, gsg_d, bgsg, wsT_d, bws, bsb_d, bbs, sgT_d, bsg, ntok):
    p = c.p
    T = 512
    NTL = T // 128
    hT = c.sb([128, 32, T], BF16)
    b_hT = Buf()
    wbufs = [c.sb([128, 32, 512], BF16) for _ in range(2)]
    b_wbufs = [Buf(), Buf()]
    banks = [c.ps([128, 512], F32) for _ in range(8)]
    b_banks = [Buf() for _ in range(8)]
    gsg = c.sb([128, 2048], F32)
    b_gsg = Buf()
    p.dma("act", gsg[:], gsg_d, reads=[bgsg], writes=[b_gsg])
    wsT = c.sb([128, 16, 128], BF16)
    b_wsT = Buf()
    p.dma("pool", wsT[:], wsT_d.rearrange("g q p -> q g p"), reads=[bws], writes=[b_wsT])
    bsb = c.sb([128, 16, 128], F32)
    b_bsb = Buf()
    p.dma("act", bsb[:], bsb_d, reads=[bbs], writes=[b_bsb])
    vn = c.sb([128, NTL, 2048], BF16)
    b_vn = Buf()
    gv = [c.sb([128, 4, 128], F32) for _ in range(2)]
    b_gv = [Buf(), Buf()]
    xc = [c.sb([128, 4, 128], F32) for _ in range(2)]
    b_xc = [Buf(), Buf()]
    sqj = c.sb([128, 4, 128], F32)
    b_sqj = Buf()
    mu = [c.sb([128, 4], F32) for _ in range(2)]
    b_mu = [Buf(), Buf()]
    var = [c.sb([128, 4], F32) for _ in range(2)]
    b_var = [Buf(), Buf()]
    uT = [c.sb([128, T], F32) for _ in range(2)]
    b_uT = [Buf(), Buf()]
    tt = [c.sb([128, T], F32) for _ in range(2)]
    b_tt = [Buf(), Buf()]
    stg = [c.sb([128, T], BF16) for _ in range(2)]
    b_stg = [Buf(), Buf()]
    st = St()
    for s in range(ntok // T):
        t0 = s * T
        load_hT(c, hT, b_hT, hT_d, bh, t0, T)
        for vb in range(4):
            w = st.wi % 2
            st.wi += 1
            col0 = 4416 + vb * 512
            p.dma("pool", wbufs[w][:], win_d[:, col0:col0 + 512].rearrange("(c p) n -> p c n", p=128),
                  reads=[bw], writes=[b_wbufs[w]])
            for tl in range(NTL):
                bk = st.bi % 8
                st.bi += 1
                for kc in range(32):
                    p.op("pe", lambda e, bk=bk, kc=kc, tl=tl, w=w: e.matmul(
                        banks[bk][:, :], lhsT=hT[:, kc, tl * 128:(tl + 1) * 128], rhs=wbufs[w][:, kc, :],
                        start=(kc == 0), stop=(kc == 31)), reads=[b_hT, b_wbufs[w]], writes=[b_banks[bk]])
                i = st.k % 2
                st.k += 1
                bank3 = banks[bk][:, :].rearrange("p (g d) -> p g d", d=128)
                p.op("act", lambda e, i=i, bank3=bank3: e.activation(out=gv[i][:], in_=bank3, func=AF.Gelu_apprx_tanh),
                     reads=[b_banks[bk]], writes=[b_gv[i]])
                p.op("dve", lambda e, i=i: e.reduce_sum(out=mu[i][:], in_=gv[i][:], axis=AX.X),
                     reads=[b_gv[i]], writes=[b_mu[i]])
                p.op("dve", lambda e, i=i: e.tensor_scalar(out=mu[i][:], in0=mu[i][:], scalar1=1.0 / 128, scalar2=None,
                                                           op0=ALU.mult), reads=[b_mu[i]], writes=[b_mu[i]])
                p.op("dve", lambda e, i=i: e.tensor_tensor(out=xc[i][:], in0=gv[i][:],
                                                           in1=mu[i][:].unsqueeze(2).to_broadcast([128, 4, 128]),
                                                           op=ALU.subtract), reads=[b_gv[i], b_mu[i]], writes=[b_xc[i]])
                p.op("dve", lambda e, i=i: e.tensor_tensor(out=sqj[:], in0=xc[i][:], in1=xc[i][:], op=ALU.mult),
                     reads=[b_xc[i]], writes=[b_sqj])
                p.op("dve", lambda e, i=i: e.reduce_sum(out=var[i][:], in_=sqj[:], axis=AX.X),
                     reads=[b_sqj], writes=[b_var[i]])
                rstd_ops(p, "dve", var[i][:], var[i][:], 128, [b_var[i]], [b_var[i]])
                p.op("dve", lambda e, i=i: e.tensor_tensor(out=xc[i][:], in0=xc[i][:],
                                                           in1=var[i][:].unsqueeze(2).to_broadcast([128, 4, 128]),
                                                           op=ALU.mult), reads=[b_xc[i], b_var[i]], writes=[b_xc[i]])
                p.op("dve", lambda e, i=i, tl=tl, vb=vb: e.tensor_tensor(
                    out=vn[:, tl, vb * 512:(vb + 1) * 512], in0=xc[i][:].rearrange("p g d -> p (g d)"),
                    in1=gsg[:, vb * 512:(vb + 1) * 512], op=ALU.mult),
                    reads=[b_xc[i], b_gsg], wpart=[b_vn])

        def epi(tag, ch, ti, bank, b_bank, wd):
            g = tag * 4 + ch
            i = g % 2
            p.op("act", lambda e: e.activation(out=uT[i][:], in_=bank[:, :], func=AF.Gelu_apprx_tanh),
                 reads=[b_bank], writes=[b_uT[i]])
            bk = st.bi % 8
            st.bi += 1
            for tl in range(NTL):
                p.op("pe", lambda e, tl=tl: e.matmul(banks[bk][:, tl * 128:(tl + 1) * 128],
                                                     lhsT=vn[:, tl, g * 128:(g + 1) * 128], rhs=wsT[:, g, :],
                                                     start=True, stop=True),
                     reads=[b_vn, b_wsT], writes=[b_banks[bk]])
            p.op("dve", lambda e: e.tensor_tensor(out=tt[i][:].rearrange("p (a b) -> p a b", b=128),
                                                  in0=banks[bk][:, :].rearrange("p (a b) -> p a b", b=128),
                                                  in1=bsb[:, g, :].unsqueeze(1).to_broadcast([128, NTL, 128]),
                                                  op=ALU.add), reads=[b_banks[bk], b_bsb], writes=[b_tt[i]])
            p.op("dve", lambda e: e.tensor_tensor(out=stg[i][:], in0=tt[i][:], in1=uT[i][:], op=ALU.mult),
                 reads=[b_tt[i], b_uT[i]], writes=[b_stg[i]])
            p.dma("sp", sgT_d[g * 128:(g + 1) * 128, t0:t0 + T], stg[i][:], reads=[b_stg[i]], wpart=[bsg])

        fm_stream(c, [(win_d, bw, 2368 + 512 * i, 512, i) for i in range(4)], 32,
                  lambda kc, tc0, tn: hT[:, kc, tc0:tc0 + tn], b_hT, [(0, T)], epi, wbufs, b_wbufs, banks, b_banks, st)


def ph_merge(c, hT_d, bh, fT_d, bf, aT_d, ba, sgT_d, bsg, wg_d, bwg, bgT_d, bbg, wf_d, bwf, wa_d, bwa, ws_d, bws,
             mT_d, bm, ntok):
    p = c.p
    T = 1024 if ntok % 1024 == 0 else 512
    CW = 128
    NTG = T // 512
    hT = c.sb([128, 32, T], BF16)
    fT = c.sb([128, 8, T], BF16)
    aT = c.sb([128, 8, T], BF16)
    sT = c.sb([128, 16, T], BF16)
    b_hT, b_fT, b_aT, b_sT = Buf(), Buf(), Buf(), Buf()
    wg = [c.sb([128, 32, 3, CW], BF16) for _ in range(2)]
    wf = [c.sb([128, 8, CW], BF16) for _ in range(2)]
    wa = [c.sb([128, 8, CW], BF16) for _ in range(2)]
    ws = [c.sb([128, 16, CW], BF16) for _ in range(2)]
    b_wg, b_wf, b_wa, b_ws = [Buf(), Buf()], [Buf(), Buf()], [Buf(), Buf()], [Buf(), Buf()]
    bgT = c.sb([128, 96], F32)
    b_bgT = Buf()
    p.dma("act", bgT[:], bgT_d, reads=[bbg], writes=[b_bgT])
    banks = [c.ps([128, 512], F32) for _ in range(8)]
    b_banks = [Buf() for _ in range(8)]
    gt = [c.sb([128, 512], F32) for _ in range(3)]
    b_gt = [Buf() for _ in range(3)]
    m1 = c.sb([128, 512], F32)
    m2 = c.sb([128, 512], F32)
    b_m1, b_m2 = Buf(), Buf()
    stg = [c.sb([128, 512], BF16) for _ in range(2)]
    b_stg = [Buf(), Buf()]
    bi = 0
    wi = 0
    for s in range(ntok // T):
        t0 = s * T
        load_hT(c, hT, b_hT, hT_d, bh, t0, T)
        p.dma("sp", fT[:], fT_d[:, t0:t0 + T].rearrange("(c p) t -> p c t", p=128), reads=[bf], writes=[b_fT])
        p.dma("sp", aT[:], aT_d[:, t0:t0 + T].rearrange("(c p) t -> p c t", p=128), reads=[ba], writes=[b_aT])
        p.dma("sp", sT[:], sgT_d[:, t0:t0 + T].rearrange("(c p) t -> p c t", p=128), reads=[bsg], writes=[b_sT])
        for jb in range(D // CW):
            w = wi % 2
            wi += 1
            c0 = jb * CW
            for b in range(3):
                p.dma("pool", wg[w][:, :, b, :],
                      wg_d[:, b * D + c0:b * D + c0 + CW].rearrange("(c p) n -> p c n", p=128),
                      reads=[bwg], wpart=[b_wg[w]])
            p.dma("pool", wf[w][:], wf_d[:, c0:c0 + CW].rearrange("(c p) n -> p c n", p=128), reads=[bwf], writes=[b_wf[w]])
            p.dma("pool", wa[w][:], wa_d[:, c0:c0 + CW].rearrange("(c p) n -> p c n", p=128), reads=[bwa], writes=[b_wa[w]])
            p.dma("pool", ws[w][:], ws_d[:, c0:c0 + CW].rearrange("(c p) n -> p c n", p=128), reads=[bws], writes=[b_ws[w]])
            for sub in range(CW // 128):
              for tg in range(NTG):
                tsl = slice(tg * 512, (tg + 1) * 512)
                j = jb * (CW // 128) + sub
                n0 = j * 128
                cs_ = slice(sub * 128, (sub + 1) * 128)
                gb_ = []
                for b in range(3):
                    bk = bi % 8
                    bi += 1
                    gb_.append(bk)
                    for kc in range(32):
                        p.op("pe", lambda e, bk=bk, kc=kc, b=b, w=w, cs_=cs_, tsl=tsl: e.matmul(
                            banks[bk][:, :], lhsT=wg[w][:, kc, b, cs_], rhs=hT[:, kc, tsl], start=(kc == 0), stop=(kc == 31)),
                            reads=[b_wg[w], b_hT], writes=[b_banks[bk]])
                yb_ = []
                for b, (wt, b_wt, act, b_act, nk) in enumerate(((wf, b_wf, fT, b_fT, 8), (wa, b_wa, aT, b_aT, 8),
                                                                (ws, b_ws, sT, b_sT, 16))):
                    bk = bi % 8
                    bi += 1
                    yb_.append(bk)
                    for kc in range(nk):
                        p.op("pe", lambda e, bk=bk, kc=kc, wt=wt, act=act, nk=nk, w=w, cs_=cs_, tsl=tsl: e.matmul(
                            banks[bk][:, :], lhsT=wt[w][:, kc, cs_], rhs=act[:, kc, tsl], start=(kc == 0),
                            stop=(kc == nk - 1)), reads=[b_wt[w], b_act], writes=[b_banks[bk]])
                for b in range(3):
                    p.op("act", lambda e, b=b, bk=gb_[b], j=j: e.activation(out=gt[b][:], in_=banks[bk][:, :],
                                                                            func=AF.Sigmoid,
                                                                            bias=bgT[:, b * 32 + j:b * 32 + j + 1]),
                         reads=[b_banks[gb_[b]], b_bgT], writes=[b_gt[b]])
                p.op("dve", lambda e, bk=yb_[0]: e.tensor_tensor(out=m1[:], in0=banks[bk][:, :], in1=gt[0][:], op=ALU.mult),
                     reads=[b_banks[yb_[0]], b_gt[0]], writes=[b_m1])
                p.op("dve", lambda e, bk=yb_[1]: e.tensor_tensor(out=m2[:], in0=banks[bk][:, :], in1=gt[1][:], op=ALU.mult),
                     reads=[b_banks[yb_[1]], b_gt[1]], writes=[b_m2])
                p.op("dve", lambda e: e.tensor_tensor(out=m1[:], in0=m1[:], in1=m2[:], op=ALU.add),
                     reads=[b_m1, b_m2], writes=[b_m1])
                p.op("dve", lambda e, bk=yb_[2]: e.tensor_tensor(out=m2[:], in0=banks[bk][:, :], in1=gt[2][:], op=ALU.mult),
                     reads=[b_banks[yb_[2]], b_gt[2]], writes=[b_m2])
                si = (j * NTG + tg) % 2
                p.op("dve", lambda e, si=si: e.tensor_tensor(out=stg[si][:], in0=m1[:], in1=m2[:], op=ALU.add),
                     reads=[b_m1, b_m2], writes=[b_stg[si]])
                p.dma("sp", mT_d[n0:n0 + 128, t0 + tg * 512:t0 + (tg + 1) * 512], stg[si][:], reads=[b_stg[si]], wpart=[bm])


def ph_memkv(c, mT_d, bmT, wmk_d, bwk, wmv_d, bwv, kmT_d, bkm, vm_d, bvm):
    p = c.p
    mT = c.sb([128, 32, 256], BF16)
    b_mT = Buf()
    load_hT(c, mT, b_mT, mT_d, bmT, 0, 256)
    wbufs = [c.sb([128, 32, 512], BF16) for _ in range(2)]
    b_wbufs = [Buf(), Buf()]
    banks = [c.ps([128, 512], F32) for _ in range(4)]
    b_banks = [Buf() for _ in range(4)]
    stg = [c.sb([128, 512], BF16) for _ in range(2)]
    b_stg = [Buf(), Buf()]
    st = St()

    def epi(tag, ch, ti, bank, b_bank, wd):
        si = st.k % 2
        st.k += 1
        evac(p, st.k, stg[si][:, :256], bank[:, :256], [b_bank], wb=[b_stg[si]])
        r0 = (tag * 4 + ch) * 128
        p.dma("sp", kmT_d[r0:r0 + 128, :], stg[si][:, :256], reads=[b_stg[si]], wpart=[bkm])

    fm_stream(c, [(wmk_d, bwk, 0, 512, 0), (wmk_d, bwk, 512, 512, 1)], 32,
              lambda kc, tc0, tn: mT[:, kc, tc0:tc0 + tn], b_mT, [(0, 256)], epi, wbufs, b_wbufs, banks, b_banks, st)
    for vb in range(2):
        w = st.wi % 2
        st.wi += 1
        p.dma("pool", wbufs[w][:], wmv_d[:, vb * 512:(vb + 1) * 512].rearrange("(c p) n -> p c n", p=128),
              reads=[bwv], writes=[b_wbufs[w]])
        for mt in range(2):
            bk = st.bi % 4
            st.bi += 1
            for kc in range(32):
                p.op("pe", lambda e, bk=bk, kc=kc, mt=mt, w=w: e.matmul(banks[bk][:, :], lhsT=mT[:, kc, mt * 128:(mt + 1) * 128],
                                                                        rhs=wbufs[w][:, kc, :], start=(kc == 0), stop=(kc == 31)),
                     reads=[b_mT, b_wbufs[w]], writes=[b_banks[bk]])
            si = st.k % 2
            st.k += 1
            evac(p, st.k, stg[si][:], banks[bk][:, :], [b_banks[bk]], wb=[b_stg[si]])
            p.dma("sp", vm_d[mt * 128:(mt + 1) * 128, vb * 512:(vb + 1) * 512], stg[si][:], reads=[b_stg[si]], wpart=[bvm])


def ph_mq(c, hT_d, bh, wmq_d, bwq, kmT_d, bkm, vm_d, bvm, omT_d, bom, ntok):
    p = c.p
    T = 512
    idb, b_id = c.ident()
    hT = c.sb([128, 32, T], BF16)
    b_hT = Buf()
    wbufs = [c.sb([128, 32, 512], BF16) for _ in range(2)]
    b_wbufs = [Buf(), Buf()]
    banks = [c.ps([128, 512], F32) for _ in range(6)]
    b_banks = [Buf() for _ in range(6)]
    tbank = c.ps([128, 4, 2, 128], BF16)
    b_tbank = Buf()
    km = c.sb([128, 8, 256], BF16)
    b_km = Buf()
    p.dma("sp", km[:], kmT_d.rearrange("(c p) m -> p c m", p=128), reads=[bkm], writes=[b_km])
    vm = c.sb([128, 2, 4, 260], BF16)
    b_vm = Buf()
    p.op("pool", lambda e: e.memset(vm[:, :, :, 256:260], 1.0), wpart=[b_vm])
    for mt in range(2):
        p.dma("sp", vm[:, mt, :, 0:256], vm_d[mt * 128:(mt + 1) * 128, :].rearrange("p (h d) -> p h d", d=256),
              reads=[bvm], wpart=[b_vm])
    qm = c.sb([128, 8, T], BF16)
    b_qm = Buf()
    P = [c.sb([128, T], BF16) for _ in range(4)]
    b_P = [Buf() for _ in range(4)]
    rec = c.sb([128, 1], F32)
    b_rec = Buf()
    on = c.sb([128, 4, 256], BF16)
    b_on = Buf()
    stg = [c.sb([128, 2, T], BF16) for _ in range(2)]
    b_stg = [Buf(), Buf()]
    st = St()
    pk = 0
    for s in range(ntok // T):
        t0 = s * T
        load_hT(c, hT, b_hT, hT_d, bh, t0, T)

        def epi(tag, ch, ti, bank, b_bank, wd):
            st.k += 1
            evac(p, st.k, qm[:, tag * 4 + ch, :], bank[:, :], [b_bank], wp=[b_qm])

        fm_stream(c, [(wmq_d, bwq, 0, 512, 0), (wmq_d, bwq, 512, 512, 1)], 32,
                  lambda kc, tc0, tn: hT[:, kc, tc0:tc0 + tn], b_hT, [(0, T)], epi, wbufs, b_wbufs, banks, b_banks, st)
        for h in range(4):
            pi = []
            for mt in range(2):
                bk = st.bi % 6
                st.bi += 1
                for dc in range(2):
                    p.op("pe", lambda e, bk=bk, dc=dc, mt=mt, h=h: e.matmul(
                        banks[bk][:, :], lhsT=km[:, 2 * h + dc, mt * 128:(mt + 1) * 128], rhs=qm[:, 2 * h + dc, :],
                        start=(dc == 0), stop=(dc == 1)), reads=[b_km, b_qm], writes=[b_banks[bk]])
                pj = pk % 4
                pk += 1
                pi.append(pj)
                p.op("act", lambda e, bk=bk, pj=pj: e.activation(out=P[pj][:], in_=banks[bk][:, :], func=AF.Exp,
                                                                 scale=1.0 / 16.0), reads=[b_banks[bk]], writes=[b_P[pj]])
            for qt in range(4):
                bk = st.bi % 6
                st.bi += 1
                for mt in range(2):
                    p.op("pe", lambda e, bk=bk, mt=mt, qt=qt, h=h, pj=pi[mt]: e.matmul(
                        banks[bk][:, 0:257], lhsT=P[pj][:, qt * 128:(qt + 1) * 128], rhs=vm[:, mt, h, 0:257],
                        start=(mt == 0), stop=(mt == 1)), reads=[b_P[pi[mt]], b_vm], writes=[b_banks[bk]])
                p.op("dve", lambda e, bk=bk: e.reciprocal(out=rec[:], in_=banks[bk][:, 256:257]),
                     reads=[b_banks[bk]], writes=[b_rec])
                p.op("dve", lambda e, bk=bk, qt=qt: e.tensor_scalar(out=on[:, qt, :], in0=banks[bk][:, 0:256],
                                                                    scalar1=rec[:, 0:1], scalar2=None, op0=ALU.mult),
                     reads=[b_banks[bk], b_rec], wpart=[b_on])
            for qt in range(4):
                for dc in range(2):
                    p.op("pe", lambda e, qt=qt, dc=dc: e.transpose(out=tbank[:, qt, dc, :],
                                                                   in_=on[:, qt, dc * 128:(dc + 1) * 128], identity=idb[:]),
                         reads=[b_on, b_id], writes=[b_tbank])
            si = h % 2
            p.op("act", lambda e, si=si: e.activation(out=stg[si][:].rearrange("p d (q t) -> p q d t", t=128),
                                                      in_=tbank[:], func=AF.Copy), reads=[b_tbank], writes=[b_stg[si]])
            p.dma("sp", omT_d[h * 256:(h + 1) * 256, t0:t0 + T].rearrange("(d p) t -> p d t", p=128), stg[si][:],
                  reads=[b_stg[si]], wpart=[bom])


def ph_up(c, hT_d, bh, wup_d, bwu, cw_d, bcw, pT_d, bpT, ntok):
    p = c.p
    T = 1024
    G = 342
    hT = c.sb([128, 32, T + 2], BF16)
    b_hT = Buf()
    wbufs = [c.sb([128, 32, 256], BF16) for _ in range(4)]
    b_wbufs = [Buf() for _ in range(4)]
    banks = [c.ps([128, 512], F32) for _ in range(8)]
    b_banks = [Buf() for _ in range(8)]
    cw = c.sb([128, 172, 4], F32)
    b_cw = Buf()
    p.dma("act", cw[:], cw_d, reads=[bcw], writes=[b_cw])
    ue = {(k, ch): c.sb([128, T + 2], F32) for k in "gv" for ch in range(2)}
    b_ue = {k: Buf() for k in ue}
    cg = c.sb([128, T], F32)
    cv = c.sb([128, T], F32)
    b_cg, b_cv = Buf(), Buf()
    stg = [c.sb([128, T], BF16) for _ in range(2)]
    b_stg = [Buf(), Buf()]
    st = St()
    NB = 43
    for s in range(ntok // T):
        t0 = s * T
        nst = ntok // T
        lc = ntok if s == 0 else t0 - 1
        rc = ntok + 1 if s == nst - 1 else t0 + T
        p.dma("sp", hT[:, :, 1:T + 1], hT_d[:, :, t0:t0 + T].rearrange("c p t -> p c t"), reads=[bh], writes=[b_hT])
        p.dma("sp", hT[:, :, 0:1], hT_d[:, :, lc:lc + 1].rearrange("c p t -> p c t"), reads=[bh], wpart=[b_hT],
              allow_slow_non_contiguous=True)
        p.dma("sp", hT[:, :, T + 1:T + 2], hT_d[:, :, rc:rc + 1].rearrange("c p t -> p c t"), reads=[bh], wpart=[b_hT],
              allow_slow_non_contiguous=True)

        def epi(tag, ch, ti, bank, b_bank, wd):
            kind, bj = tag
            u = ue[(kind, ch)]
            st.k += 1
            evac(p, st.k, u[:, ti * G:(ti + 1) * G], bank[:, :G], [b_bank], wp=[b_ue[(kind, ch)]])
            if kind == "v" and ti == 2:
                j = bj * 2 + ch
                for (kd, ci, acc, b_acc, eng) in (("g", j, cg, b_cg, "dve"), ("v", 86 + j, cv, b_cv, "dve")):
                    uu = ue[(kd, ch)]
                    bu = b_ue[(kd, ch)]
                    p.op(eng, lambda e, uu=uu, ci=ci, acc=acc: e.tensor_scalar(
                        out=acc[:], in0=uu[:, 0:T], scalar1=cw[:, ci, 0:1], scalar2=cw[:, ci, 3:4],
                        op0=ALU.mult, op1=ALU.add), reads=[bu, b_cw], writes=[b_acc])
                    p.op(eng, lambda e, uu=uu, ci=ci, acc=acc: e.scalar_tensor_tensor(
                        out=acc[:], in0=uu[:, 1:T + 1], scalar=cw[:, ci, 1:2], in1=acc[:], op0=ALU.mult, op1=ALU.add),
                        reads=[bu, b_cw, b_acc], writes=[b_acc])
                    p.op(eng, lambda e, uu=uu, ci=ci, acc=acc: e.scalar_tensor_tensor(
                        out=acc[:], in0=uu[:, 2:T + 2], scalar=cw[:, ci, 2:3], in1=acc[:], op0=ALU.mult, op1=ALU.add),
                        reads=[bu, b_cw, b_acc], writes=[b_acc])
                p.op("act", lambda e: e.activation(out=cg[:], in_=cg[:], func=AF.Gelu_apprx_tanh),
                     reads=[b_cg], writes=[b_cg])
                si = j % 2
                p.op("dve", lambda e: e.tensor_tensor(out=stg[si][:], in0=cg[:], in1=cv[:], op=ALU.mult),
                     reads=[b_cg, b_cv], writes=[b_stg[si]])
                p.dma("sp", pT_d[j, :, t0:t0 + T], stg[si][:], reads=[b_stg[si]], wpart=[bpT])

        blocks = []
        for bj in range(NB):
            blocks.append((wup_d, bwu, 256 * bj, 256, ("g", bj)))
            blocks.append((wup_d, bwu, 11008 + 256 * bj, 256, ("v", bj)))
        fm_stream(c, blocks, 32, lambda kc, tc0, tn: hT[:, kc, tc0:tc0 + tn], b_hT,
                  [(0, G), (G, G), (2 * G, G)], epi, wbufs, b_wbufs, banks, b_banks, st)


BF = ml_dtypes.bfloat16
_PROGS = {}


def _rep(v, n=128):
    return np.ascontiguousarray(np.broadcast_to(v, (n,) + tuple(v.shape)))


def prog_l1():
    c = Ctx()
    x_d, bx = c.din("x", [TOK, D], F32)
    gb_d, bg = c.din("gb", [128, D], F32)
    win_d, bw = c.din("win", [D, 6464], F32)
    dft_d, bd = c.din("dft", [2, 256, 256], F32)
    pos_d, bp = c.din("pos", [64, TOK], I32)
    invf_d, bi = c.din("invf", [64, 1], F32)
    wuq_d, bwuq = c.din("wuq", [768, 1536], F32)
    gq_d, bgq = c.din("gq", [128, 6], F32)
    wukv_d, bwukv = c.din("wukv", [512, 2048], F32)
    gkv_d, bgkv = c.din("gkv", [128, 4], F32)
    gsg_d, bgsg = c.din("gsg", [128, 2048], F32)
    wsT_d, bws = c.din("wsT", [16, 128, 128], F32)
    bsb_d, bbs = c.din("bsb", [128, 16, 128], F32)
    hT_d, bh = c.dint("hT", [32, 128, TOK], BF16)
    cs_d, bcs = c.dint("cs", [2, 64, TOK], F32)
    ab_d, bab = c.dout("ab", [2, 1024, TOK], BF16)
    qT_d, bq = c.dout("qT", [8, 192, TOK], BF16)
    kT_d, bk_ = c.dout("kT", [8, 128, TOK], BF16)
    krT_d, bkr = c.dout("krT", [64, TOK], BF16)
    v_d, bv = c.dout("v", [TOK, 1024], BF16)
    sgT_d, bsg = c.dout("sgT", [2048, TOK], BF16)
    with c.phase():
        ph_head(c, x_d, bx, gb_d, bg, hT_d, bh, TOK)
    with c.phase():
        ph_rope(c, pos_d, bp, invf_d, bi, cs_d, bcs, TOK)
    with c.phase():
        ph_f(c, hT_d, bh, win_d, bw, dft_d, bd, ab_d, bab, TOK)
    with c.phase():
        ph_q(c, hT_d, bh, win_d, bw, wuq_d, bwuq, gq_d, bgq, cs_d, bcs, qT_d, bq, TOK)
    with c.phase():
        ph_kv(c, hT_d, bh, win_d, bw, wukv_d, bwukv, gkv_d, bgkv, cs_d, bcs, kT_d, bk_, krT_d, bkr, v_d, bv, TOK)
    with c.phase():
        ph_sg(c, hT_d, bh, win_d, bw, gsg_d, bgsg, wsT_d, bws, bsb_d, bbs, sgT_d, bsg, TOK)
    c.finish()
    return c.nc


def prog_l2():
    c = Ctx()
    m_d, bm = c.din("m", [128, 2, 128, 128], BF16)
    cs_d, bcs = c.din("cs128", [2, 128, 128], F32)
    tw_d, btw = c.din("tw", [2, 128, 128], F32)
    qT_d, bq = c.din("qT", [8, 192, TOK], BF16)
    kT_d, bk_ = c.din("kT", [8, 128, SEQ], BF16)
    krT_d, bkr = c.din("krT", [64, SEQ], BF16)
    v_d, bv = c.din("v", [SEQ, 1024], BF16)
    fo_d, bfo = c.dout("fo", [128, 128, 128], BF16)
    aT_d, ba = c.dout("aT", [1024, TOK], BF16)
    with c.phase():
        ph_fft(c, m_d, bm, cs_d, bcs, tw_d, btw, fo_d, bfo)
    with c.phase():
        ph_attn(c, qT_d, bq, kT_d, bk_, krT_d, bkr, v_d, bv, aT_d, ba, TOK, SEQ)
    c.finish()
    return c.nc


def prog_l3():
    c = Ctx()
    x_d, bx = c.din("x", [TOK, D], F32)
    gb_d, bg = c.din("gb", [128, D], F32)
    fT_d, bf = c.din("fT", [1024, TOK], BF16)
    aT_d, ba = c.din("aT", [1024, TOK], BF16)
    sT_d, bs_ = c.din("sgT", [2048, TOK], BF16)
    wg_d, bwg = c.din("wg", [D, 3 * D], F32)
    bgT_d, bbg = c.din("bgT", [128, 96], F32)
    wf_d, bwf = c.din("wf", [1024, D], F32)
    wa_d, bwa = c.din("wa", [1024, D], F32)
    ws_d, bws = c.din("ws", [2048, D], F32)
    wo_d, bwo = c.din("wo", [D, D], F32)
    gp1_d, bgp1 = c.din("gp1", [128, D], F32)
    gb2_d, bg2 = c.din("gb2", [128, D], F32)
    mem_d, bmem = c.din("mem", [256, D], F32)
    gkv_d, bgkv = c.din("gmkv", [128, D], F32)
    wmq_d, bwq = c.din("wmq", [D, 1024], F32)
    wmk_d, bwk = c.din("wmk", [D, 1024], F32)
    wmv_d, bwv = c.din("wmv", [D, 1024], F32)
    wmo_d, bwmo = c.din("wmo", [1024, D], F32)
    gp2_d, bgp2 = c.din("gp2", [128, D], F32)
    hT_d, bh = c.dint("hT", [32, 128, TOK], BF16)
    mT_d, bm = c.dint("mgT", [32, 128, TOK], BF16)
    y_d, by = c.dint("y", [TOK, D], F32)
    ssq_d, bss = c.dint("ssq", [TOK, 8], F32)
    x1_d, bx1 = c.dint("x1", [TOK, D], F32)
    memT_d, bmemT = c.dint("memT", [32, 128, 256], BF16)
    kmT_d, bkm = c.dint("kmT", [1024, 256], BF16)
    vm_d, bvm = c.dint("vm", [256, 1024], BF16)
    omT_d, bom = c.dint("omT", [8, 128, TOK], BF16)
    x2_d, bx2 = c.dout("x2", [TOK, D], F32)
    with c.phase():
        ph_head(c, x_d, bx, gb_d, bg, hT_d, bh, TOK)
    with c.phase():
        ph_merge(c, hT_d, bh, fT_d, bf, aT_d, ba, sT_d, bs_, wg_d, bwg, bgT_d, bbg, wf_d, bwf, wa_d, bwa, ws_d, bws,
                 mT_d.rearrange("c p t -> (c p) t"), bm, TOK)
    with c.phase():
        ph_lin(c, mT_d, bm, 32, wo_d, bwo, y_d, by, ssq_d, bss, TOK)
    with c.phase():
        ph_nr(c, y_d, by, ssq_d, bss, gp1_d, bgp1, x_d, bx, x1_d, bx1, TOK)
    with c.phase():
        ph_head(c, x1_d, bx1, gb2_d, bg2, hT_d, bh, TOK)
    with c.phase():
        ph_head(c, mem_d, bmem, gkv_d, bgkv, memT_d, bmemT, 256)
    with c.phase():
        ph_memkv(c, memT_d, bmemT, wmk_d, bwk, wmv_d, bwv, kmT_d, bkm, vm_d, bvm)
    with c.phase():
        ph_mq(c, hT_d, bh, wmq_d, bwq, kmT_d, bkm, vm_d, bvm, omT_d.rearrange("c p t -> (c p) t"), bom, TOK)
    with c.phase():
        ph_lin(c, omT_d, bom, 8, wmo_d, bwmo, y_d, by, ssq_d, bss, TOK)
    with c.phase():
        ph_nr(c, y_d, by, ssq_d, bss, gp2_d, bgp2, x1_d, bx1, x2_d, bx2, TOK)
    c.finish()
    return c.nc


def prog_l4():
    c = Ctx()
    x_d, bx = c.din("x", [TOK, D], F32)
    xh_d, bxh = c.din("xh", [2, D], F32)
    gb_d, bg = c.din("gb", [128, D], F32)
    wup_d, bwu = c.din("wup", [D, 22016], F32)
    cw_d, bcw = c.din("cw", [128, 172, 4], F32)
    wdn_d, bwd = c.din("wdn", [11008, D], F32)
    gp_d, bgp = c.din("gp", [128, D], F32)
    hT_d, bh = c.dint("hT", [32, 128, TOK + 2], BF16)
    pT_d, bpT = c.dint("pT", [86, 128, TOK], BF16)
    y_d, by = c.dint("y", [TOK, D], F32)
    ssq_d, bss = c.dint("ssq", [TOK, 8], F32)
    xo_d, bxo = c.dout("x3", [TOK, D], F32)
    with c.phase():
        ph_head(c, x_d, bx, gb_d, bg, hT_d, bh, TOK, halo=(xh_d, bxh))
    with c.phase():
        ph_up(c, hT_d, bh, wup_d, bwu, cw_d, bcw, pT_d, bpT, TOK)
    with c.phase():
        ph_lin(c, pT_d, bpT, 86, wdn_d, bwd, y_d, by, ssq_d, bss, TOK)
    with c.phase():
        ph_nr(c, y_d, by, ssq_d, bss, gp_d, bgp, x_d, bx, xo_d, bxo, TOK)
    c.finish()
    return c.nc


def _prog(name, fn):
    if name not in _PROGS:
        _PROGS[name] = fn()
    return _PROGS[name]


def _run(nc, maps):
    res = run_bass_kernel_spmd(nc, maps, core_ids=list(range(NCORES)))
    return res.results


def kernel(x, mem, positions, mix_pre_norm, mix_post_norm, w_in, mla_q_norm, w_uq, mla_kv_norm, w_ukv, sg_norm,
           w_spatial, b_spatial, w_br_f, w_br_a, w_br_s, w_gate, b_gate, w_out, mem_pre_norm, mem_post_norm,
           mem_kv_norm, w_mq, w_mk, w_mv, w_mo, ffn_pre_norm, ffn_post_norm, w_up, conv_w, conv_b, w_down):
    f32 = np.float32
    xs = np.asarray(x, f32)[0]
    memv = np.ascontiguousarray(np.asarray(mem, f32)[0])
    pos = np.asarray(positions)[0].astype(np.int32)
    jj = np.arange(256)
    a256 = 2 * np.pi * np.outer(jj, jj) / 256
    dft = np.stack([np.cos(a256), -np.sin(a256)]).astype(f32)
    kk = np.arange(128)
    a128 = 2 * np.pi * np.outer(kk, kk) / 128
    cs128 = np.stack([np.cos(a128), np.sin(a128)]).astype(f32)
    atw = 2 * np.pi * np.outer(kk, kk) / SEQ
    tw = np.stack([np.cos(atw), np.sin(atw)]).astype(f32)
    invf = (10000.0 ** (-np.arange(0, 64, 2, dtype=f32) / 64)).astype(f32)
    invf2 = np.concatenate([invf, invf]).reshape(64, 1)
    A = lambda v: np.ascontiguousarray(np.asarray(v, f32))
    for l in range(2):
        com = {"gb": _rep(A(mix_pre_norm[l])), "win": A(w_in[l]), "dft": dft, "invf": invf2, "wuq": A(w_uq[l]),
               "gq": np.ascontiguousarray(A(mla_q_norm[l]).reshape(6, 128).T), "wukv": A(w_ukv[l]),
               "gkv": np.ascontiguousarray(A(mla_kv_norm[l]).reshape(4, 128).T), "gsg": _rep(A(sg_norm[l])),
               "wsT": np.ascontiguousarray(A(w_spatial[l]).transpose(0, 2, 1)), "bsb": _rep(A(b_spatial[l]))}
        maps = []
        for c in range(NCORES):
            sl = slice(c * TOK, (c + 1) * TOK)
            m = dict(com)
            m["x"] = np.ascontiguousarray(xs[sl])
            m["pos"] = _rep(pos[sl], 64)
            maps.append(m)
        r1 = _run(_prog("l1", prog_l1), maps)
        ab = np.stack([r1[c]["ab"] for c in range(NCORES)])
        abr = ab.reshape(NCORES, 2, 8, 128, 16, 128)
        kT_all = np.ascontiguousarray(np.concatenate([r1[c]["kT"] for c in range(NCORES)], axis=2))
        krT_all = np.ascontiguousarray(np.concatenate([r1[c]["krT"] for c in range(NCORES)], axis=1))
        v_all = np.ascontiguousarray(np.concatenate([r1[c]["v"] for c in range(NCORES)], axis=0))
        maps = []
        for j in range(NCORES):
            mm = abr[:, :, j].transpose(0, 3, 1, 2, 4).reshape(128, 2, 128, 128)
            maps.append({"m": np.ascontiguousarray(mm), "cs128": cs128, "tw": tw, "qT": r1[j]["qT"], "kT": kT_all,
                         "krT": krT_all, "v": v_all})
        r2 = _run(_prog("l2", prog_l2), maps)
        fo = np.stack([r2[j]["fo"] for j in range(NCORES)])
        fall = fo.transpose(1, 3, 0, 2).reshape(SEQ, 1024)
        com = {"gb": _rep(A(mix_pre_norm[l])), "wg": A(w_gate[l]),
               "bgT": np.ascontiguousarray(A(b_gate[l]).reshape(96, 128).T), "wf": A(w_br_f[l]), "wa": A(w_br_a[l]),
               "ws": A(w_br_s[l]), "wo": A(w_out[l]), "gp1": _rep(A(mix_post_norm[l])),
               "gb2": _rep(A(mem_pre_norm[l])), "mem": memv, "gmkv": _rep(A(mem_kv_norm[l])), "wmq": A(w_mq[l]),
               "wmk": A(w_mk[l]), "wmv": A(w_mv[l]), "wmo": A(w_mo[l]), "gp2": _rep(A(mem_post_norm[l]))}
        maps = []
        for c in range(NCORES):
            sl = slice(c * TOK, (c + 1) * TOK)
            m = dict(com)
            m["x"] = np.ascontiguousarray(xs[sl])
            m["fT"] = np.ascontiguousarray(fall[sl].T)
            m["aT"] = r2[c]["aT"]
            m["sgT"] = r1[c]["sgT"]
            maps.append(m)
        r3 = _run(_prog("l3", prog_l3), maps)
        x2 = np.concatenate([r3[c]["x2"] for c in range(NCORES)], axis=0)
        cwp = np.concatenate([A(conv_w[l]), A(conv_b[l])[None]], axis=0)
        cwp = np.ascontiguousarray(cwp.reshape(4, 172, 128).transpose(2, 1, 0))
        com = {"gb": _rep(A(ffn_pre_norm[l])), "wup": A(w_up[l]), "cw": cwp, "wdn": A(w_down[l]),
               "gp": _rep(A(ffn_post_norm[l]))}
        zero = np.zeros(D, f32)
        maps = []
        for c in range(NCORES):
            sl = slice(c * TOK, (c + 1) * TOK)
            m = dict(com)
            m["x"] = np.ascontiguousarray(x2[sl])
            prev = x2[c * TOK - 1] if c > 0 else zero
            nxt = x2[(c + 1) * TOK] if c < NCORES - 1 else zero
            m["xh"] = np.ascontiguousarray(np.stack([prev, nxt]))
            maps.append(m)
        r4 = _run(_prog("l4", prog_l4), maps)
        xs = np.concatenate([r4[c]["x3"] for c in range(NCORES)], axis=0)
    return np.ascontiguousarray(xs[None].astype(f32))
```

```python
import math
from contextlib import ExitStack
import numpy as np
import ml_dtypes
import concourse.bass as bass
import concourse.mybir as mybir
from concourse.bass_utils import run_bass_kernel_spmd

F32 = mybir.dt.float32
BF16 = mybir.dt.bfloat16
I32 = mybir.dt.int32
AF = mybir.ActivationFunctionType
ALU = mybir.AluOpType
AX = mybir.AxisListType

NCORES = 8
SEQ = 16384
TOK = SEQ // NCORES
D = 4096
EPS = 1e-6
NSLOT = 6


class Buf:
    __slots__ = ("wr", "rd")

    def __init__(self):
        self.wr = {}
        self.rd = {}


class Op:
    __slots__ = ("eng", "fn", "reads", "writes", "wpart", "dma", "deps", "marked", "semval", "slot", "key", "xdeps")

    def __init__(self, eng, fn, reads, writes, wpart, dma):
        self.eng = eng
        self.fn = fn
        self.reads = reads
        self.writes = writes
        self.wpart = wpart
        self.dma = dma
        self.deps = ()
        self.marked = False
        self.semval = None
        self.slot = None
        self.key = None
        self.xdeps = ()


class Prog:
    def __init__(self, nc):
        self.nc = nc
        self.ops = []
        self.dma_count = {}

    def op(self, eng, fn, reads=(), writes=(), wpart=(), dma=False):
        o = Op(eng, fn, tuple(reads), tuple(writes), tuple(wpart), dma)
        if dma:
            k = self.dma_count.get(eng, 0)
            self.dma_count[eng] = k + 1
            o.slot = k % NSLOT
            o.semval = 16 * (k // NSLOT + 1)
            o.key = (eng, o.slot)
            o.marked = True
        else:
            o.key = eng
        self.ops.append(o)

    def dma(self, q, out, in_, reads=(), writes=(), wpart=(), **kw):
        self.op(q, lambda e: e.dma_start(out=out, in_=in_, **kw), reads, writes, wpart, dma=True)

    def barrier(self):
        last = {}
        for i, o in enumerate(self.ops):
            last[o.key] = i
        engs = sorted(set(o.eng for o in self.ops))
        tgt = tuple(last.values())
        for e in engs:
            self.op(e, lambda eo: eo.nop())
            self.ops[-1].xdeps = tgt

    def emit(self):
        nc = self.nc
        ops = self.ops
        last_on_slot = {}
        for i, o in enumerate(ops):
            deps = set(o.xdeps)
            for b in o.reads:
                deps.update(b.wr.values())
            for b in o.writes:
                deps.update(b.wr.values())
                deps.update(b.rd.values())
            for b in o.wpart:
                deps.update(b.rd.values())
            for b in o.writes:
                b.wr = {o.key: i}
                b.rd = {}
            for b in o.wpart:
                if b.rd:
                    b.wr = {o.key: i}
                    b.rd = {}
                else:
                    b.wr[o.key] = i
            for b in o.reads:
                if b.wr.get(o.key) != i:
                    b.rd[o.key] = i
            if o.dma:
                prev = last_on_slot.get(o.key)
                if prev is not None:
                    deps.add(prev)
                last_on_slot[o.key] = i
            deps.discard(i)
            fd = []
            for d in deps:
                od = ops[d]
                if od.eng == "pe" and o.eng == "pe" and not od.dma and not o.dma:
                    continue
                fd.append(d)
            o.deps = fd
            for d in fd:
                ops[d].marked = True
        cnt = {}
        for o in ops:
            if o.dma:
                continue
            if o.marked:
                cnt[o.eng] = cnt.get(o.eng, 0) + 1
                o.semval = cnt[o.eng]
        engs = sorted(set(o.eng for o in ops))
        ctxs = []
        sems = {}
        for e in engs:
            if cnt.get(e, 0) > 0:
                cm = nc.semaphore("s_" + e)
                sems[e] = cm.__enter__()
                ctxs.append(cm)
            for s in range(min(NSLOT, self.dma_count.get(e, 0))):
                cm = nc.semaphore("d_%s_%d" % (e, s))
                sems[(e, s)] = cm.__enter__()
                ctxs.append(cm)
        per_eng = {e: [] for e in engs}
        for i, o in enumerate(ops):
            per_eng[o.eng].append(i)

        def run_engine(ename, eobj):
            seen = {}
            for i in per_eng.get(ename, ()):
                o = ops[i]
                need = {}
                for d in o.deps:
                    od = ops[d]
                    v = od.semval
                    if need.get(od.key, 0) < v:
                        need[od.key] = v
                for key, v in need.items():
                    if seen.get(key, 0) >= v:
                        continue
                    eobj.wait_ge(sems[key], v)
                    seen[key] = v
                ins = o.fn(eobj)
                if o.marked:
                    ins.then_inc(sems[o.key], 16 if o.dma else 1)

        with nc.Block() as block:
            if "sp" in per_eng:
                @block.sync
                def _(e):
                    run_engine("sp", e)
            if "act" in per_eng:
                @block.scalar
                def _(e):
                    run_engine("act", e)
            if "dve" in per_eng:
                @block.vector
                def _(e):
                    run_engine("dve", e)
            if "pool" in per_eng:
                @block.gpsimd
                def _(e):
                    run_engine("pool", e)
            if "pe" in per_eng:
                @block.tensor
                def _(e):
                    run_engine("pe", e)
        for cm in reversed(ctxs):
            cm.__exit__(None, None, None)
        return len(ops)


class Ctx:
    def __init__(self):
        self.nc = bass.Bass("TRN2", target_bir_lowering=False)
        self.p = Prog(self.nc)
        self.es = None
        self.n = 0
        self.outs = []
        self.rr = 0

    def din(self, name, shape, dt):
        return self.nc.dram_tensor(name, list(shape), dt, kind="ExternalInput").ap(), Buf()

    def dout(self, name, shape, dt):
        b = Buf()
        self.outs.append(b)
        return self.nc.dram_tensor(name, list(shape), dt, kind="ExternalOutput").ap(), b

    def dint(self, name, shape, dt):
        return self.nc.dram_tensor(name, list(shape), dt, kind="Internal").ap(), Buf()

    def sb(self, shape, dt):
        self.n += 1
        return self.es.enter_context(self.nc.sbuf_tensor("sb%d" % self.n, list(shape), dt))

    def ps(self, shape, dt):
        self.n += 1
        return self.es.enter_context(self.nc.psum_tensor("ps%d" % self.n, list(shape), dt))

    def phase(self):
        c = self

        class _Ph:
            def __enter__(self_):
                self_.es = ExitStack()
                self_.es.__enter__()
                c.es = self_.es
                return c

            def __exit__(self_, *a):
                c.p.barrier()
                c.es = None
                return self_.es.__exit__(*a)
        return _Ph()

    def finish(self):
        self.p.op("sp", lambda e: e.nop(), reads=list(self.outs))
        return self.p.emit()

    def ident(self):
        p = self.p
        idf = self.sb([128, 128], F32)
        idb = self.sb([128, 128], BF16)
        b = Buf()
        p.op("pool", lambda e: e.memset(idf[:], 0.0), writes=[b])
        p.op("pool", lambda e: e.affine_select(out=idf[:], in_=idf[:], pattern=[[-1, 128]],
                                               compare_op=ALU.not_equal, fill=1.0, base=0,
                                               channel_multiplier=1), reads=[b], writes=[b])
        p.op("dve", lambda e: e.tensor_copy(out=idb[:], in_=idf[:]), reads=[b], writes=[b])
        return idb, b


def rstd_ops(p, eng, out, in_, n, rb, wb):
    p.op(eng, lambda e: e.tensor_scalar(out=out, in0=in_, scalar1=1.0 / n, scalar2=EPS,
                                        op0=ALU.mult, op1=ALU.add), reads=rb, writes=wb)
    p.op("act", lambda e: e.activation(out=out, in_=out, func=AF.Sqrt), reads=wb, writes=wb)
    p.op(eng, lambda e: e.reciprocal(out=out, in_=out), reads=wb, writes=wb)


def ph_head(c, x_d, bx, gb_d, bg, hT_d, bh, ntok, col0=0, halo=None):
    p = c.p
    if True:
        idb, b_id = c.ident()
        gb = c.sb([128, D], F32)
        b_gb = Buf()
        p.dma("act", gb[:], gb_d, reads=[bg], writes=[b_gb])
        xt = [c.sb([128, D], F32) for _ in range(2)]
        b_xt = [Buf(), Buf()]
        xb = [c.sb([128, D], BF16) for _ in range(2)]
        b_xb = [Buf(), Buf()]
        ss = [c.sb([128, 1], F32) for _ in range(2)]
        b_ss = [Buf(), Buf()]
        hTt = [c.sb([128, 32, 512], BF16) for _ in range(2)]
        b_hTt = [Buf(), Buf()]
        pt = [c.ps([128, 8, 128], BF16) for _ in range(4)]
        b_pt = [Buf() for _ in range(4)]
        ntile = ntok // 128
        jobs = [(t * 128, 128, None) for t in range(ntile)]
        if halo is not None:
            jobs.append((0, 2, halo))
        ptc = 0
        for ji, (r0, nr, hal) in enumerate(jobs):
            s = ji % 2
            if hal is None:
                p.dma("sp", xt[s][:nr, :], x_d[r0:r0 + nr, :], reads=[bx], writes=[b_xt[s]])
            else:
                p.dma("sp", xt[s][:nr, :], hal[0][0:nr, :], reads=[hal[1]], writes=[b_xt[s]])
            p.op("act", lambda e, s=s, nr=nr: e.activation(out=xb[s][:nr, :], in_=xt[s][:nr, :], func=AF.Square,
                                                           accum_out=ss[s][:nr, :]),
                 reads=[b_xt[s]], writes=[b_xb[s], b_ss[s]])
            rstd_ops(p, "dve", ss[s][:nr, :], ss[s][:nr, :], D, [b_ss[s]], [b_ss[s]])
            p.op("dve", lambda e, s=s, nr=nr: e.scalar_tensor_tensor(out=xb[s][:nr, :], in0=xt[s][:nr, :],
                                                                     scalar=ss[s][:nr, 0:1], in1=gb[:nr, :],
                                                                     op0=ALU.mult, op1=ALU.mult),
                 reads=[b_xt[s], b_ss[s], b_gb], writes=[b_xb[s]])
            hs = (ji // 4) % 2
            tcol = (ji % 4) * 128
            for q in range(4):
                pi = ptc % 4
                ptc += 1
                for cc in range(8):
                    ch = q * 8 + cc
                    p.op("pe", lambda e, s=s, nr=nr, ch=ch, cc=cc, pi=pi: e.transpose(
                        out=pt[pi][:, cc, :nr], in_=xb[s][:nr, ch * 128:(ch + 1) * 128], identity=idb[:nr, :nr]),
                        reads=[b_xb[s], b_id], writes=[b_pt[pi]])
                eng = "act" if q % 2 == 0 else "dve"
                if eng == "act":
                    p.op("act", lambda e, hs=hs, q=q, pi=pi, nr=nr, tcol=tcol: e.activation(
                        out=hTt[hs][:, q * 8:(q + 1) * 8, tcol:tcol + nr], in_=pt[pi][:, :, :nr], func=AF.Copy),
                        reads=[b_pt[pi]], wpart=[b_hTt[hs]])
                else:
                    p.op("dve", lambda e, hs=hs, q=q, pi=pi, nr=nr, tcol=tcol: e.tensor_copy(
                        out=hTt[hs][:, q * 8:(q + 1) * 8, tcol:tcol + nr], in_=pt[pi][:, :, :nr]),
                        reads=[b_pt[pi]], wpart=[b_hTt[hs]])
            if hal is not None:
                p.dma("sp", hT_d[:, :, ntok:ntok + 2].rearrange("c p t -> p c t"), hTt[hs][:, :, tcol:tcol + 2],
                      reads=[b_hTt[hs]], wpart=[bh])
            elif ji % 4 == 3 or ji == ntile - 1:
                g0 = col0 + (ji // 4) * 512
                wdt = (ji % 4 + 1) * 128
                p.dma("sp", hT_d[:, :, g0:g0 + wdt].rearrange("c p t -> p c t"), hTt[hs][:, :, :wdt],
                      reads=[b_hTt[hs]], wpart=[bh])


def ph_lin(c, aT_d, ba, KC, W_d, bw, y_d, by, ssq_d, bs, ntok, acol0=0):
    p = c.p
    KS = 8
    nks = (KC + KS - 1) // KS
    T = 1024 if (KC <= 32 and ntok % 1024 == 0) else 512
    NTL = T // 128
    if True:
        aT = c.sb([128, KC, T], BF16)
        b_aT = Buf()
        wb = [c.sb([128, KS, 512], BF16) for _ in range(3)]
        b_wb = [Buf() for _ in range(3)]
        acc = [c.ps([128, 512], F32) for _ in range(8)]
        b_acc = [Buf() for _ in range(8)]
        ysb = [c.sb([128, 512], F32) for _ in range(4)]
        b_ysb = [Buf() for _ in range(4)]
        junk = c.sb([128, 512], BF16)
        b_junk = Buf()
        ssq = [c.sb([128, 8], F32) for _ in range(NTL)]
        b_ssq = [Buf() for _ in range(NTL)]
        wi = 0
        yi = 0
        for st in range(ntok // T):
            t0 = st * T
            p.dma("sp", aT[:], aT_d[:, :, acol0 + t0:acol0 + t0 + T].rearrange("c p t -> p c t"),
                  reads=[ba], writes=[b_aT])
            for mb in range(8):
                par = (mb % 2) * 4 if NTL == 4 else 0
                for ks in range(nks):
                    k0 = ks * KS
                    kn = min(KS, KC - k0)
                    w = wi % 3
                    wi += 1
                    p.dma("pool", wb[w][:, :kn, :],
                          W_d[k0 * 128:(k0 + kn) * 128, mb * 512:(mb + 1) * 512].rearrange("(c p) n -> p c n", p=128),
                          reads=[bw], writes=[b_wb[w]])
                    for tl in range(NTL):
                        for cc in range(kn):
                            kc = k0 + cc
                            p.op("pe", lambda e, tl=tl, cc=cc, kc=kc, w=w, par=par: e.matmul(
                                acc[par + tl][:], lhsT=aT[:, kc, tl * 128:(tl + 1) * 128], rhs=wb[w][:, cc, :],
                                start=(kc == 0), stop=(kc == KC - 1)),
                                reads=[b_aT, b_wb[w]], writes=[b_acc[par + tl]])
                for tl in range(NTL):
                    y = yi % 4
                    yi += 1
                    p.op("act", lambda e, y=y, a=par + tl: e.activation(out=ysb[y][:], in_=acc[a][:], func=AF.Copy),
                         reads=[b_acc[par + tl]], writes=[b_ysb[y]])
                    p.op("act", lambda e, y=y, tl=tl, mb=mb: e.activation(out=junk[:], in_=ysb[y][:], func=AF.Square,
                                                                          accum_out=ssq[tl][:, mb:mb + 1]),
                         reads=[b_ysb[y]], writes=[b_junk], wpart=[b_ssq[tl]])
                    r0 = t0 + tl * 128
                    p.dma("sp", y_d[r0:r0 + 128, mb * 512:(mb + 1) * 512], ysb[y][:], reads=[b_ysb[y]], wpart=[by])
            for tl in range(NTL):
                r0 = t0 + tl * 128
                p.dma("sp", ssq_d[r0:r0 + 128, :], ssq[tl][:], reads=[b_ssq[tl]], wpart=[bs])


def ph_nr(c, y_d, by, ssq_d, bs, gb_d, bg, xi_d, bxi, xo_d, bxo, ntok):
    p = c.p
    if True:
        gb = c.sb([128, D], F32)
        b_gb = Buf()
        p.dma("act", gb[:], gb_d, reads=[bg], writes=[b_gb])
        yt = [c.sb([128, D], F32) for _ in range(2)]
        xt = [c.sb([128, D], F32) for _ in range(2)]
        sq = [c.sb([128, 8], F32) for _ in range(2)]
        rs = [c.sb([128, 1], F32) for _ in range(2)]
        b_yt = [Buf(), Buf()]
        b_xt = [Buf(), Buf()]
        b_sq = [Buf(), Buf()]
        b_rs = [Buf(), Buf()]
        for t in range(ntok // 128):
            s = t % 2
            r0 = t * 128
            p.dma("sp", yt[s][:], y_d[r0:r0 + 128, :], reads=[by], writes=[b_yt[s]])
            p.dma("act", xt[s][:], xi_d[r0:r0 + 128, :], reads=[bxi], writes=[b_xt[s]])
            p.dma("sp", sq[s][:], ssq_d[r0:r0 + 128, :], reads=[bs], writes=[b_sq[s]])
            p.op("dve", lambda e, s=s: e.reduce_sum(out=rs[s][:], in_=sq[s][:], axis=AX.X),
                 reads=[b_sq[s]], writes=[b_rs[s]])
            rstd_ops(p, "dve", rs[s][:], rs[s][:], D, [b_rs[s]], [b_rs[s]])
            p.op("dve", lambda e, s=s: e.scalar_tensor_tensor(out=yt[s][:], in0=yt[s][:], scalar=rs[s][:, 0:1],
                                                              in1=gb[:], op0=ALU.mult, op1=ALU.mult),
                 reads=[b_yt[s], b_rs[s], b_gb], writes=[b_yt[s]])
            p.op("pool", lambda e, s=s: e.tensor_tensor(out=yt[s][:], in0=yt[s][:], in1=xt[s][:], op=ALU.add),
                 reads=[b_yt[s], b_xt[s]], writes=[b_yt[s]])
            p.dma("sp", xo_d[r0:r0 + 128, :], yt[s][:], reads=[b_yt[s]], wpart=[bxo])


class St:
    def __init__(self):
        self.wi = 0
        self.bi = 0
        self.k = 0


def evac(p, k, out, in_, rb, wb=(), wp=()):
    if k % 2 == 0:
        p.op("act", lambda e: e.activation(out=out, in_=in_, func=AF.Copy), reads=rb, writes=wb, wpart=wp)
    else:
        p.op("dve", lambda e: e.tensor_copy(out=out, in_=in_), reads=rb, writes=wb, wpart=wp)


def load_hT(c, hT, b_hT, hT_d, bh, t0, n, KC=32, q="sp"):
    c.p.dma(q, hT[:, :KC, :n], hT_d[:KC, :, t0:t0 + n].rearrange("c p t -> p c t"), reads=[bh], writes=[b_hT])


def fm_stream(c, blocks, KC, rhs_fn, b_rhs, tgs, epi, wbufs, b_wbufs, banks, b_banks, st):
    p = c.p
    for (W_d, bw, col0, ncols, tag) in blocks:
        w = st.wi % len(wbufs)
        st.wi += 1
        p.dma("pool", wbufs[w][:, :KC, :ncols],
              W_d[0:KC * 128, col0:col0 + ncols].rearrange("(c p) n -> p c n", p=128),
              reads=[bw], writes=[b_wbufs[w]])
        for ch in range((ncols + 127) // 128):
            wd = min(128, ncols - ch * 128)
            for ti, (tc0, tn) in enumerate(tgs):
                bk = st.bi % len(banks)
                st.bi += 1
                for kc in range(KC):
                    p.op("pe", lambda e, bk=bk, w=w, kc=kc, ch=ch, wd=wd, tc0=tc0, tn=tn: e.matmul(
                        banks[bk][:wd, :tn], lhsT=wbufs[w][:, kc, ch * 128:ch * 128 + wd], rhs=rhs_fn(kc, tc0, tn),
                        start=(kc == 0), stop=(kc == KC - 1)),
                        reads=[b_wbufs[w], b_rhs], writes=[b_banks[bk]])
                epi(tag, ch, ti, banks[bk], b_banks[bk], wd)


def ones_bf(c, shape):
    t = c.sb(shape, BF16)
    b = Buf()
    c.p.op("pool", lambda e: e.memset(t[:], 1.0), writes=[b])
    return t, b


def fm_rmsnorm(c, raw, b_raw, nch, nfeat, tn, gcol, b_g, ssb, b_ssb, rs, b_rs, outT, b_out):
    p = c.p
    p.op("dve", lambda e: e.tensor_scalar(out=rs[:, :tn], in0=ssb[:, :tn], scalar1=1.0 / nfeat, scalar2=EPS,
                                          op0=ALU.mult, op1=ALU.add), reads=[b_ssb], writes=[b_rs])
    p.op("act", lambda e: e.activation(out=rs[:, :tn], in_=rs[:, :tn], func=AF.Sqrt), reads=[b_rs], writes=[b_rs])
    p.op("dve", lambda e: e.reciprocal(out=rs[:, :tn], in_=rs[:, :tn]), reads=[b_rs], writes=[b_rs])
    for ch in range(nch):
        p.op("dve", lambda e, ch=ch: e.scalar_tensor_tensor(out=outT[:, ch, :tn], in0=raw[:, ch, :tn],
                                                            scalar=gcol[:, ch:ch + 1], in1=rs[:, :tn],
                                                            op0=ALU.mult, op1=ALU.mult),
             reads=[b_raw, b_g, b_rs], wpart=[b_out])


def ph_rope(c, pos_d, bp, invf_d, bi, cs_d, bcs, ntok):
    p = c.p
    pi_ = c.sb([64, ntok], I32)
    pf = c.sb([64, ntok], F32)
    t1 = c.sb([64, ntok], F32)
    t2 = c.sb([64, ntok], F32)
    ki = c.sb([64, ntok], I32)
    iv = c.sb([64, 1], F32)
    b_pi, b_pf, b_t1, b_t2, b_ki, b_iv = Buf(), Buf(), Buf(), Buf(), Buf(), Buf()
    p.dma("sp", pi_[:], pos_d, reads=[bp], writes=[b_pi])
    p.dma("sp", iv[:], invf_d, reads=[bi], writes=[b_iv])
    p.op("dve", lambda e: e.tensor_copy(out=pf[:], in_=pi_[:]), reads=[b_pi], writes=[b_pf])
    p.op("dve", lambda e: e.tensor_scalar(out=pf[:], in0=pf[:], scalar1=iv[:, 0:1], scalar2=1.0 / (2 * math.pi),
                                          op0=ALU.mult, op1=ALU.mult), reads=[b_pf, b_iv], writes=[b_pf])
    for k, sh in enumerate((0.25, 0.0)):
        p.op("dve", lambda e, sh=sh: e.tensor_scalar(out=t1[:], in0=pf[:], scalar1=sh, scalar2=None, op0=ALU.add),
             reads=[b_pf], writes=[b_t1])
        p.op("dve", lambda e: e.tensor_copy(out=ki[:], in_=t1[:]), reads=[b_t1], writes=[b_ki])
        p.op("dve", lambda e: e.tensor_copy(out=t2[:], in_=ki[:]), reads=[b_ki], writes=[b_t2])
        p.op("dve", lambda e: e.tensor_tensor(out=t1[:], in0=t1[:], in1=t2[:], op=ALU.subtract),
             reads=[b_t1, b_t2], writes=[b_t1])
        p.op("dve", lambda e: e.tensor_scalar(out=t2[:], in0=t1[:], scalar1=0.5, scalar2=None, op0=ALU.is_gt),
             reads=[b_t1], writes=[b_t2])
        p.op("dve", lambda e: e.tensor_tensor(out=t1[:], in0=t1[:], in1=t2[:], op=ALU.subtract),
             reads=[b_t1, b_t2], writes=[b_t1])
        p.op("dve", lambda e: e.tensor_scalar(out=t2[:], in0=t1[:], scalar1=-0.5, scalar2=None, op0=ALU.is_lt),
             reads=[b_t1], writes=[b_t2])
        p.op("dve", lambda e: e.tensor_tensor(out=t1[:], in0=t1[:], in1=t2[:], op=ALU.add),
             reads=[b_t1, b_t2], writes=[b_t1])
        p.op("act", lambda e: e.activation(out=t1[:], in_=t1[:], func=AF.Sin, scale=2 * math.pi),
             reads=[b_t1], writes=[b_t1])
        p.dma("sp", cs_d[k], t1[:], reads=[b_t1], wpart=[bcs])


def ph_f(c, hT_d, bh, win_d, bw, dft_d, bd, ab_d, bab, ntok):
    p = c.p
    T = 512
    hT = c.sb([128, 32, T], BF16)
    b_hT = Buf()
    wbufs = [c.sb([128, 32, 512], BF16) for _ in range(2)]
    b_wbufs = [Buf(), Buf()]
    banks = [c.ps([128, 512], F32) for _ in range(6)]
    b_banks = [Buf() for _ in range(6)]
    tab = c.sb([128, 2, 2, 256], BF16)
    b_tab = Buf()
    p.dma("pool", tab[:], dft_d.rearrange("a (c p) j -> p a c j", p=128), reads=[bd], writes=[b_tab])
    zf = c.sb([128, 8, T], BF16)
    b_zf = Buf()
    stg = [c.sb([128, T], BF16) for _ in range(3)]
    b_stg = [Buf() for _ in range(3)]
    st = St()
    for s in range(ntok // T):
        t0 = s * T
        load_hT(c, hT, b_hT, hT_d, bh, t0, T)

        def epi(tag, ch, ti, bank, b_bank, wd):
            fch = tag * 4 + ch
            st.k += 1
            evac(p, st.k, zf[:, fch, :], bank[:, :], [b_bank], wp=[b_zf])

        fm_stream(c, [(win_d, bw, 0, 512, 0), (win_d, bw, 512, 512, 1)], 32,
                  lambda kc, tc0, tn: hT[:, kc, tc0:tc0 + tn], b_hT, [(0, T)], epi, wbufs, b_wbufs, banks, b_banks, st)
        for g in range(4):
            for jc in range(2):
                for ab in range(2):
                    bk = st.bi % 6
                    st.bi += 1
                    for cc in range(2):
                        p.op("pe", lambda e, bk=bk, ab=ab, cc=cc, jc=jc, g=g: e.matmul(
                            banks[bk][:, :], lhsT=tab[:, ab, cc, jc * 128:(jc + 1) * 128], rhs=zf[:, 2 * g + cc, :],
                            start=(cc == 0), stop=(cc == 1)), reads=[b_tab, b_zf], writes=[b_banks[bk]])
                    si = st.k % 3
                    st.k += 1
                    evac(p, st.k, stg[si][:], banks[bk][:, :], [b_banks[bk]], wb=[b_stg[si]])
                    r0 = g * 256 + jc * 128
                    p.dma("sp", ab_d[ab, r0:r0 + 128, t0:t0 + T], stg[si][:], reads=[b_stg[si]], wpart=[bab])


def ph_fft(c, m_d, bm, cs_d, bcs, tw_d, btw, fo_d, bfo):
    p = c.p
    M = c.sb([128, 2, 128, 128], BF16)
    b_M = Buf()
    p.dma("sp", M[:], m_d, reads=[bm], writes=[b_M])
    csf = c.sb([128, 2, 128], F32)
    tw = c.sb([128, 2, 128], F32)
    b_csf, b_tw = Buf(), Buf()
    p.dma("act", csf[:], cs_d.rearrange("a p k -> p a k"), reads=[bcs], writes=[b_csf])
    p.dma("act", tw[:], tw_d.rearrange("a p k -> p a k"), reads=[btw], writes=[b_tw])
    r1 = c.sb([128, 256], BF16)
    r2 = c.sb([128, 256], BF16)
    b_r = Buf()
    p.op("dve", lambda e: e.tensor_copy(out=r1[:, 0:128], in_=csf[:, 0, :]), reads=[b_csf], wpart=[b_r])
    p.op("dve", lambda e: e.tensor_scalar(out=r1[:, 128:256], in0=csf[:, 1, :], scalar1=-1.0, scalar2=None,
                                          op0=ALU.mult), reads=[b_csf], wpart=[b_r])
    p.op("dve", lambda e: e.tensor_copy(out=r2[:, 0:128], in_=csf[:, 1, :]), reads=[b_csf], wpart=[b_r])
    p.op("dve", lambda e: e.tensor_copy(out=r2[:, 128:256], in_=csf[:, 0, :]), reads=[b_csf], wpart=[b_r])
    Yr = c.sb([128, 128, 128], BF16)
    Yi = c.sb([128, 128, 128], BF16)
    b_Y = Buf()
    banks = [c.ps([128, 2, 2, 128], F32) for _ in range(4)]
    b_banks = [Buf() for _ in range(4)]
    tmp = [c.sb([128, 2, 128], F32) for _ in range(4)]
    b_tmp = [Buf() for _ in range(4)]
    tcb = tw[:, 0, :].unsqueeze(1).to_broadcast([128, 2, 128])
    tsb = tw[:, 1, :].unsqueeze(1).to_broadcast([128, 2, 128])
    for cp in range(64):
        bk = cp % 4
        for j in range(2):
            ch = cp * 2 + j
            p.op("pe", lambda e, bk=bk, j=j, ch=ch: e.matmul(banks[bk][:, j, :, :], lhsT=M[:, 0, ch, :], rhs=r1[:],
                                                            start=True, stop=False),
                 reads=[b_M, b_r], writes=[b_banks[bk]])
            p.op("pe", lambda e, bk=bk, j=j, ch=ch: e.matmul(banks[bk][:, j, :, :], lhsT=M[:, 1, ch, :], rhs=r2[:],
                                                            start=False, stop=True),
                 reads=[b_M, b_r], writes=[b_banks[bk]])
        yr = banks[bk][:, :, 0, :]
        yi = banks[bk][:, :, 1, :]
        ch0 = cp * 2
        p.op("dve", lambda e, yr=yr: e.tensor_tensor(out=tmp[0][:], in0=yr, in1=tcb, op=ALU.mult),
             reads=[b_banks[bk], b_tw], writes=[b_tmp[0]])
        p.op("dve", lambda e, yi=yi: e.tensor_tensor(out=tmp[1][:], in0=yi, in1=tsb, op=ALU.mult),
             reads=[b_banks[bk], b_tw], writes=[b_tmp[1]])
        p.op("dve", lambda e, yi=yi: e.tensor_tensor(out=tmp[2][:], in0=yi, in1=tcb, op=ALU.mult),
             reads=[b_banks[bk], b_tw], writes=[b_tmp[2]])
        p.op("dve", lambda e, yr=yr: e.tensor_tensor(out=tmp[3][:], in0=yr, in1=tsb, op=ALU.mult),
             reads=[b_banks[bk], b_tw], writes=[b_tmp[3]])
        p.op("pool", lambda e, ch0=ch0: e.tensor_tensor(out=Yr[:, ch0:ch0 + 2, :], in0=tmp[0][:], in1=tmp[1][:],
                                                        op=ALU.add), reads=[b_tmp[0], b_tmp[1]], wpart=[b_Y])
        p.op("pool", lambda e, ch0=ch0: e.tensor_tensor(out=Yi[:, ch0:ch0 + 2, :], in0=tmp[2][:], in1=tmp[3][:],
                                                        op=ALU.subtract), reads=[b_tmp[2], b_tmp[3]], wpart=[b_Y])
    cc_ = c.sb([128, 128], BF16)
    ss_ = c.sb([128, 128], BF16)
    b_c2 = Buf()
    p.op("dve", lambda e: e.tensor_copy(out=cc_[:], in_=csf[:, 0, :]), reads=[b_csf], wpart=[b_c2])
    p.op("dve", lambda e: e.tensor_copy(out=ss_[:], in_=csf[:, 1, :]), reads=[b_csf], wpart=[b_c2])
    fo = c.sb([128, 128, 128], BF16)
    b_fo = Buf()
    for q in range(32):
        bk = q % 4
        bnk = banks[bk][:].rearrange("p a b k -> p (a b) k")
        p.op("pe", lambda e, bnk=bnk, q=q: e.matmul(bnk, lhsT=cc_[:], rhs=Yr[:, q * 4:(q + 1) * 4, :],
                                                    start=True, stop=False), reads=[b_c2, b_Y], writes=[b_banks[bk]])
        p.op("pe", lambda e, bnk=bnk, q=q: e.matmul(bnk, lhsT=ss_[:], rhs=Yi[:, q * 4:(q + 1) * 4, :],
                                                    start=False, stop=True), reads=[b_c2, b_Y], writes=[b_banks[bk]])
        p.op("act", lambda e, bnk=bnk, q=q: e.activation(out=fo[:, q * 4:(q + 1) * 4, :], in_=bnk, func=AF.Copy,
                                                         scale=1.0 / 2048.0), reads=[b_banks[bk]], wpart=[b_fo])
    p.dma("sp", fo_d, fo[:], reads=[b_fo], writes=[bfo])


def rope_weights(c, w4, b_w4, nk, nh, hd, r0):
    p = c.p
    wr = c.sb([128, nk, nh, 64], BF16)
    b_wr = Buf()
    p.op("dve", lambda e: e.tensor_scalar(out=wr[:, :, :, 0:32], in0=w4[:, :, :, r0 + 32:r0 + 64], scalar1=-1.0,
                                          scalar2=None, op0=ALU.mult), reads=[b_w4], wpart=[b_wr])
    p.op("dve", lambda e: e.tensor_copy(out=wr[:, :, :, 32:64], in_=w4[:, :, :, r0:r0 + 32]),
         reads=[b_w4], wpart=[b_wr])
    return wr, b_wr


def rope_combine(c, br, b_br, brot, b_brot, cs, b_cs, tc0, tn, t1, b_t1, t2, b_t2, out, b_out_w):
    p = c.p
    p.op("dve", lambda e: e.tensor_tensor(out=t1[:64, :tn], in0=br[:64, :tn], in1=cs[:, 0, tc0:tc0 + tn], op=ALU.mult),
         reads=[b_br, b_cs], writes=[b_t1])
    p.op("dve", lambda e: e.tensor_tensor(out=t2[:64, :tn], in0=brot[:64, :tn], in1=cs[:, 1, tc0:tc0 + tn], op=ALU.mult),
         reads=[b_brot, b_cs], writes=[b_t2])
    p.op("dve", lambda e: e.tensor_tensor(out=out, in0=t1[:64, :tn], in1=t2[:64, :tn], op=ALU.add),
         reads=[b_t1, b_t2], writes=b_out_w)


def ph_q(c, hT_d, bh, win_d, bw, wuq_d, bwuq, gq_d, bgq, cs_d, bcs, qT_d, bq, ntok):
    p = c.p
    T = 512
    hT = c.sb([128, 32, T], BF16)
    b_hT = Buf()
    wbufs = [c.sb([128, 32, 512], BF16) for _ in range(2)]
    b_wbufs = [Buf(), Buf()]
    banks = [c.ps([128, 512], F32) for _ in range(6)]
    b_banks = [Buf() for _ in range(6)]
    ssb = c.ps([128, 512], F32)
    b_ssb = Buf()
    wuq = c.sb([128, 6, 8, 192], BF16)
    b_wuq = Buf()
    p.dma("pool", wuq[:], wuq_d.rearrange("(c p) (h d) -> p c h d", p=128, d=192), reads=[bwuq], writes=[b_wuq])
    wr, b_wr = rope_weights(c, wuq, b_wuq, 6, 8, 192, 128)
    gq = c.sb([128, 6], F32)
    b_gq = Buf()
    p.dma("act", gq[:], gq_d, reads=[bgq], writes=[b_gq])
    cs = c.sb([64, 2, ntok], F32)
    b_cs = Buf()
    p.dma("act", cs[:], cs_d.rearrange("a p t -> p a t"), reads=[bcs], writes=[b_cs])
    ones, b_ones = ones_bf(c, [128, 128])
    cq = c.sb([128, 6, T], BF16)
    cqn = c.sb([128, 6, T], BF16)
    b_cq, b_cqn = Buf(), Buf()
    sq = [c.sb([128, T], BF16) for _ in range(2)]
    b_sq = [Buf(), Buf()]
    rs = c.sb([128, T], F32)
    b_rs = Buf()
    stg = [c.sb([128, T], BF16) for _ in range(3)]
    b_stg = [Buf() for _ in range(3)]
    t1 = c.sb([64, T], F32)
    t2 = c.sb([64, T], F32)
    b_t1, b_t2 = Buf(), Buf()
    st = St()
    for s in range(ntok // T):
        t0 = s * T
        load_hT(c, hT, b_hT, hT_d, bh, t0, T)

        def epi(tag, ch, ti, bank, b_bank, wd):
            qc = tag * 4 + ch
            p.op("act", lambda e: e.activation(out=cq[:, qc, :], in_=bank[:, :], func=AF.Copy),
                 reads=[b_bank], wpart=[b_cq])
            si = qc % 2
            p.op("act", lambda e: e.activation(out=sq[si][:], in_=bank[:, :], func=AF.Square),
                 reads=[b_bank], writes=[b_sq[si]])
            p.op("pe", lambda e: e.matmul(ssb[:, :], lhsT=ones[:], rhs=sq[si][:], start=(qc == 0), stop=(qc == 5)),
                 reads=[b_ones, b_sq[si]], writes=[b_ssb])

        fm_stream(c, [(win_d, bw, 1024, 512, 0), (win_d, bw, 1536, 256, 1)], 32,
                  lambda kc, tc0, tn: hT[:, kc, tc0:tc0 + tn], b_hT, [(0, T)], epi, wbufs, b_wbufs, banks, b_banks, st)
        fm_rmsnorm(c, cq, b_cq, 6, 768, T, gq, b_gq, ssb, b_ssb, rs, b_rs, cqn, b_cqn)
        for h in range(8):
            bk = st.bi % 6
            st.bi += 1
            for kc in range(6):
                p.op("pe", lambda e, bk=bk, kc=kc, h=h: e.matmul(banks[bk][:, :], lhsT=wuq[:, kc, h, 0:128],
                                                                 rhs=cqn[:, kc, :], start=(kc == 0), stop=(kc == 5)),
                     reads=[b_wuq, b_cqn], writes=[b_banks[bk]])
            si = st.k % 3
            st.k += 1
            evac(p, st.k, stg[si][:], banks[bk][:, :], [b_banks[bk]], wb=[b_stg[si]])
            p.dma("sp", qT_d[h, 0:128, t0:t0 + T], stg[si][:], reads=[b_stg[si]], wpart=[bq])
            bk1 = st.bi % 6
            bk2 = (st.bi + 1) % 6
            st.bi += 2
            for kc in range(6):
                p.op("pe", lambda e, bk1=bk1, kc=kc, h=h: e.matmul(banks[bk1][:64, :], lhsT=wuq[:, kc, h, 128:192],
                                                                   rhs=cqn[:, kc, :], start=(kc == 0), stop=(kc == 5)),
                     reads=[b_wuq, b_cqn], writes=[b_banks[bk1]])
            for kc in range(6):
                p.op("pe", lambda e, bk2=bk2, kc=kc, h=h: e.matmul(banks[bk2][:64, :], lhsT=wr[:, kc, h, :],
                                                                   rhs=cqn[:, kc, :], start=(kc == 0), stop=(kc == 5)),
                     reads=[b_wr, b_cqn], writes=[b_banks[bk2]])
            si = st.k % 3
            st.k += 1
            rope_combine(c, banks[bk1], b_banks[bk1], banks[bk2], b_banks[bk2], cs, b_cs, t0, T, t1, b_t1, t2, b_t2,
                         stg[si][:64, :], [b_stg[si]])
            p.dma("sp", qT_d[h, 128:192, t0:t0 + T], stg[si][:64, :], reads=[b_stg[si]], wpart=[bq])


def ph_kv(c, hT_d, bh, win_d, bw, wukv_d, bwukv, gkv_d, bgkv, cs_d, bcs, kT_d, bk_, krT_d, bkr, v_d, bv, ntok):
    p = c.p
    T = 512
    hT = c.sb([128, 32, T], BF16)
    b_hT = Buf()
    wbufs = [c.sb([128, 32, 512], BF16) for _ in range(2)]
    b_wbufs = [Buf(), Buf()]
    banks = [c.ps([128, 512], F32) for _ in range(6)]
    b_banks = [Buf() for _ in range(6)]
    ssb = c.ps([128, 512], F32)
    b_ssb = Buf()
    wukv = c.sb([128, 4, 8, 256], BF16)
    b_wukv = Buf()
    p.dma("pool", wukv[:], wukv_d.rearrange("(c p) (h d) -> p c h d", p=128, d=256), reads=[bwukv], writes=[b_wukv])
    wkr = c.sb([128, 32, 1, 64], BF16)
    b_wkr = Buf()
    p.dma("pool", wkr[:, :, 0, :], win_d[:, 2304:2368].rearrange("(c p) n -> p c n", p=128), reads=[bw], writes=[b_wkr])
    wkrr, b_wkrr = rope_weights(c, wkr, b_wkr, 32, 1, 64, 0)
    gkv = c.sb([128, 4], F32)
    b_gkv = Buf()
    p.dma("act", gkv[:], gkv_d, reads=[bgkv], writes=[b_gkv])
    cs = c.sb([64, 2, ntok], F32)
    b_cs = Buf()
    p.dma("act", cs[:], cs_d.rearrange("a p t -> p a t"), reads=[bcs], writes=[b_cs])
    ones, b_ones = ones_bf(c, [128, 128])
    ck = c.sb([128, 4, T], BF16)
    ckn = c.sb([128, 4, T], BF16)
    b_ck, b_ckn = Buf(), Buf()
    sq = [c.sb([128, T], BF16) for _ in range(2)]
    b_sq = [Buf(), Buf()]
    rs = c.sb([128, T], F32)
    b_rs = Buf()
    stg = [c.sb([128, T], BF16) for _ in range(3)]
    b_stg = [Buf() for _ in range(3)]
    t1 = c.sb([64, T], F32)
    t2 = c.sb([64, T], F32)
    b_t1, b_t2 = Buf(), Buf()
    st = St()
    for s in range(ntok // T):
        t0 = s * T
        load_hT(c, hT, b_hT, hT_d, bh, t0, T)

        def epi(tag, ch, ti, bank, b_bank, wd):
            qc = ch
            p.op("act", lambda e: e.activation(out=ck[:, qc, :], in_=bank[:, :], func=AF.Copy),
                 reads=[b_bank], wpart=[b_ck])
            si = qc % 2
            p.op("act", lambda e: e.activation(out=sq[si][:], in_=bank[:, :], func=AF.Square),
                 reads=[b_bank], writes=[b_sq[si]])
            p.op("pe", lambda e: e.matmul(ssb[:, :], lhsT=ones[:], rhs=sq[si][:], start=(qc == 0), stop=(qc == 3)),
                 reads=[b_ones, b_sq[si]], writes=[b_ssb])

        fm_stream(c, [(win_d, bw, 1792, 512, 0)], 32,
                  lambda kc, tc0, tn: hT[:, kc, tc0:tc0 + tn], b_hT, [(0, T)], epi, wbufs, b_wbufs, banks, b_banks, st)
        fm_rmsnorm(c, ck, b_ck, 4, 512, T, gkv, b_gkv, ssb, b_ssb, rs, b_rs, ckn, b_ckn)
        for h in range(8):
            bk = st.bi % 6
            st.bi += 1
            for kc in range(4):
                p.op("pe", lambda e, bk=bk, kc=kc, h=h: e.matmul(banks[bk][:, :], lhsT=wukv[:, kc, h, 0:128],
                                                                 rhs=ckn[:, kc, :], start=(kc == 0), stop=(kc == 3)),
                     reads=[b_wukv, b_ckn], writes=[b_banks[bk]])
            si = st.k % 3
            st.k += 1
            evac(p, st.k, stg[si][:], banks[bk][:, :], [b_banks[bk]], wb=[b_stg[si]])
            p.dma("sp", kT_d[h, :, t0:t0 + T], stg[si][:], reads=[b_stg[si]], wpart=[bk_])
        for tl in range(T // 128):
            for hb in range(2):
                bk = st.bi % 6
                st.bi += 1
                for kc in range(4):
                    p.op("pe", lambda e, bk=bk, kc=kc, hb=hb, tl=tl: e.matmul(
                        banks[bk][:, :].rearrange("p (h d) -> p h d", d=128),
                        lhsT=ckn[:, kc, tl * 128:(tl + 1) * 128],
                        rhs=wukv[:, kc, hb * 4:(hb + 1) * 4, 128:256], start=(kc == 0), stop=(kc == 3)),
                        reads=[b_wukv, b_ckn], writes=[b_banks[bk]])
                si = st.k % 3
                st.k += 1
                evac(p, st.k, stg[si][:], banks[bk][:, :], [b_banks[bk]], wb=[b_stg[si]])
                r0 = t0 + tl * 128
                p.dma("sp", v_d[r0:r0 + 128, hb * 512:(hb + 1) * 512], stg[si][:], reads=[b_stg[si]], wpart=[bv])
        bk1 = st.bi % 6
        bk2 = (st.bi + 1) % 6
        st.bi += 2
        for kc in range(32):
            p.op("pe", lambda e, bk1=bk1, kc=kc: e.matmul(banks[bk1][:64, :], lhsT=wkr[:, kc, 0, :], rhs=hT[:, kc, :],
                                                          start=(kc == 0), stop=(kc == 31)),
                 reads=[b_wkr, b_hT], writes=[b_banks[bk1]])
        for kc in range(32):
            p.op("pe", lambda e, bk2=bk2, kc=kc: e.matmul(banks[bk2][:64, :], lhsT=wkrr[:, kc, 0, :], rhs=hT[:, kc, :],
                                                          start=(kc == 0), stop=(kc == 31)),
                 reads=[b_wkrr, b_hT], writes=[b_banks[bk2]])
        si = st.k % 3
        st.k += 1
        rope_combine(c, banks[bk1], b_banks[bk1], banks[bk2], b_banks[bk2], cs, b_cs, t0, T, t1, b_t1, t2, b_t2,
                     stg[si][:64, :], [b_stg[si]])
        p.dma("sp", krT_d[:, t0:t0 + T], stg[si][:64, :], reads=[b_stg[si]], wpart=[bkr])


def ph_attn(c, qT_d, bq, kT_d, bk_, krT_d, bkr, v_d, bv, aT_d, ba, ntok, nkeys, nheads=8):
    p = c.p
    NKT = nkeys // 128
    scale = 1.0 / math.sqrt(192.0)
    kr = c.sb([64, nkeys], BF16)
    b_kr = Buf()
    p.dma("sp", kr[:], krT_d, reads=[bkr], writes=[b_kr])
    kT2 = [c.sb([128, nkeys], BF16) for _ in range(2)]
    b_kT2 = [Buf(), Buf()]
    V2 = [c.sb([128, NKT, 128], BF16) for _ in range(2)]
    b_V2 = [Buf(), Buf()]
    ones, b_ones = ones_bf(c, [128, 128])
    qn2 = [c.sb([128, ntok], BF16) for _ in range(2)]
    qr2 = [c.sb([64, ntok], BF16) for _ in range(2)]
    b_qn2 = [Buf(), Buf()]
    b_qr2 = [Buf(), Buf()]

    def load_head(hh):
        sl = hh % 2
        p.dma("sp", kT2[sl][:], kT_d[hh], reads=[bk_], writes=[b_kT2[sl]])
        p.dma("act", V2[sl][:], v_d[:, hh * 128:(hh + 1) * 128].rearrange("(t p) d -> p t d", p=128),
              reads=[bv], writes=[b_V2[sl]])
        p.dma("sp", qn2[sl][:], qT_d[hh, 0:128, :], reads=[bq], writes=[b_qn2[sl]])
        p.dma("sp", qr2[sl][:], qT_d[hh, 128:192, :], reads=[bq], writes=[b_qr2[sl]])
    NSB = 4
    sbank = [c.ps([128, 512], F32) for _ in range(NSB)]
    b_sbank = [Buf() for _ in range(NSB)]
    obank = [c.ps([128, 512], F32) for _ in range(2)]
    b_obank = [Buf() for _ in range(2)]
    rbank = [c.ps([128, 512], F32) for _ in range(2)]
    b_rbank = [Buf() for _ in range(2)]
    P = [c.sb([128, 512], BF16) for _ in range(NSB)]
    b_P = [Buf() for _ in range(NSB)]
    rec = c.sb([128, 512], F32)
    b_rec = Buf()
    ast = [c.sb([128, 512], BF16) for _ in range(2)]
    b_ast = [Buf(), Buf()]
    k = 0
    gi = 0
    load_head(0)
    for h in range(nheads):
        if h + 1 < nheads:
            load_head(h + 1)
        kT, b_kT = kT2[h % 2], b_kT2[h % 2]
        V, b_V = V2[h % 2], b_V2[h % 2]
        qn, b_qn = qn2[h % 2], b_qn2[h % 2]
        qr, b_qr = qr2[h % 2], b_qr2[h % 2]
        for qg in range(ntok // 512):
            q0 = qg * 512
            kbase = k
            k += NKT
            ob = gi % 2
            gi += 1

            def emit_S(kt, q0=q0, kbase=kbase, kT=kT, qn=qn, qr=qr, b_kT=b_kT, b_qn=b_qn, b_qr=b_qr):
                sb_ = (kbase + kt) % NSB
                p.op("pe", lambda e: e.matmul(sbank[sb_][:, :], lhsT=kT[:, kt * 128:(kt + 1) * 128],
                                              rhs=qn[:, q0:q0 + 512], start=True, stop=False),
                     reads=[b_kT, b_qn], writes=[b_sbank[sb_]])
                p.op("pe", lambda e: e.matmul(sbank[sb_][:, :], lhsT=kr[:, kt * 128:(kt + 1) * 128],
                                              rhs=qr[:, q0:q0 + 512], start=False, stop=True),
                     reads=[b_kr, b_qr], writes=[b_sbank[sb_]])
                p.op("act", lambda e: e.activation(out=P[sb_][:], in_=sbank[sb_][:, :], func=AF.Exp, scale=scale),
                     reads=[b_sbank[sb_]], writes=[b_P[sb_]])

            emit_S(0)
            emit_S(1)
            for kt in range(NKT):
                if kt + 2 < NKT:
                    emit_S(kt + 2)
                sb_ = (kbase + kt) % NSB
                p.op("pe", lambda e, sb_=sb_, kt=kt, ob=ob, V=V: e.matmul(obank[ob][:, :], lhsT=V[:, kt, :], rhs=P[sb_][:],
                                                                     start=(kt == 0), stop=(kt == NKT - 1)),
                     reads=[b_P[sb_], b_V], writes=[b_obank[ob]])
                p.op("pe", lambda e, sb_=sb_, kt=kt, ob=ob: e.matmul(rbank[ob][:, :], lhsT=ones[:], rhs=P[sb_][:],
                                                                     start=(kt == 0), stop=(kt == NKT - 1)),
                     reads=[b_P[sb_], b_ones], writes=[b_rbank[ob]])
            p.op("dve", lambda e, ob=ob: e.reciprocal(out=rec[:], in_=rbank[ob][:, :]),
                 reads=[b_rbank[ob]], writes=[b_rec])
            ai = ob
            p.op("dve", lambda e, ob=ob, ai=ai: e.tensor_tensor(out=ast[ai][:], in0=obank[ob][:, :], in1=rec[:],
                                                                op=ALU.mult),
                 reads=[b_obank[ob], b_rec], writes=[b_ast[ai]])
            p.dma("sp", aT_d[h * 128:(h + 1) * 128, q0:q0 + 512], ast[ai][:], reads=[b_ast[ai]], wpart=[ba])


def ph_sg(c, hT_d, bh, win_d, bw, gsg_d, bgsg, wsT_d, bws, bsb_d, bbs, sgT_d, bsg, ntok):
    p = c.p
    T = 512
    NTL = T // 128
    hT = c.sb([128, 32, T], BF16)
    b_hT = Buf()
    wbufs = [c.sb([128, 32, 512], BF16) for _ in range(2)]
    b_wbufs = [Buf(), Buf()]
    banks = [c.ps([128, 512], F32) for _ in range(8)]
    b_banks = [Buf() for _ in range(8)]
    gsg = c.sb([128, 2048], F32)
    b_gsg = Buf()
    p.dma("act", gsg[:], gsg_d, reads=[bgsg], writes=[b_gsg])
    wsT = c.sb([128, 16, 128], BF16)
    b_wsT = Buf()
    p.dma("pool", wsT[:], wsT_d.rearrange("g q p -> q g p"), reads=[bws], writes=[b_wsT])
    bsb = c.sb([128, 16, 128], F32)
    b_bsb = Buf()
    p.dma("act", bsb[:], bsb_d, reads=[bbs], writes=[b_bsb])
    vn = c.sb([128, NTL, 2048], BF16)
    b_vn = Buf()
    gv = [c.sb([128, 4, 128], F32) for _ in range(2)]
    b_gv = [Buf(), Buf()]
    xc = [c.sb([128, 4, 128], F32) for _ in range(2)]
    b_xc = [Buf(), Buf()]
    sqj = c.sb([128, 4, 128], F32)
    b_sqj = Buf()
    mu = [c.sb([128, 4], F32) for _ in range(2)]
    b_mu = [Buf(), Buf()]
    var = [c.sb([128, 4], F32) for _ in range(2)]
    b_var = [Buf(), Buf()]
    uT = [c.sb([128, T], F32) for _ in range(2)]
    b_uT = [Buf(), Buf()]
    tt = [c.sb([128, T], F32) for _ in range(2)]
    b_tt = [Buf(), Buf()]
    stg = [c.sb([128, T], BF16) for _ in range(2)]
    b_stg = [Buf(), Buf()]
    st = St()
    for s in range(ntok // T):
        t0 = s * T
        load_hT(c, hT, b_hT, hT_d, bh, t0, T)
        for vb in range(4):
            w = st.wi % 2
            st.wi += 1
            col0 = 4416 + vb * 512
            p.dma("pool", wbufs[w][:], win_d[:, col0:col0 + 512].rearrange("(c p) n -> p c n", p=128),
                  reads=[bw], writes=[b_wbufs[w]])
            for tl in range(NTL):
                bk = st.bi % 8
                st.bi += 1
                for kc in range(32):
                    p.op("pe", lambda e, bk=bk, kc=kc, tl=tl, w=w: e.matmul(
                        banks[bk][:, :], lhsT=hT[:, kc, tl * 128:(tl + 1) * 128], rhs=wbufs[w][:, kc, :],
                        start=(kc == 0), stop=(kc == 31)), reads=[b_hT, b_wbufs[w]], writes=[b_banks[bk]])
                i = st.k % 2
                st.k += 1
                bank3 = banks[bk][:, :].rearrange("p (g d) -> p g d", d=128)
                p.op("act", lambda e, i=i, bank3=bank3: e.activation(out=gv[i][:], in_=bank3, func=AF.Gelu_apprx_tanh),
                     reads=[b_banks[bk]], writes=[b_gv[i]])
                p.op("dve", lambda e, i=i: e.reduce_sum(out=mu[i][:], in_=gv[i][:], axis=AX.X),
                     reads=[b_gv[i]], writes=[b_mu[i]])
                p.op("dve", lambda e, i=i: e.tensor_scalar(out=mu[i][:], in0=mu[i][:], scalar1=1.0 / 128, scalar2=None,
                                                           op0=ALU.mult), reads=[b_mu[i]], writes=[b_mu[i]])
                p.op("dve", lambda e, i=i: e.tensor_tensor(out=xc[i][:], in0=gv[i][:],
                                                           in1=mu[i][:].unsqueeze(2).to_broadcast([128, 4, 128]),
                                                           op=ALU.subtract), reads=[b_gv[i], b_mu[i]], writes=[b_xc[i]])
                p.op("dve", lambda e, i=i: e.tensor_tensor(out=sqj[:], in0=xc[i][:], in1=xc[i][:], op=ALU.mult),
                     reads=[b_xc[i]], writes=[b_sqj])
                p.op("dve", lambda e, i=i: e.reduce_sum(out=var[i][:], in_=sqj[:], axis=AX.X),
                     reads=[b_sqj], writes=[b_var[i]])
                rstd_ops(p, "dve", var[i][:], var[i][:], 128, [b_var[i]], [b_var[i]])
                p.op("dve", lambda e, i=i: e.tensor_tensor(out=xc[i][:], in0=xc[i][:],
                                                           in1=var[i][:].unsqueeze(2).to_broadcast([128, 4, 128]),
                                                           op=ALU.mult), reads=[b_xc[i], b_var[i]], writes=[b_xc[i]])
                p.op("dve", lambda e, i=i, tl=tl, vb=vb: e.tensor_tensor(
                    out=vn[:, tl, vb * 512:(vb + 1) * 512], in0=xc[i][:].rearrange("p g d -> p (g d)"),
                    in1=gsg[:, vb * 512:(vb + 1) * 512], op=ALU.mult),
                    reads=[b_xc[i], b_gsg], wpart=[b_vn])

        def epi(tag, ch, ti, bank, b_bank, wd):
            g = tag * 4 + ch
            i = g % 2
            p.op("act", lambda e: e.activation(out=uT[i][:], in_=bank[:, :], func=AF.Gelu_apprx_tanh),
                 reads=[b_bank], writes=[b_uT[i]])
            bk = st.bi % 8
            st.bi += 1
            for tl in range(NTL):
                p.op("pe", lambda e, tl=tl: e.matmul(banks[bk][:, tl * 128:(tl + 1) * 128],
                                                     lhsT=vn[:, tl, g * 128:(g + 1) * 128], rhs=wsT[:, g, :],
                                                     start=True, stop=True),
                     reads=[b_vn, b_wsT], writes=[b_banks[bk]])
            p.op("dve", lambda e: e.tensor_tensor(out=tt[i][:].rearrange("p (a b) -> p a b", b=128),
                                                  in0=banks[bk][:, :].rearrange("p (a b) -> p a b", b=128),
                                                  in1=bsb[:, g, :].unsqueeze(1).to_broadcast([128, NTL, 128]),
                                                  op=ALU.add), reads=[b_banks[bk], b_bsb], writes=[b_tt[i]])
            p.op("dve", lambda e: e.tensor_tensor(out=stg[i][:], in0=tt[i][:], in1=uT[i][:], op=ALU.mult),
                 reads=[b_tt[i], b_uT[i]], writes=[b_stg[i]])
            p.dma("sp", sgT_d[g * 128:(g + 1) * 128, t0:t0 + T], stg[i][:], reads=[b_stg[i]], wpart=[bsg])

        fm_stream(c, [(win_d, bw, 2368 + 512 * i, 512, i) for i in range(4)], 32,
                  lambda kc, tc0, tn: hT[:, kc, tc0:tc0 + tn], b_hT, [(0, T)], epi, wbufs, b_wbufs, banks, b_banks, st)


def ph_merge(c, hT_d, bh, fT_d, bf, aT_d, ba, sgT_d, bsg, wg_d, bwg, bgT_d, bbg, wf_d, bwf, wa_d, bwa, ws_d, bws,
             mT_d, bm, ntok):
    p = c.p
    T = 1024 if ntok % 1024 == 0 else 512
    CW = 128
    NTG = T // 512
    hT = c.sb([128, 32, T], BF16)
    fT = c.sb([128, 8, T], BF16)
    aT = c.sb([128, 8, T], BF16)
    sT = c.sb([128, 16, T], BF16)
    b_hT, b_fT, b_aT, b_sT = Buf(), Buf(), Buf(), Buf()
    wg = [c.sb([128, 32, 3, CW], BF16) for _ in range(2)]
    wf = [c.sb([128, 8, CW], BF16) for _ in range(2)]
    wa = [c.sb([128, 8, CW], BF16) for _ in range(2)]
    ws = [c.sb([128, 16, CW], BF16) for _ in range(2)]
    b_wg, b_wf, b_wa, b_ws = [Buf(), Buf()], [Buf(), Buf()], [Buf(), Buf()], [Buf(), Buf()]
    bgT = c.sb([128, 96], F32)
    b_bgT = Buf()
    p.dma("act", bgT[:], bgT_d, reads=[bbg], writes=[b_bgT])
    banks = [c.ps([128, 512], F32) for _ in range(8)]
    b_banks = [Buf() for _ in range(8)]
    gt = [c.sb([128, 512], F32) for _ in range(3)]
    b_gt = [Buf() for _ in range(3)]
    m1 = c.sb([128, 512], F32)
    m2 = c.sb([128, 512], F32)
    b_m1, b_m2 = Buf(), Buf()
    stg = [c.sb([128, 512], BF16) for _ in range(2)]
    b_stg = [Buf(), Buf()]
    bi = 0
    wi = 0
    for s in range(ntok // T):
        t0 = s * T
        load_hT(c, hT, b_hT, hT_d, bh, t0, T)
        p.dma("sp", fT[:], fT_d[:, t0:t0 + T].rearrange("(c p) t -> p c t", p=128), reads=[bf], writes=[b_fT])
        p.dma("sp", aT[:], aT_d[:, t0:t0 + T].rearrange("(c p) t -> p c t", p=128), reads=[ba], writes=[b_aT])
        p.dma("sp", sT[:], sgT_d[:, t0:t0 + T].rearrange("(c p) t -> p c t", p=128), reads=[bsg], writes=[b_sT])
        for jb in range(D // CW):
            w = wi % 2
            wi += 1
            c0 = jb * CW
            for b in range(3):
                p.dma("pool", wg[w][:, :, b, :],
                      wg_d[:, b * D + c0:b * D + c0 + CW].rearrange("(c p) n -> p c n", p=128),
                      reads=[bwg], wpart=[b_wg[w]])
            p.dma("pool", wf[w][:], wf_d[:, c0:c0 + CW].rearrange("(c p) n -> p c n", p=128), reads=[bwf], writes=[b_wf[w]])
            p.dma("pool", wa[w][:], wa_d[:, c0:c0 + CW].rearrange("(c p) n -> p c n", p=128), reads=[bwa], writes=[b_wa[w]])
            p.dma("pool", ws[w][:], ws_d[:, c0:c0 + CW].rearrange("(c p) n -> p c n", p=128), reads=[bws], writes=[b_ws[w]])
            for sub in range(CW // 128):
              for tg in range(NTG):
                tsl = slice(tg * 512, (tg + 1) * 512)
                j = jb * (CW // 128) + sub
                n0 = j * 128
                cs_ = slice(sub * 128, (sub + 1) * 128)
                gb_ = []
                for b in range(3):
                    bk = bi % 8
                    bi += 1
                    gb_.append(bk)
                    for kc in range(32):
                        p.op("pe", lambda e, bk=bk, kc=kc, b=b, w=w, cs_=cs_, tsl=tsl: e.matmul(
                            banks[bk][:, :], lhsT=wg[w][:, kc, b, cs_], rhs=hT[:, kc, tsl], start=(kc == 0), stop=(kc == 31)),
                            reads=[b_wg[w], b_hT], writes=[b_banks[bk]])
                yb_ = []
                for b, (wt, b_wt, act, b_act, nk) in enumerate(((wf, b_wf, fT, b_fT, 8), (wa, b_wa, aT, b_aT, 8),
                                                                (ws, b_ws, sT, b_sT, 16))):
                    bk = bi % 8
                    bi += 1
                    yb_.append(bk)
                    for kc in range(nk):
                        p.op("pe", lambda e, bk=bk, kc=kc, wt=wt, act=act, nk=nk, w=w, cs_=cs_, tsl=tsl: e.matmul(
                            banks[bk][:, :], lhsT=wt[w][:, kc, cs_], rhs=act[:, kc, tsl], start=(kc == 0),
                            stop=(kc == nk - 1)), reads=[b_wt[w], b_act], writes=[b_banks[bk]])
                for b in range(3):
                    p.op("act", lambda e, b=b, bk=gb_[b], j=j: e.activation(out=gt[b][:], in_=banks[bk][:, :],
                                                                            func=AF.Sigmoid,
                                                                            bias=bgT[:, b * 32 + j:b * 32 + j + 1]),
                         reads=[b_banks[gb_[b]], b_bgT], writes=[b_gt[b]])
                p.op("dve", lambda e, bk=yb_[0]: e.tensor_tensor(out=m1[:], in0=banks[bk][:, :], in1=gt[0][:], op=ALU.mult),
                     reads=[b_banks[yb_[0]], b_gt[0]], writes=[b_m1])
                p.op("dve", lambda e, bk=yb_[1]: e.tensor_tensor(out=m2[:], in0=banks[bk][:, :], in1=gt[1][:], op=ALU.mult),
                     reads=[b_banks[yb_[1]], b_gt[1]], writes=[b_m2])
                p.op("dve", lambda e: e.tensor_tensor(out=m1[:], in0=m1[:], in1=m2[:], op=ALU.add),
                     reads=[b_m1, b_m2], writes=[b_m1])
                p.op("dve", lambda e, bk=yb_[2]: e.tensor_tensor(out=m2[:], in0=banks[bk][:, :], in1=gt[2][:], op=ALU.mult),
                     reads=[b_banks[yb_[2]], b_gt[2]], writes=[b_m2])
                si = (j * NTG + tg) % 2
                p.op("dve", lambda e, si=si: e.tensor_tensor(out=stg[si][:], in0=m1[:], in1=m2[:], op=ALU.add),
                     reads=[b_m1, b_m2], writes=[b_stg[si]])
                p.dma("sp", mT_d[n0:n0 + 128, t0 + tg * 512:t0 + (tg + 1) * 512], stg[si][:], reads=[b_stg[si]], wpart=[bm])


def ph_memkv(c, mT_d, bmT, wmk_d, bwk, wmv_d, bwv, kmT_d, bkm, vm_d, bvm):
    p = c.p
    mT = c.sb([128, 32, 256], BF16)
    b_mT = Buf()
    load_hT(c, mT, b_mT, mT_d, bmT, 0, 256)
    wbufs = [c.sb([128, 32, 512], BF16) for _ in range(2)]
    b_wbufs = [Buf(), Buf()]
    banks = [c.ps([128, 512], F32) for _ in range(4)]
    b_banks = [Buf() for _ in range(4)]
    stg = [c.sb([128, 512], BF16) for _ in range(2)]
    b_stg = [Buf(), Buf()]
    st = St()

    def epi(tag, ch, ti, bank, b_bank, wd):
        si = st.k % 2
        st.k += 1
        evac(p, st.k, stg[si][:, :256], bank[:, :256], [b_bank], wb=[b_stg[si]])
        r0 = (tag * 4 + ch) * 128
        p.dma("sp", kmT_d[r0:r0 + 128, :], stg[si][:, :256], reads=[b_stg[si]], wpart=[bkm])

    fm_stream(c, [(wmk_d, bwk, 0, 512, 0), (wmk_d, bwk, 512, 512, 1)], 32,
              lambda kc, tc0, tn: mT[:, kc, tc0:tc0 + tn], b_mT, [(0, 256)], epi, wbufs, b_wbufs, banks, b_banks, st)
    for vb in range(2):
        w = st.wi % 2
        st.wi += 1
        p.dma("pool", wbufs[w][:], wmv_d[:, vb * 512:(vb + 1) * 512].rearrange("(c p) n -> p c n", p=128),
              reads=[bwv], writes=[b_wbufs[w]])
        for mt in range(2):
            bk = st.bi % 4
            st.bi += 1
            for kc in range(32):
                p.op("pe", lambda e, bk=bk, kc=kc, mt=mt, w=w: e.matmul(banks[bk][:, :], lhsT=mT[:, kc, mt * 128:(mt + 1) * 128],
                                                                        rhs=wbufs[w][:, kc, :], start=(kc == 0), stop=(kc == 31)),
                     reads=[b_mT, b_wbufs[w]], writes=[b_banks[bk]])
            si = st.k % 2
            st.k += 1
            evac(p, st.k, stg[si][:], banks[bk][:, :], [b_banks[bk]], wb=[b_stg[si]])
            p.dma("sp", vm_d[mt * 128:(mt + 1) * 128, vb * 512:(vb + 1) * 512], stg[si][:], reads=[b_stg[si]], wpart=[bvm])


def ph_mq(c, hT_d, bh, wmq_d, bwq, kmT_d, bkm, vm_d, bvm, omT_d, bom, ntok):
    p = c.p
    T = 512
    idb, b_id = c.ident()
    hT = c.sb([128, 32, T], BF16)
    b_hT = Buf()
    wbufs = [c.sb([128, 32, 512], BF16) for _ in range(2)]
    b_wbufs = [Buf(), Buf()]
    banks = [c.ps([128, 512], F32) for _ in range(6)]
    b_banks = [Buf() for _ in range(6)]
    tbank = c.ps([128, 4, 2, 128], BF16)
    b_tbank = Buf()
    km = c.sb([128, 8, 256], BF16)
    b_km = Buf()
    p.dma("sp", km[:], kmT_d.rearrange("(c p) m -> p c m", p=128), reads=[bkm], writes=[b_km])
    vm = c.sb([128, 2, 4, 260], BF16)
    b_vm = Buf()
    p.op("pool", lambda e: e.memset(vm[:, :, :, 256:260], 1.0), wpart=[b_vm])
    for mt in range(2):
        p.dma("sp", vm[:, mt, :, 0:256], vm_d[mt * 128:(mt + 1) * 128, :].rearrange("p (h d) -> p h d", d=256),
              reads=[bvm], wpart=[b_vm])
    qm = c.sb([128, 8, T], BF16)
    b_qm = Buf()
    P = [c.sb([128, T], BF16) for _ in range(4)]
    b_P = [Buf() for _ in range(4)]
    rec = c.sb([128, 1], F32)
    b_rec = Buf()
    on = c.sb([128, 4, 256], BF16)
    b_on = Buf()
    stg = [c.sb([128, 2, T], BF16) for _ in range(2)]
    b_stg = [Buf(), Buf()]
    st = St()
    pk = 0
    for s in range(ntok // T):
        t0 = s * T
        load_hT(c, hT, b_hT, hT_d, bh, t0, T)

        def epi(tag, ch, ti, bank, b_bank, wd):
            st.k += 1
            evac(p, st.k, qm[:, tag * 4 + ch, :], bank[:, :], [b_bank], wp=[b_qm])

        fm_stream(c, [(wmq_d, bwq, 0, 512, 0), (wmq_d, bwq, 512, 512, 1)], 32,
                  lambda kc, tc0, tn: hT[:, kc, tc0:tc0 + tn], b_hT, [(0, T)], epi, wbufs, b_wbufs, banks, b_banks, st)
        for h in range(4):
            pi = []
            for mt in range(2):
                bk = st.bi % 6
                st.bi += 1
                for dc in range(2):
                    p.op("pe", lambda e, bk=bk, dc=dc, mt=mt, h=h: e.matmul(
                        banks[bk][:, :], lhsT=km[:, 2 * h + dc, mt * 128:(mt + 1) * 128], rhs=qm[:, 2 * h + dc, :],
                        start=(dc == 0), stop=(dc == 1)), reads=[b_km, b_qm], writes=[b_banks[bk]])
                pj = pk % 4
                pk += 1
                pi.append(pj)
                p.op("act", lambda e, bk=bk, pj=pj: e.activation(out=P[pj][:], in_=banks[bk][:, :], func=AF.Exp,
                                                                 scale=1.0 / 16.0), reads=[b_banks[bk]], writes=[b_P[pj]])
            for qt in range(4):
                bk = st.bi % 6
                st.bi += 1
                for mt in range(2):
                    p.op("pe", lambda e, bk=bk, mt=mt, qt=qt, h=h, pj=pi[mt]: e.matmul(
                        banks[bk][:, 0:257], lhsT=P[pj][:, qt * 128:(qt + 1) * 128], rhs=vm[:, mt, h, 0:257],
                        start=(mt == 0), stop=(mt == 1)), reads=[b_P[pi[mt]], b_vm], writes=[b_banks[bk]])
                p.op("dve", lambda e, bk=bk: e.reciprocal(out=rec[:], in_=banks[bk][:, 256:257]),
                     reads=[b_banks[bk]], writes=[b_rec])
                p.op("dve", lambda e, bk=bk, qt=qt: e.tensor_scalar(out=on[:, qt, :], in0=banks[bk][:, 0:256],
                                                                    scalar1=rec[:, 0:1], scalar2=None, op0=ALU.mult),
                     reads=[b_banks[bk], b_rec], wpart=[b_on])
            for qt in range(4):
                for dc in range(2):
                    p.op("pe", lambda e, qt=qt, dc=dc: e.transpose(out=tbank[:, qt, dc, :],
                                                                   in_=on[:, qt, dc * 128:(dc + 1) * 128], identity=idb[:]),
                         reads=[b_on, b_id], writes=[b_tbank])
            si = h % 2
            p.op("act", lambda e, si=si: e.activation(out=stg[si][:].rearrange("p d (q t) -> p q d t", t=128),
                                                      in_=tbank[:], func=AF.Copy), reads=[b_tbank], writes=[b_stg[si]])
            p.dma("sp", omT_d[h * 256:(h + 1) * 256, t0:t0 + T].rearrange("(d p) t -> p d t", p=128), stg[si][:],
                  reads=[b_stg[si]], wpart=[bom])


def ph_up(c, hT_d, bh, wup_d, bwu, cw_d, bcw, pT_d, bpT, ntok):
    p = c.p
    T = 1024
    G = 342
    hT = c.sb([128, 32, T + 2], BF16)
    b_hT = Buf()
    wbufs = [c.sb([128, 32, 256], BF16) for _ in range(4)]
    b_wbufs = [Buf() for _ in range(4)]
    banks = [c.ps([128, 512], F32) for _ in range(8)]
    b_banks = [Buf() for _ in range(8)]
    cw = c.sb([128, 172, 4], F32)
    b_cw = Buf()
    p.dma("act", cw[:], cw_d, reads=[bcw], writes=[b_cw])
    ue = {(k, ch): c.sb([128, T + 2], F32) for k in "gv" for ch in range(2)}
    b_ue = {k: Buf() for k in ue}
    cg = c.sb([128, T], F32)
    cv = c.sb([128, T], F32)
    b_cg, b_cv = Buf(), Buf()
    stg = [c.sb([128, T], BF16) for _ in range(2)]
    b_stg = [Buf(), Buf()]
    st = St()
    NB = 43
    for s in range(ntok // T):
        t0 = s * T
        nst = ntok // T
        lc = ntok if s == 0 else t0 - 1
        rc = ntok + 1 if s == nst - 1 else t0 + T
        p.dma("sp", hT[:, :, 1:T + 1], hT_d[:, :, t0:t0 + T].rearrange("c p t -> p c t"), reads=[bh], writes=[b_hT])
        p.dma("sp", hT[:, :, 0:1], hT_d[:, :, lc:lc + 1].rearrange("c p t -> p c t"), reads=[bh], wpart=[b_hT],
              allow_slow_non_contiguous=True)
        p.dma("sp", hT[:, :, T + 1:T + 2], hT_d[:, :, rc:rc + 1].rearrange("c p t -> p c t"), reads=[bh], wpart=[b_hT],
              allow_slow_non_contiguous=True)

        def epi(tag, ch, ti, bank, b_bank, wd):
            kind, bj = tag
            u = ue[(kind, ch)]
            st.k += 1
            evac(p, st.k, u[:, ti * G:(ti + 1) * G], bank[:, :G], [b_bank], wp=[b_ue[(kind, ch)]])
            if kind == "v" and ti == 2:
                j = bj * 2 + ch
                for (kd, ci, acc, b_acc, eng) in (("g", j, cg, b_cg, "dve"), ("v", 86 + j, cv, b_cv, "dve")):
                    uu = ue[(kd, ch)]
                    bu = b_ue[(kd, ch)]
                    p.op(eng, lambda e, uu=uu, ci=ci, acc=acc: e.tensor_scalar(
                        out=acc[:], in0=uu[:, 0:T], scalar1=cw[:, ci, 0:1], scalar2=cw[:, ci, 3:4],
                        op0=ALU.mult, op1=ALU.add), reads=[bu, b_cw], writes=[b_acc])
                    p.op(eng, lambda e, uu=uu, ci=ci, acc=acc: e.scalar_tensor_tensor(
                        out=acc[:], in0=uu[:, 1:T + 1], scalar=cw[:, ci, 1:2], in1=acc[:], op0=ALU.mult, op1=ALU.add),
                        reads=[bu, b_cw, b_acc], writes=[b_acc])
                    p.op(eng, lambda e, uu=uu, ci=ci, acc=acc: e.scalar_tensor_tensor(
                        out=acc[:], in0=uu[:, 2:T + 2], scalar=cw[:, ci, 2:3], in1=acc[:], op0=ALU.mult, op1=ALU.add),
                        reads=[bu, b_cw, b_acc], writes=[b_acc])
                p.op("act", lambda e: e.activation(out=cg[:], in_=cg[:], func=AF.Gelu_apprx_tanh),
                     reads=[b_cg], writes=[b_cg])
                si = j % 2
                p.op("dve", lambda e: e.tensor_tensor(out=stg[si][:], in0=cg[:], in1=cv[:], op=ALU.mult),
                     reads=[b_cg, b_cv], writes=[b_stg[si]])
                p.dma("sp", pT_d[j, :, t0:t0 + T], stg[si][:], reads=[b_stg[si]], wpart=[bpT])

        blocks = []
        for bj in range(NB):
            blocks.append((wup_d, bwu, 256 * bj, 256, ("g", bj)))
            blocks.append((wup_d, bwu, 11008 + 256 * bj, 256, ("v", bj)))
        fm_stream(c, blocks, 32, lambda kc, tc0, tn: hT[:, kc, tc0:tc0 + tn], b_hT,
                  [(0, G), (G, G), (2 * G, G)], epi, wbufs, b_wbufs, banks, b_banks, st)


BF = ml_dtypes.bfloat16
_PROGS = {}


def _rep(v, n=128):
    return np.ascontiguousarray(np.broadcast_to(v, (n,) + tuple(v.shape)))


def prog_l1():
    c = Ctx()
    x_d, bx = c.din("x", [TOK, D], F32)
    gb_d, bg = c.din("gb", [128, D], F32)
    win_d, bw = c.din("win", [D, 6464], F32)
    dft_d, bd = c.din("dft", [2, 256, 256], F32)
    pos_d, bp = c.din("pos", [64, TOK], I32)
    invf_d, bi = c.din("invf", [64, 1], F32)
    wuq_d, bwuq = c.din("wuq", [768, 1536], F32)
    gq_d, bgq = c.din("gq", [128, 6], F32)
    wukv_d, bwukv = c.din("wukv", [512, 2048], F32)
    gkv_d, bgkv = c.din("gkv", [128, 4], F32)
    gsg_d, bgsg = c.din("gsg", [128, 2048], F32)
    wsT_d, bws = c.din("wsT", [16, 128, 128], F32)
    bsb_d, bbs = c.din("bsb", [128, 16, 128], F32)
    hT_d, bh = c.dint("hT", [32, 128, TOK], BF16)
    cs_d, bcs = c.dint("cs", [2, 64, TOK], F32)
    ab_d, bab = c.dout("ab", [2, 1024, TOK], BF16)
    qT_d, bq = c.dout("qT", [8, 192, TOK], BF16)
    kT_d, bk_ = c.dout("kT", [8, 128, TOK], BF16)
    krT_d, bkr = c.dout("krT", [64, TOK], BF16)
    v_d, bv = c.dout("v", [TOK, 1024], BF16)
    sgT_d, bsg = c.dout("sgT", [2048, TOK], BF16)
    with c.phase():
        ph_head(c, x_d, bx, gb_d, bg, hT_d, bh, TOK)
    with c.phase():
        ph_rope(c, pos_d, bp, invf_d, bi, cs_d, bcs, TOK)
    with c.phase():
        ph_f(c, hT_d, bh, win_d, bw, dft_d, bd, ab_d, bab, TOK)
    with c.phase():
        ph_q(c, hT_d, bh, win_d, bw, wuq_d, bwuq, gq_d, bgq, cs_d, bcs, qT_d, bq, TOK)
    with c.phase():
        ph_kv(c, hT_d, bh, win_d, bw, wukv_d, bwukv, gkv_d, bgkv, cs_d, bcs, kT_d, bk_, krT_d, bkr, v_d, bv, TOK)
    with c.phase():
        ph_sg(c, hT_d, bh, win_d, bw, gsg_d, bgsg, wsT_d, bws, bsb_d, bbs, sgT_d, bsg, TOK)
    c.finish()
    return c.nc


def prog_l2():
    c = Ctx()
    m_d, bm = c.din("m", [128, 2, 128, 128], BF16)
    cs_d, bcs = c.din("cs128", [2, 128, 128], F32)
    tw_d, btw = c.din("tw", [2, 128, 128], F32)
    qT_d, bq = c.din("qT", [8, 192, TOK], BF16)
    kT_d, bk_ = c.din("kT", [8, 128, SEQ], BF16)
    krT_d, bkr = c.din("krT", [64, SEQ], BF16)
    v_d, bv = c.din("v", [SEQ, 1024], BF16)
    fo_d, bfo = c.dout("fo", [128, 128, 128], BF16)
    aT_d, ba = c.dout("aT", [1024, TOK], BF16)
    with c.phase():
        ph_fft(c, m_d, bm, cs_d, bcs, tw_d, btw, fo_d, bfo)
    with c.phase():
        ph_attn(c, qT_d, bq, kT_d, bk_, krT_d, bkr, v_d, bv, aT_d, ba, TOK, SEQ)
    c.finish()
    return c.nc


def prog_l3():
    c = Ctx()
    x_d, bx = c.din("x", [TOK, D], F32)
    gb_d, bg = c.din("gb", [128, D], F32)
    fT_d, bf = c.din("fT", [1024, TOK], BF16)
    aT_d, ba = c.din("aT", [1024, TOK], BF16)
    sT_d, bs_ = c.din("sgT", [2048, TOK], BF16)
    wg_d, bwg = c.din("wg", [D, 3 * D], F32)
    bgT_d, bbg = c.din("bgT", [128, 96], F32)
    wf_d, bwf = c.din("wf", [1024, D], F32)
    wa_d, bwa = c.din("wa", [1024, D], F32)
    ws_d, bws = c.din("ws", [2048, D], F32)
    wo_d, bwo = c.din("wo", [D, D], F32)
    gp1_d, bgp1 = c.din("gp1", [128, D], F32)
    gb2_d, bg2 = c.din("gb2", [128, D], F32)
    mem_d, bmem = c.din("mem", [256, D], F32)
    gkv_d, bgkv = c.din("gmkv", [128, D], F32)
    wmq_d, bwq = c.din("wmq", [D, 1024], F32)
    wmk_d, bwk = c.din("wmk", [D, 1024], F32)
    wmv_d, bwv = c.din("wmv", [D, 1024], F32)
    wmo_d, bwmo = c.din("wmo", [1024, D], F32)
    gp2_d, bgp2 = c.din("gp2", [128, D], F32)
    hT_d, bh = c.dint("hT", [32, 128, TOK], BF16)
    mT_d, bm = c.dint("mgT", [32, 128, TOK], BF16)
    y_d, by = c.dint("y", [TOK, D], F32)
    ssq_d, bss = c.dint("ssq", [TOK, 8], F32)
    x1_d, bx1 = c.dint("x1", [TOK, D], F32)
    memT_d, bmemT = c.dint("memT", [32, 128, 256], BF16)
    kmT_d, bkm = c.dint("kmT", [1024, 256], BF16)
    vm_d, bvm = c.dint("vm", [256, 1024], BF16)
    omT_d, bom = c.dint("omT", [8, 128, TOK], BF16)
    x2_d, bx2 = c.dout("x2", [TOK, D], F32)
    with c.phase():
        ph_head(c, x_d, bx, gb_d, bg, hT_d, bh, TOK)
    with c.phase():
        ph_merge(c, hT_d, bh, fT_d, bf, aT_d, ba, sT_d, bs_, wg_d, bwg, bgT_d, bbg, wf_d, bwf, wa_d, bwa, ws_d, bws,
                 mT_d.rearrange("c p t -> (c p) t"), bm, TOK)
    with c.phase():
        ph_lin(c, mT_d, bm, 32, wo_d, bwo, y_d, by, ssq_d, bss, TOK)
    with c.phase():
        ph_nr(c, y_d, by, ssq_d, bss, gp1_d, bgp1, x_d, bx, x1_d, bx1, TOK)
    with c.phase():
        ph_head(c, x1_d, bx1, gb2_d, bg2, hT_d, bh, TOK)
    with c.phase():
        ph_head(c, mem_d, bmem, gkv_d, bgkv, memT_d, bmemT, 256)
    with c.phase():
        ph_memkv(c, memT_d, bmemT, wmk_d, bwk, wmv_d, bwv, kmT_d, bkm, vm_d, bvm)
    with c.phase():
        ph_mq(c, hT_d, bh, wmq_d, bwq, kmT_d, bkm, vm_d, bvm, omT_d.rearrange("c p t -> (c p) t"), bom, TOK)
    with c.phase():
        ph_lin(c, omT_d, bom, 8, wmo_d, bwmo, y_d, by, ssq_d, bss, TOK)
    with c.phase():
        ph_nr(c, y_d, by, ssq_d, bss, gp2_d, bgp2, x1_d, bx1, x2_d, bx2, TOK)
    c.finish()
    return c.nc


def prog_l4():
    c = Ctx()
    x_d, bx = c.din("x", [TOK, D], F32)
    xh_d, bxh = c.din("xh", [2, D], F32)
    gb_d, bg = c.din("gb", [128, D], F32)
    wup_d, bwu = c.din("wup", [D, 22016], F32)
    cw_d, bcw = c.din("cw", [128, 172, 4], F32)
    wdn_d, bwd = c.din("wdn", [11008, D], F32)
    gp_d, bgp = c.din("gp", [128, D], F32)
    hT_d, bh = c.dint("hT", [32, 128, TOK + 2], BF16)
    pT_d, bpT = c.dint("pT", [86, 128, TOK], BF16)
    y_d, by = c.dint("y", [TOK, D], F32)
    ssq_d, bss = c.dint("ssq", [TOK, 8], F32)
    xo_d, bxo = c.dout("x3", [TOK, D], F32)
    with c.phase():
        ph_head(c, x_d, bx, gb_d, bg, hT_d, bh, TOK, halo=(xh_d, bxh))
    with c.phase():
        ph_up(c, hT_d, bh, wup_d, bwu, cw_d, bcw, pT_d, bpT, TOK)
    with c.phase():
        ph_lin(c, pT_d, bpT, 86, wdn_d, bwd, y_d, by, ssq_d, bss, TOK)
    with c.phase():
        ph_nr(c, y_d, by, ssq_d, bss, gp_d, bgp, x_d, bx, xo_d, bxo, TOK)
    c.finish()
    return c.nc


def _prog(name, fn):
    if name not in _PROGS:
        _PROGS[name] = fn()
    return _PROGS[name]


def _run(nc, maps):
    res = run_bass_kernel_spmd(nc, maps, core_ids=list(range(NCORES)))
    return res.results


def kernel(x, mem, positions, mix_pre_norm, mix_post_norm, w_in, mla_q_norm, w_uq, mla_kv_norm, w_ukv, sg_norm,
           w_spatial, b_spatial, w_br_f, w_br_a, w_br_s, w_gate, b_gate, w_out, mem_pre_norm, mem_post_norm,
           mem_kv_norm, w_mq, w_mk, w_mv, w_mo, ffn_pre_norm, ffn_post_norm, w_up, conv_w, conv_b, w_down):
    f32 = np.float32
    xs = np.asarray(x, f32)[0]
    memv = np.ascontiguousarray(np.asarray(mem, f32)[0])
    pos = np.asarray(positions)[0].astype(np.int32)
    jj = np.arange(256)
    a256 = 2 * np.pi * np.outer(jj, jj) / 256
    dft = np.stack([np.cos(a256), -np.sin(a256)]).astype(f32)
    kk = np.arange(128)
    a128 = 2 * np.pi * np.outer(kk, kk) / 128
    cs128 = np.stack([np.cos(a128), np.sin(a128)]).astype(f32)
    atw = 2 * np.pi * np.outer(kk, kk) / SEQ
    tw = np.stack([np.cos(atw), np.sin(atw)]).astype(f32)
    invf = (10000.0 ** (-np.arange(0, 64, 2, dtype=f32) / 64)).astype(f32)
    invf2 = np.concatenate([invf, invf]).reshape(64, 1)
    A = lambda v: np.ascontiguousarray(np.asarray(v, f32))
    for l in range(2):
        com = {"gb": _rep(A(mix_pre_norm[l])), "win": A(w_in[l]), "dft": dft, "invf": invf2, "wuq": A(w_uq[l]),
               "gq": np.ascontiguousarray(A(mla_q_norm[l]).reshape(6, 128).T), "wukv": A(w_ukv[l]),
               "gkv": np.ascontiguousarray(A(mla_kv_norm[l]).reshape(4, 128).T), "gsg": _rep(A(sg_norm[l])),
               "wsT": np.ascontiguousarray(A(w_spatial[l]).transpose(0, 2, 1)), "bsb": _rep(A(b_spatial[l]))}
        maps = []
        for c in range(NCORES):
            sl = slice(c * TOK, (c + 1) * TOK)
            m = dict(com)
            m["x"] = np.ascontiguousarray(xs[sl])
            m["pos"] = _rep(pos[sl], 64)
            maps.append(m)
        r1 = _run(_prog("l1", prog_l1), maps)
        ab = np.stack([r1[c]["ab"] for c in range(NCORES)])
        abr = ab.reshape(NCORES, 2, 8, 128, 16, 128)
        kT_all = np.ascontiguousarray(np.concatenate([r1[c]["kT"] for c in range(NCORES)], axis=2))
        krT_all = np.ascontiguousarray(np.concatenate([r1[c]["krT"] for c in range(NCORES)], axis=1))
        v_all = np.ascontiguousarray(np.concatenate([r1[c]["v"] for c in range(NCORES)], axis=0))
        maps = []
        for j in range(NCORES):
            mm = abr[:, :, j].transpose(0, 3, 1, 2, 4).reshape(128, 2, 128, 128)
            maps.append({"m": np.ascontiguousarray(mm), "cs128": cs128, "tw": tw, "qT": r1[j]["qT"], "kT": kT_all,
                         "krT": krT_all, "v": v_all})
        r2 = _run(_prog("l2", prog_l2), maps)
        fo = np.stack([r2[j]["fo"] for j in range(NCORES)])
        fall = fo.transpose(1, 3, 0, 2).reshape(SEQ, 1024)
        com = {"gb": _rep(A(mix_pre_norm[l])), "wg": A(w_gate[l]),
               "bgT": np.ascontiguousarray(A(b_gate[l]).reshape(96, 128).T), "wf": A(w_br_f[l]), "wa": A(w_br_a[l]),
               "ws": A(w_br_s[l]), "wo": A(w_out[l]), "gp1": _rep(A(mix_post_norm[l])),
               "gb2": _rep(A(mem_pre_norm[l])), "mem": memv, "gmkv": _rep(A(mem_kv_norm[l])), "wmq": A(w_mq[l]),
               "wmk": A(w_mk[l]), "wmv": A(w_mv[l]), "wmo": A(w_mo[l]), "gp2": _rep(A(mem_post_norm[l]))}
        maps = []
        for c in range(NCORES):
            sl = slice(c * TOK, (c + 1) * TOK)
            m = dict(com)
            m["x"] = np.ascontiguousarray(xs[sl])
            m["fT"] = np.ascontiguousarray(fall[sl].T)
            m["aT"] = r2[c]["aT"]
            m["sgT"] = r1[c]["sgT"]
            maps.append(m)
        r3 = _run(_prog("l3", prog_l3), maps)
        x2 = np.concatenate([r3[c]["x2"] for c in range(NCORES)], axis=0)
        cwp = np.concatenate([A(conv_w[l]), A(conv_b[l])[None]], axis=0)
        cwp = np.ascontiguousarray(cwp.reshape(4, 172, 128).transpose(2, 1, 0))
        com = {"gb": _rep(A(ffn_pre_norm[l])), "wup": A(w_up[l]), "cw": cwp, "wdn": A(w_down[l]),
               "gp": _rep(A(ffn_post_norm[l]))}
        zero = np.zeros(D, f32)
        maps = []
        for c in range(NCORES):
            sl = slice(c * TOK, (c + 1) * TOK)
            m = dict(com)
            m["x"] = np.ascontiguousarray(x2[sl])
            prev = x2[c * TOK - 1] if c > 0 else zero
            nxt = x2[(c + 1) * TOK] if c < NCORES - 1 else zero
            m["xh"] = np.ascontiguousarray(np.stack([prev, nxt]))
            maps.append(m)
        r4 = _run(_prog("l4", prog_l4), maps)
        xs = np.concatenate([r4[c]["x3"] for c in range(NCORES)], axis=0)
    return np.ascontiguousarray(xs[None].astype(f32))
```
